# Optimizing a Trainium2 kernel written in Bass

```python
import jax, jax.numpy as jnp
from jax import lax
import numpy as np

D_MODEL = 1024
BATCH = 8
SEQ = 8192
DEPTH = 2
DEC_BATCH = 16
DEC_SEQ = 4096
PAST_LEN = 128

GRID_W = 64
N_BRANCH = 3

NA_HEADS = 8
NA_HEAD_DIM = 64
NA_WIDTH = NA_HEADS * NA_HEAD_DIM
NA_WIN_ROWS = 8
NA_WIN_COLS = 16
NA_COL_BLOCK = 16
NA_KEY_COLS = 2 * NA_COL_BLOCK
NA_RPB_ROWS = 2 * NA_WIN_ROWS - 1
NA_RPB_COLS = 2 * NA_WIN_COLS - 1

MLA_HEADS = 8
MLA_NOPE = 64
MLA_ROPE = 32
MLA_QK = MLA_NOPE + MLA_ROPE
MLA_V = 64
MLA_WIDTH = MLA_HEADS * MLA_V
MLA_Q_RANK = 256
MLA_KV_RANK = 128
MLA_Q_BLOCK = 128
ROPE_THETA = 10000.0

RW_HEADS = 8
RW_HEAD_DIM = 64
RW_WIDTH = RW_HEADS * RW_HEAD_DIM
RW_DECAY_RANK = 64
RW_A_RANK = 64
RW_G_RANK = 128
RW_LN_EPS = 64e-5
RW_IN = 3 * RW_WIDTH + 2 * RW_DECAY_RANK + 2 * RW_A_RANK + RW_G_RANK
RW_SPLITS = (RW_WIDTH, 2 * RW_WIDTH, 3 * RW_WIDTH, 3 * RW_WIDTH + 2 * RW_DECAY_RANK, 3 * RW_WIDTH + 2 * RW_DECAY_RANK + 2 * RW_A_RANK)

D_FF = ((8 * D_MODEL // 3 + 255) // 256) * 256

IN_SIZES = (NA_WIDTH, NA_WIDTH, NA_WIDTH, MLA_Q_RANK, MLA_KV_RANK, MLA_ROPE, RW_IN, N_BRANCH * D_MODEL)
D_IN = sum(IN_SIZES)
IN_SPLITS = tuple(int(s) for s in np.cumsum(IN_SIZES)[:-1])
NORM_EPS = 1e-6

kernel_name = 'hybrid_na_mla_rwkv7_encoder'


def rms_norm(x, g):
    xf = x.astype(jnp.float32)
    y = xf * lax.rsqrt(jnp.mean(xf * xf, axis=-1, keepdims=True) + NORM_EPS)
    return (y * g.astype(jnp.float32)).astype(x.dtype)


def axial_rope(seq_len):
    t = jnp.arange(seq_len, dtype=jnp.int32)
    row = (t // GRID_W).astype(jnp.float32)
    col = (t % GRID_W).astype(jnp.float32)
    n_freq = MLA_ROPE // 4
    inv_freq = ROPE_THETA ** (-jnp.arange(n_freq, dtype=jnp.float32) / n_freq)
    ang = jnp.concatenate([row[:, None] * inv_freq, col[:, None] * inv_freq], axis=-1)
    return jnp.cos(ang), jnp.sin(ang)


def apply_rope(x, cos, sin):
    half = x.shape[-1] // 2
    xf = x.astype(jnp.float32)
    x1, x2 = xf[..., :half], xf[..., half:]
    c = cos[None, :, None, :]
    s = sin[None, :, None, :]
    return jnp.concatenate([x1 * c - x2 * s, x1 * s + x2 * c], axis=-1).astype(x.dtype)


def neighbourhood_attention(q, k, v, rpb):
    b, L, h, dh = q.shape
    rows = L // GRID_W
    wr = min(NA_WIN_ROWS, rows)
    ncb = GRID_W // NA_COL_BLOCK
    qg = q.reshape(b, rows, ncb, NA_COL_BLOCK, h, dh)
    kg = k.reshape(b, rows, GRID_W, h, dh)
    vg = v.reshape(b, rows, GRID_W, h, dh)
    j = np.arange(ncb)
    kc_start = np.clip(j * NA_COL_BLOCK - NA_WIN_COLS // 2, 0, GRID_W - NA_KEY_COLS)
    key_cols = kc_start[:, None] + np.arange(NA_KEY_COLS)[None, :]
    q_cols = j[:, None] * NA_COL_BLOCK + np.arange(NA_COL_BLOCK)[None, :]
    win_start = np.clip(q_cols - NA_WIN_COLS // 2, 0, GRID_W - NA_WIN_COLS)
    kc = key_cols[:, None, :]
    col_ok = (kc >= win_start[:, :, None]) & (kc < win_start[:, :, None] + NA_WIN_COLS)
    dc_idx = np.clip(kc - q_cols[:, :, None] + NA_WIN_COLS - 1, 0, NA_RPB_COLS - 1)
    col_bias = rpb.astype(jnp.float32)[:, :, dc_idx]
    col_mask = jnp.asarray(col_ok)[None, None, :, :, None, :]
    scale = dh ** -0.5

    def one_row(r):
        rs = jnp.clip(r - wr // 2, 0, rows - wr)
        k_rows = lax.dynamic_slice_in_dim(kg, rs, wr, axis=1)
        v_rows = lax.dynamic_slice_in_dim(vg, rs, wr, axis=1)
        k_blk = k_rows[:, :, key_cols]
        v_blk = v_rows[:, :, key_cols]
        q_row = lax.dynamic_index_in_dim(qg, r, axis=1, keepdims=False)
        s = jnp.einsum('bjqhd,bwjkhd->bhjqwk', q_row, k_blk, preferred_element_type=jnp.float32) * scale
        dr_idx = rs + jnp.arange(wr) - r + NA_WIN_ROWS - 1
        bias = jnp.transpose(jnp.take(col_bias, dr_idx, axis=1), (0, 2, 3, 1, 4))
        s = jnp.where(col_mask, s + bias[None], -jnp.inf)
        p = jax.nn.softmax(s.reshape(b, h, ncb, NA_COL_BLOCK, wr * NA_KEY_COLS), axis=-1)
        p = p.reshape(s.shape).astype(v.dtype)
        o = jnp.einsum('bhjqwk,bwjkhd->bjqhd', p, v_blk)
        return o.reshape(b, GRID_W, h * dh)

    out = lax.map(one_row, jnp.arange(rows))
    return jnp.transpose(out, (1, 0, 2, 3)).reshape(b, L, h * dh)


def mla_attention(cq, ckv, kr, cq_norm, ckv_norm, w_uq, w_ukv, q_norm, k_norm, cos, sin):
    b, L, _ = cq.shape
    q = (rms_norm(cq, cq_norm) @ w_uq).reshape(b, L, MLA_HEADS, MLA_QK)
    kv = (rms_norm(ckv, ckv_norm) @ w_ukv).reshape(b, L, MLA_HEADS, MLA_NOPE + MLA_V)
    k_nope, v = kv[..., :MLA_NOPE], kv[..., MLA_NOPE:]
    k = jnp.concatenate([k_nope, jnp.broadcast_to(kr[:, :, None, :], (b, L, MLA_HEADS, MLA_ROPE))], axis=-1)
    q = rms_norm(q, q_norm)
    k = rms_norm(k, k_norm)
    q = jnp.concatenate([q[..., :MLA_NOPE], apply_rope(q[..., MLA_NOPE:], cos, sin)], axis=-1)
    k = jnp.concatenate([k[..., :MLA_NOPE], apply_rope(k[..., MLA_NOPE:], cos, sin)], axis=-1)
    nblk = L // MLA_Q_BLOCK
    qb = jnp.transpose(q.reshape(b, nblk, MLA_Q_BLOCK, MLA_HEADS, MLA_QK), (1, 0, 2, 3, 4))
    scale = MLA_QK ** -0.5

    def one_block(qi):
        s = jnp.einsum('bqhd,bkhd->bhqk', qi, k, preferred_element_type=jnp.float32) * scale
        p = jax.nn.softmax(s, axis=-1).astype(v.dtype)
        return jnp.einsum('bhqk,bkhd->bqhd', p, v)

    o = lax.map(one_block, qb)
    return jnp.transpose(o, (1, 0, 2, 3, 4)).reshape(b, L, MLA_WIDTH)


def centred_shift(p):
    prev = jnp.pad(p[:, :-1], ((0, 0), (1, 0), (0, 0)))
    nxt = jnp.pad(p[:, 1:], ((0, 0), (0, 1), (0, 0)))
    return 0.5 * (prev + nxt)


def to_heads(t):
    return t.reshape(t.shape[:-1] + (RW_HEADS, RW_HEAD_DIM))


def rwkv7_scan(r, w, k, v, kk, a, reverse):
    b, L, h, n = r.shape

    def step(S, inp):
        r_t, w_t, k_t, v_t, kk_t, a_t = inp
        s_kk = jnp.einsum('bhvk,bhk->bhv', S, kk_t)
        S = S * w_t[:, :, None, :] - s_kk[..., None] * (kk_t * a_t)[:, :, None, :] + v_t[..., None] * k_t[:, :, None, :]
        return S, jnp.einsum('bhvk,bhk->bhv', S, r_t)

    xs = tuple(jnp.moveaxis(t, 1, 0) for t in (r, w, k, v, kk, a))
    s0 = jnp.zeros((b, h, n, n), jnp.float32)
    _, ys = lax.scan(step, s0, xs, reverse=reverse)
    return jnp.moveaxis(ys, 0, 1)


def rwkv7_mix(p, mu, w0, w_up, a0, a_up, g_up, k_k, k_a, r_k, ln_w, ln_b):
    f32 = jnp.float32
    b, L, _ = p.shape
    p = p.astype(f32)
    p = p + mu.astype(f32) * (centred_shift(p) - p)
    r, k, v, wd, ad, gd = jnp.split(p, RW_SPLITS, axis=-1)
    wd = jnp.tanh(wd).reshape(b, L, 2, RW_DECAY_RANK)
    ad = ad.reshape(b, L, 2, RW_A_RANK)
    w_raw = w0.astype(f32) + jnp.einsum('bldr,drc->bldc', wd, w_up.astype(f32))
    decay = jnp.exp(-jnp.exp(-jax.nn.softplus(-w_raw) - 0.5))
    a = jax.nn.sigmoid(a0.astype(f32) + jnp.einsum('bldr,drc->bldc', ad, a_up.astype(f32)))
    g = jax.nn.sigmoid(gd) @ g_up.astype(f32)
    kk = to_heads(k * k_k.astype(f32))
    kk = kk * lax.rsqrt(jnp.sum(kk * kk, axis=-1, keepdims=True) + 1e-12)
    kd = k[:, :, None, :] * (1.0 + (a - 1.0) * k_a.astype(f32))
    rh, vh = to_heads(r), to_heads(v)
    y = (rwkv7_scan(rh, to_heads(decay[:, :, 0]), to_heads(kd[:, :, 0]), vh, kk, to_heads(a[:, :, 0]), False)
         + rwkv7_scan(rh, to_heads(decay[:, :, 1]), to_heads(kd[:, :, 1]), vh, kk, to_heads(a[:, :, 1]), True))
    mean = jnp.mean(y, axis=-1, keepdims=True)
    var = jnp.mean(jnp.square(y - mean), axis=-1, keepdims=True)
    y = ((y - mean) * lax.rsqrt(var + RW_LN_EPS)).reshape(b, L, RW_WIDTH) * ln_w.astype(f32) + ln_b.astype(f32)
    bonus = jnp.sum(rh[:, :, None] * to_heads(kd) * r_k.astype(f32), axis=(2, -1))
    y = y + (bonus[..., None] * vh).reshape(b, L, RW_WIDTH)
    return y * g


def encoder_trunk(x, norm1_g, w_in, b_gate, na_q_norm, na_k_norm, na_rpb, na_proj,
                  mla_cq_norm, mla_ckv_norm, mla_w_uq, mla_w_ukv, mla_q_norm, mla_k_norm, mla_proj,
                  rw_mu, rw_w0, rw_w_up, rw_a0, rw_a_up, rw_g_up, rw_k_k, rw_k_a, rw_r_k, rw_ln_w, rw_ln_b, rw_proj,
                  w_out, norm2_g, ffn_w_gate, ffn_w_up, ffn_w_down):
    b, L, d = x.shape
    cos, sin = axial_rope(L)
    for l in range(DEPTH):
        h = rms_norm(x, norm1_g[l])
        proj = h @ w_in[l]
        qa, ka, va, cq, ckv, kr, rw_cols, gate_in = jnp.split(proj, IN_SPLITS, axis=-1)
        qa = rms_norm(qa.reshape(b, L, NA_HEADS, NA_HEAD_DIM), na_q_norm[l])
        ka = rms_norm(ka.reshape(b, L, NA_HEADS, NA_HEAD_DIM), na_k_norm[l])
        va = va.reshape(b, L, NA_HEADS, NA_HEAD_DIM)
        y_a = neighbourhood_attention(qa, ka, va, na_rpb[l]) @ na_proj[l]
        y_b = mla_attention(cq, ckv, kr, mla_cq_norm[l], mla_ckv_norm[l], mla_w_uq[l], mla_w_ukv[l],
                            mla_q_norm[l], mla_k_norm[l], cos, sin) @ mla_proj[l]
        y_c = rwkv7_mix(rw_cols, rw_mu[l], rw_w0[l], rw_w_up[l], rw_a0[l], rw_a_up[l], rw_g_up[l],
                        rw_k_k[l], rw_k_a[l], rw_r_k[l], rw_ln_w[l], rw_ln_b[l]).astype(x.dtype) @ rw_proj[l]
        gates = jax.nn.sigmoid((gate_in + b_gate[l]).astype(jnp.float32)).astype(x.dtype).reshape(b, L, N_BRANCH, d)
        mixed = gates[:, :, 0] * y_a + gates[:, :, 1] * y_b + gates[:, :, 2] * y_c
        x = x + mixed @ w_out[l]
        h2 = rms_norm(x, norm2_g[l])
        x = x + (jax.nn.silu(h2 @ ffn_w_gate[l]) * (h2 @ ffn_w_up[l])) @ ffn_w_down[l]
    return x


def setup_inputs(seed: int = 0) -> dict:
    key = jax.random.key(seed)
    ks = iter(jax.random.split(key, 48))

    def nrm(shape, scale):
        return scale * jax.random.normal(next(ks), shape, dtype=jnp.float32)

    def gain(shape):
        return 1.0 + nrm(shape, 0.05)

    return {
        'x_prompt': nrm((BATCH, SEQ, D_MODEL), 1.0),
        'x_sample': nrm((DEC_BATCH, DEC_SEQ, D_MODEL), 1.0),
        'norm1_g': gain((DEPTH, D_MODEL)),
        'w_in': nrm((DEPTH, D_MODEL, D_IN), D_MODEL ** -0.5),
        'b_gate': nrm((DEPTH, N_BRANCH * D_MODEL), 0.1),
        'na_q_norm': gain((DEPTH, NA_HEAD_DIM)),
        'na_k_norm': gain((DEPTH, NA_HEAD_DIM)),
        'na_rpb': nrm((DEPTH, NA_HEADS, NA_RPB_ROWS, NA_RPB_COLS), 0.5),
        'na_proj': nrm((DEPTH, NA_WIDTH, D_MODEL), NA_WIDTH ** -0.5),
        'mla_cq_norm': gain((DEPTH, MLA_Q_RANK)),
        'mla_ckv_norm': gain((DEPTH, MLA_KV_RANK)),
        'mla_w_uq': nrm((DEPTH, MLA_Q_RANK, MLA_HEADS * MLA_QK), MLA_Q_RANK ** -0.5),
        'mla_w_ukv': nrm((DEPTH, MLA_KV_RANK, MLA_HEADS * (MLA_NOPE + MLA_V)), MLA_KV_RANK ** -0.5),
        'mla_q_norm': gain((DEPTH, MLA_QK)),
        'mla_k_norm': gain((DEPTH, MLA_QK)),
        'mla_proj': nrm((DEPTH, MLA_WIDTH, D_MODEL), MLA_WIDTH ** -0.5),
        'rw_mu': jax.random.uniform(next(ks), (DEPTH, RW_IN), dtype=jnp.float32),
        'rw_w0': nrm((DEPTH, 2, RW_WIDTH), 1.0),
        'rw_w_up': nrm((DEPTH, 2, RW_DECAY_RANK, RW_WIDTH), 0.5 * RW_DECAY_RANK ** -0.5),
        'rw_a0': nrm((DEPTH, 2, RW_WIDTH), 0.5),
        'rw_a_up': nrm((DEPTH, 2, RW_A_RANK, RW_WIDTH), 0.5 * RW_A_RANK ** -0.5),
        'rw_g_up': nrm((DEPTH, RW_G_RANK, RW_WIDTH), RW_G_RANK ** -0.5),
        'rw_k_k': 0.85 + nrm((DEPTH, RW_WIDTH), 0.05),
        'rw_k_a': gain((DEPTH, RW_WIDTH)),
        'rw_r_k': nrm((DEPTH, RW_HEADS, RW_HEAD_DIM), 0.1),
        'rw_ln_w': gain((DEPTH, RW_WIDTH)),
        'rw_ln_b': nrm((DEPTH, RW_WIDTH), 0.02),
        'rw_proj': nrm((DEPTH, RW_WIDTH, D_MODEL), RW_WIDTH ** -0.5),
        'w_out': nrm((DEPTH, D_MODEL, D_MODEL), D_MODEL ** -0.5),
        'norm2_g': gain((DEPTH, D_MODEL)),
        'ffn_w_gate': nrm((DEPTH, D_MODEL, D_FF), D_MODEL ** -0.5),
        'ffn_w_up': nrm((DEPTH, D_MODEL, D_FF), D_MODEL ** -0.5),
        'ffn_w_down': nrm((DEPTH, D_FF, D_MODEL), D_FF ** -0.5),
    }


def reference(x_prompt, x_sample, norm1_g, w_in, b_gate, na_q_norm, na_k_norm, na_rpb, na_proj,
              mla_cq_norm, mla_ckv_norm, mla_w_uq, mla_w_ukv, mla_q_norm, mla_k_norm, mla_proj,
              rw_mu, rw_w0, rw_w_up, rw_a0, rw_a_up, rw_g_up, rw_k_k, rw_k_a, rw_r_k, rw_ln_w, rw_ln_b, rw_proj,
              w_out, norm2_g, ffn_w_gate, ffn_w_up, ffn_w_down):
    weights = (norm1_g, w_in, b_gate, na_q_norm, na_k_norm, na_rpb, na_proj,
               mla_cq_norm, mla_ckv_norm, mla_w_uq, mla_w_ukv, mla_q_norm, mla_k_norm, mla_proj,
               rw_mu, rw_w0, rw_w_up, rw_a0, rw_a_up, rw_g_up, rw_k_k, rw_k_a, rw_r_k, rw_ln_w, rw_ln_b, rw_proj,
               w_out, norm2_g, ffn_w_gate, ffn_w_up, ffn_w_down)
    y_prompt = encoder_trunk(x_prompt, *weights)
    y_sample = encoder_trunk(x_sample, *weights)
    return (y_prompt, y_sample)
```

```python
import numpy as np
import ml_dtypes
from contextlib import ExitStack
import concourse.bass as bass
import concourse.mybir as mybir
from concourse.bass_utils import run_bass_kernel_spmd

F32 = mybir.dt.float32
BF16 = mybir.dt.bfloat16
AF = mybir.ActivationFunctionType
ALU = mybir.AluOpType
AX = mybir.AxisListType

DEPTH = 2
D = 1024
DIN = 6944
DFF = 2816
NH = 8
C_Q, C_K, C_V, C_CQ, C_CKV, C_KR, C_RW, C_G = 0, 512, 1024, 1536, 1792, 1920, 1952, 3872
RWC = 5128
FULL_SEQS = (8192, 4096, 4096)


class V:
    __slots__ = ("ap", "buf")

    def __init__(self, ap, buf):
        self.ap = ap
        self.buf = buf

    def __getitem__(self, key):
        return V(self.ap[key], self.buf)

    def rearrange(self, pat, **kw):
        return V(self.ap.rearrange(pat, **kw), self.buf)

    def unsqueeze(self, a):
        return V(self.ap.unsqueeze(a), self.buf)

    def bc(self, shape):
        return V(self.ap.broadcast_to(list(shape)), self.buf)

    def bitcast(self, dt):
        return V(self.ap.bitcast(dt), self.buf)


class Buf:
    __slots__ = ("t", "w", "r", "name")

    def __init__(self, t, name):
        self.t = t
        self.w = None
        self.r = {}
        self.name = name

    def __getitem__(self, key):
        return V(self.t[key], self)


class K:
    NDMA = 16

    def __init__(self, nc, es):
        self.nc = nc
        self.es = es
        self.eng = {"pe": nc.tensor, "act": nc.scalar, "dve": nc.vector, "pool": nc.gpsimd, "sp": nc.sync}
        self.sem = {}
        self.cnt = {}
        self.seen = {n: {} for n in self.eng}
        for n in self.eng:
            self.sem[n] = es.enter_context(nc.semaphore("s_" + n))
            self.cnt[n] = 0
        self.dq = {"sp": 0, "pool": 0, "act": 0}
        for q in self.dq:
            for i in range(self.NDMA):
                n = "%s_d%d" % (q, i)
                self.sem[n] = es.enter_context(nc.semaphore("s_" + n))
                self.cnt[n] = 0
        self.uid = 0
        import os as _os
        self.limit = int(_os.environ.get("KLIMIT", "0")) or None
        self.nops = 0
        self.log = [] if _os.environ.get("KLOG") else None

    def _lim(self):
        self.nops += 1
        if self.log is not None:
            import traceback
            fr = traceback.extract_stack(limit=4)[0:2]
            self.log.append((self.nops, [(f.lineno) for f in fr]))
        return self.limit is not None and self.nops > self.limit

    def sb(self, name, shape, dt):
        self.uid += 1
        return Buf(self.es.enter_context(self.nc.sbuf_tensor("%s_%d" % (name, self.uid), list(shape), dt)), name)

    def ps(self, name, shape, dt):
        self.uid += 1
        return Buf(self.es.enter_context(self.nc.psum_tensor("%s_%d" % (name, self.uid), list(shape), dt)), name)

    def rot(self, name, shape, dt, n, psum=False):
        return [(self.ps if psum else self.sb)("%s%d" % (name, i), shape, dt) for i in range(n)]

    def _wait(self, e, deps):
        seen = self.seen[e]
        for key, v in deps.items():
            if key == e and e == "pe":
                continue
            if seen.get(key, 0) >= v:
                continue
            self.eng[e].wait_ge(self.sem[key], v)
            seen[key] = v

    @staticmethod
    def _add(deps, ev):
        if ev is None:
            return
        key, v = ev
        if deps.get(key, 0) < v:
            deps[key] = v

    def _deps(self, reads, writes):
        deps = {}
        for b in reads:
            self._add(deps, b.w)
        for b in writes:
            self._add(deps, b.w)
            for key, v in b.r.items():
                self._add(deps, (key, v))
        return deps

    def _mark(self, ev, reads, writes):
        key, v = ev
        for b in reads:
            if b.r.get(key, 0) < v:
                b.r[key] = v
        for b in writes:
            b.w = ev
            b.r = {}

    def op(self, e, meth, **kw):
        if self._lim():
            return None
        reads, writes = [], []
        args = {}
        for name, val in kw.items():
            if isinstance(val, V):
                (writes if name in ("out", "accum_out") else reads).append(val.buf)
                args[name] = val.ap
            else:
                args[name] = val
        self._wait(e, self._deps(reads, writes))
        ins = getattr(self.eng[e], meth)(**args)
        self.cnt[e] += 1
        ins.then_inc(self.sem[e], 1)
        self._mark((e, self.cnt[e]), reads, writes)
        return ins

    def dma(self, q, out, in_, slow=False):
        if self._lim():
            return None
        reads, writes = [], []
        if isinstance(out, V):
            writes.append(out.buf)
            out = out.ap
        if isinstance(in_, V):
            reads.append(in_.buf)
            in_ = in_.ap
        i = self.dq[q] % self.NDMA
        self.dq[q] += 1
        key = "%s_d%d" % (q, i)
        deps = self._deps(reads, writes)
        if self.cnt[key] > 0:
            self._add(deps, (key, self.cnt[key]))
        self._wait(q, deps)
        if slow:
            ins = self.eng[q].dma_start(out=out, in_=in_, allow_slow_non_contiguous=True)
        else:
            ins = self.eng[q].dma_start(out=out, in_=in_)
        self.cnt[key] += 16
        ins.then_inc(self.sem[key], 16)
        self._mark((key, self.cnt[key]), reads, writes)
        return ins

    def barrier(self):
        allev = {key: v for key, v in self.cnt.items() if v > 0}
        for e in self.eng:
            self._wait(e, dict(allev))

    def act(self, out, in_, func, **kw):
        return self.op("act", "activation", out=out, in_=in_, func=func, **kw)

    def tt(self, e, out, in0, in1, op):
        return self.op(e, "tensor_tensor", out=out, in0=in0, in1=in1, op=op)

    def stt(self, e, out, in0, scalar, in1, op0, op1):
        return self.op(e, "scalar_tensor_tensor", out=out, in0=in0, scalar=scalar, in1=in1, op0=op0, op1=op1)

    def ts(self, e, out, in0, s1, s2, op0, op1=None):
        if op1 is None:
            return self.op(e, "tensor_scalar", out=out, in0=in0, scalar1=s1, scalar2=None, op0=op0)
        return self.op(e, "tensor_scalar", out=out, in0=in0, scalar1=s1, scalar2=s2, op0=op0, op1=op1)

    def copy(self, e, out, in_):
        if e == "act":
            return self.act(out, in_, AF.Copy)
        return self.op(e, "tensor_copy", out=out, in_=in_)

    def memset(self, e, out, val):
        return self.op(e, "memset", ap=out.ap, **{}) if False else self._memset(e, out, val)

    def _memset(self, e, out, val):
        if self._lim():
            return None
        self._wait(e, self._deps([], [out.buf]))
        ins = self.eng[e].memset(out.ap, val)
        self.cnt[e] += 1
        ins.then_inc(self.sem[e], 1)
        self._mark((e, self.cnt[e]), [], [out.buf])
        return ins

    def mm(self, out, lhsT, rhs, start=True, stop=True):
        return self.op("pe", "matmul", out=out, lhsT=lhsT, rhs=rhs, start=start, stop=stop)

    def tr(self, out, in_, ident):
        return self.op("pe", "transpose", out=out, in_=in_, identity=ident)

    def reduce(self, e, out, in_, op=ALU.add):
        return self.op(e, "tensor_reduce", out=out, in_=in_, axis=AX.X, op=op)


class Prog:
    def __init__(self, seqs, dbg=()):
        self.seqs = tuple(seqs)
        self.ntok = sum(seqs)
        self.nt = self.ntok // 128
        self.dbg = set(dbg)
        self.seq_off = [sum(seqs[:i]) for i in range(len(seqs))]
        self.nc = bass.Bass("TRN2", target_bir_lowering=False)
        self.build()

    def din(self, name, shape, dt=F32):
        return self.nc.dram_tensor(name, list(shape), dt, kind="ExternalInput").ap()

    def dscr(self, name, shape, dt=F32):
        kind = "ExternalOutput" if name in self.dbg else "Internal"
        return self.nc.dram_tensor(name, list(shape), dt, kind=kind).ap()

    def seq_of_tile(self, ti):
        tok = ti * 128
        for s, (o, L) in enumerate(zip(self.seq_off, self.seqs)):
            if o <= tok < o + L:
                return s, o, L
        raise AssertionError

    def build(self):
        nc = self.nc
        NT = self.ntok
        W = {}
        self.W = W
        self.x = self.din("x", [NT, D])
        shapes = dict(
            norm1_g=[DEPTH, D], w_in=[DEPTH, D, DIN], b_gate=[DEPTH, 3 * D], na_q_norm=[DEPTH, 64], na_k_norm=[DEPTH, 64],
            na_proj=[DEPTH, 512, D], mla_cq_norm=[DEPTH, 256], mla_ckv_norm=[DEPTH, 128], mla_w_uq=[DEPTH, 256, 768],
            mla_w_ukv=[DEPTH, 128, 1024], mla_q_norm=[DEPTH, 96], mla_k_norm=[DEPTH, 96], mla_proj=[DEPTH, 512, D],
            rw_mu=[DEPTH, 1920], rw_w0=[DEPTH, 2, 512], rw_w_up=[DEPTH, 2, 64, 512], rw_a0=[DEPTH, 2, 512],
            rw_a_up=[DEPTH, 2, 64, 512], rw_g_up=[DEPTH, 128, 512], rw_k_k=[DEPTH, 512], rw_k_a=[DEPTH, 512],
            rw_r_k=[DEPTH, 8, 64], rw_ln_w=[DEPTH, 512], rw_ln_b=[DEPTH, 512], rw_proj=[DEPTH, 512, D],
            w_out=[DEPTH, D, D], norm2_g=[DEPTH, D], ffn_w_gate=[DEPTH, D, DFF], ffn_w_up=[DEPTH, D, DFF],
            ffn_w_down=[DEPTH, DFF, D])
        for n, s in shapes.items():
            W[n] = self.din(n, s)
        self.c_ident = self.din("c_ident", [128, 128])
        self.c_rope = self.din("c_rope", [NT, 32])
        self.c_nab = self.din("c_nab", [DEPTH, 4, 128, 8 * 512])
        self.c_rwm = self.din("c_rwm", [2, 4, 128, 128])
        self.c_eblk = self.din("c_eblk", [128, 512])
        self.y = nc.dram_tensor("y", [NT, D], F32, kind="ExternalOutput").ap()
        S = {}
        self.S = S
        S["xmid"] = self.dscr("xmid", [NT, D])
        S["xres"] = self.dscr("xres", [NT, D])
        S["qT_na"] = self.dscr("qT_na", [512, NT], BF16)
        S["kT_na"] = self.dscr("kT_na", [512, NT], BF16)
        S["v_na"] = self.dscr("v_na", [NT, 512], BF16)
        S["qT_m"] = self.dscr("qT_m", [8, 96, NT], BF16)
        S["kT_m"] = self.dscr("kT_m", [8, 96, NT], BF16)
        S["v_m"] = self.dscr("v_m", [NT, 512], BF16)
        S["rw_raw"] = self.dscr("rw_raw", [NT, 1920])
        S["gates"] = self.dscr("gates", [NT, 3072])
        S["na_out"] = self.dscr("na_out", [NT, 512], BF16)
        S["oT_b"] = self.dscr("oT_b", [512, NT], BF16)
        S["rwpA"] = self.dscr("rwpA", [NT, 2048])
        S["rwpB"] = self.dscr("rwpB", [NT, RWC - 2048])
        S["yd0"] = self.dscr("yd0", [NT, 512])
        S["yd1"] = self.dscr("yd1", [NT, 512])
        S["yc"] = self.dscr("yc", [NT, 512], BF16)

        with ExitStack() as es:
            k = K(nc, es)
            self.k = k
            stop_after = None
            for d in self.dbg:
                if d.startswith("stop:"):
                    stop_after = d[5:]
            done = False
            for l in range(DEPTH):
                xsrc = self.x if l == 0 else S["xres"]
                xdst = self.y if l == DEPTH - 1 else S["xres"]
                for name, fn in (("inproj", lambda: self.pass_inproj(l, xsrc)),
                                 ("na", lambda: self.pass_na(l)),
                                 ("mla", lambda: self.pass_mla(l)),
                                 ("rwprep", lambda: self.pass_rwprep(l)),
                                 ("rwscan", lambda: self.pass_rwscan(l)),
                                 ("rwfin", lambda: self.pass_rwfin(l)),
                                 ("merge", lambda: self.pass_merge(l, xsrc)),
                                 ("ffn", lambda: self.pass_ffn(l, xdst))):
                    if "skip:" + name in self.dbg:
                        continue
                    fn()
                    if stop_after == "%s%d" % (name, l):
                        done = True
                        break
                if done:
                    break
            k.barrier()

    def phase(self):
        prog = self

        class _P:
            def __enter__(s):
                s.st = ExitStack()
                s.old = prog.k.es
                prog.k.es = s.st
                s.st.__enter__()
                return s

            def __exit__(s, *a):
                prog.k.barrier()
                prog.k.es = s.old
                return s.st.__exit__(*a)
        return _P()

    def load_bc(self, dst, src_row):
        self.k.dma("sp", dst, src_row.partition_broadcast(128))

    def load_w(self, dst, src, kc, ncols, gcol=None, chunk=512, f32=False):
        k = self.k
        with ExitStack() as st:
            old = k.es
            k.es = st
            stg = k.rot("stg", [128, kc, chunk], F32, 2)
            i = 0
            for n0 in range(0, ncols, chunk):
                n1 = min(ncols, n0 + chunk)
                s = stg[i % 2]
                i += 1
                k.dma("sp", s[:, :, 0:n1 - n0], src[:, n0:n1].rearrange("(c p) n -> p c n", p=128))
                for c in range(kc):
                    if gcol is not None:
                        k.act(dst[:, c, n0:n1], s[:, c, 0:n1 - n0], AF.Copy, scale=gcol[:, c:c + 1])
                    else:
                        k.copy("dve" if c % 2 else "act", dst[:, c, n0:n1], s[:, c, 0:n1 - n0])
            k.barrier()
            k.es = old

    def rstd(self, out, ss, tmp, scale, eps):
        k = self.k
        k.act(tmp, ss, AF.Ln, scale=scale, bias=eps)
        k.act(out, tmp, AF.Exp, scale=-0.5)

    def pass_inproj(self, l, xsrc):
        k, W, S = self.k, self.W, self.S
        with self.phase():
            Wb = k.sb("Wb", [128, 8, DIN], BF16)
            wuq = k.sb("wuq", [128, 2, 768], BF16)
            wukv = k.sb("wukv", [128, 1, 1024], BF16)
            g1 = k.sb("g1", [128, 8], F32)
            gcq = k.sb("gcq", [128, 2], F32)
            gckv = k.sb("gckv", [128, 1], F32)
            gq = k.sb("gq", [128, 64], F32)
            gk = k.sb("gk", [128, 64], F32)
            mq = k.sb("mq", [128, 96], F32)
            mk = k.sb("mk", [128, 96], F32)
            bg = k.sb("bg", [128, 3072], F32)
            idf = k.sb("idf", [128, 128], F32)
            idb = k.sb("idb", [128, 128], BF16)
            k.dma("sp", g1[:, :], W["norm1_g"][l, :].rearrange("(c p) -> p c", p=128), slow=True)
            k.dma("sp", gcq[:, :], W["mla_cq_norm"][l, :].rearrange("(c p) -> p c", p=128), slow=True)
            k.dma("sp", gckv[:, :], W["mla_ckv_norm"][l, :].rearrange("(c p) -> p c", p=128), slow=True)
            self.load_bc(gq[:, :], W["na_q_norm"][l, :])
            self.load_bc(gk[:, :], W["na_k_norm"][l, :])
            self.load_bc(mq[:, :], W["mla_q_norm"][l, :])
            self.load_bc(mk[:, :], W["mla_k_norm"][l, :])
            self.load_bc(bg[:, :], W["b_gate"][l, :])
            k.dma("sp", idf[:, :], self.c_ident[:, :])
            k.copy("dve", idb[:, :], idf[:, :])
            self.load_w(Wb, W["w_in"][l], 8, DIN, gcol=g1)
            self.load_w(wuq, W["mla_w_uq"][l], 2, 768, gcol=gcq)
            self.load_w(wukv, W["mla_w_ukv"][l], 1, 1024, gcol=gckv)

            XT = k.rot("xt", [128, D], F32, 2)
            junk = k.sb("junk", [128, D], BF16)
            hb = k.sb("hb", [128, D], BF16)
            xT = k.rot("xT", [128, D], BF16, 2)
            st = k.rot("st", [128, 16], F32, 2)
            sq = k.sb("sq", [128, 768], F32)
            qf = k.sb("qf", [128, 512], F32)
            qn = k.rot("qn", [128, 512], BF16, 2)
            qTs = k.rot("qTs", [128, 512], BF16, 2)
            vb = k.rot("vb", [128, 512], BF16, 2)
            ml = k.sb("ml", [128, 416], F32)
            cn = k.sb("cn", [128, 384], BF16)
            cT = k.sb("cT", [128, 384], BF16)
            qm = k.rot("qm", [128, 8, 96], F32, 2)
            rp = k.sb("rp", [128, 4, 8, 16], F32)
            qmb = k.rot("qmb", [128, 8, 96], BF16, 2)
            qmT = k.rot("qmT", [96, 1024], BF16, 2)
            vmb = k.rot("vmb", [128, 8, 64], BF16, 2)
            cs = k.rot("cs", [128, 32], F32, 2)
            ev = k.rot("ev", [128, 512], F32, 4)
            pT = k.ps("pT", [128, 1024], BF16)
            pm = k.rot("pm", [128, 512], F32, 2, psum=True)
            pq = k.rot("pq", [128, 512], F32, 2, psum=True)
            pk = k.rot("pk", [128, 512], F32, 2, psum=True)
            cnt = {"pm": 0, "ev": 0}

            def proj(xTt, n0, n1):
                p = pm[cnt["pm"] % 2]
                cnt["pm"] += 1
                for c in range(8):
                    k.mm(p[:, 0:n1 - n0], xTt[:, c * 128:(c + 1) * 128], Wb[:, c, n0:n1], start=(c == 0), stop=(c == 7))
                return p

            def headnorm(src3, dst3, nh, hd, stt, col, eps, scale):
                k.act(sq[:, 0:nh * hd].rearrange("p (h d) -> p h d", h=nh), src3, AF.Square)
                k.reduce("dve", stt[:, col:col + nh], sq[:, 0:nh * hd].rearrange("p (h d) -> p h d", h=nh))
                self.rstd(stt[:, col:col + nh], stt[:, col:col + nh], stt[:, col + 8:col + 8 + nh], scale, eps)
                k.tt("dve", dst3, src3, stt[:, col:col + nh].unsqueeze(2).bc([128, nh, hd]), ALU.mult)

            def na_qk(p, gain, dst, stt, ti, tok0):
                headnorm(p[:, :].rearrange("p (h d) -> p h d", h=8), qf[:, :].rearrange("p (h d) -> p h d", h=8), 8, 64, stt, 0, 1e-6, 1.0 / 64)
                q_ = qn[cnt["pm"] % 2]
                k.tt("pool", q_[:, :].rearrange("p (h d) -> p h d", h=8), qf[:, :].rearrange("p (h d) -> p h d", h=8),
                     gain[:, :].unsqueeze(1).bc([128, 8, 64]), ALU.mult)
                for j in range(4):
                    k.tr(pT[:, j * 128:(j + 1) * 128], q_[:, j * 128:(j + 1) * 128], idb[:, :])
                qs = qTs[cnt["pm"] % 2]
                k.copy("dve", qs[:, :], pT[:, 0:512])
                k.dma("pool", dst[:, tok0:tok0 + 128].rearrange("(j p) t -> p j t", p=128), qs[:, :].rearrange("p (j t) -> p j t", j=4))

            def normrope(src, gain, dst, stt, col, csb, tok0, par):
                headnorm(src[:, :, :], src[:, :, :], 8, 96, stt, col, 1e-6, 1.0 / 96)
                k.tt("pool", src[:, :, :], src[:, :, :], gain[:, :].unsqueeze(1).bc([128, 8, 96]), ALU.mult)
                x1 = src[:, :, 64:80]
                x2 = src[:, :, 80:96]
                cc = csb[:, 0:16].unsqueeze(1).bc([128, 8, 16])
                ss_ = csb[:, 16:32].unsqueeze(1).bc([128, 8, 16])
                k.tt("dve", rp[:, 0], x1, cc, ALU.mult)
                k.tt("pool", rp[:, 1], x2, ss_, ALU.mult)
                k.tt("dve", rp[:, 2], x1, ss_, ALU.mult)
                k.tt("pool", rp[:, 3], x2, cc, ALU.mult)
                qb = qmb[par]
                k.copy("act", qb[:, :, 0:64], src[:, :, 0:64])
                k.tt("dve", qb[:, :, 64:80], rp[:, 0], rp[:, 1], ALU.subtract)
                k.tt("dve", qb[:, :, 80:96], rp[:, 2], rp[:, 3], ALU.add)
                for h in range(8):
                    k.tr(pT[0:96, h * 128:(h + 1) * 128], qb[:, h, :], idb[:, :])
                qt = qmT[par]
                k.copy("dve", qt[:, :], pT[0:96, :])
                k.dma("pool", dst[:, :, tok0:tok0 + 128].rearrange("h d t -> d h t"), qt[:, :].rearrange("d (h t) -> d h t", h=8))

            for ti in range(self.nt):
                tok0 = ti * 128
                xt = XT[ti % 2]
                stt = st[ti % 2]
                csb = cs[ti % 2]
                k.dma("sp", xt[:, :], xsrc[tok0:tok0 + 128, :])
                k.dma("sp", csb[:, :], self.c_rope[tok0:tok0 + 128, :])
                k._memset("pool", stt[:, :], 0.0)
                k.act(junk[:, :], xt[:, :], AF.Square, accum_out=stt[:, 0:1])
                self.rstd(stt[:, 1:2], stt[:, 0:1], stt[:, 2:3], 1.0 / D, 1e-6)
                k.act(hb[:, :], xt[:, :], AF.Copy, scale=stt[:, 1:2])
                for c in range(8):
                    k.tr(pT[:, c * 128:(c + 1) * 128], hb[:, c * 128:(c + 1) * 128], idb[:, :])
                xTt = xT[ti % 2]
                k.copy("dve", xTt[:, :], pT[:, :])
                stn = st[(ti + 1) % 2]
                p = proj(xTt, C_Q, C_Q + 512)
                na_qk(p, gq, S["qT_na"], stt, ti, tok0)
                p = proj(xTt, C_K, C_K + 512)
                na_qk(p, gk, S["kT_na"], stt, ti, tok0)
                p = proj(xTt, C_V, C_V + 512)
                v_ = vb[ti % 2]
                k.copy("act", v_[:, :], p[:, :])
                k.dma("pool", S["v_na"][tok0:tok0 + 128, :], v_[:, :])
                p = proj(xTt, C_CQ, C_CQ + 416)
                k.copy("act", ml[:, :], p[:, 0:416])
                k._memset("pool", stt[:, 4:6], 0.0)
                k.act(junk[:, 0:256], ml[:, 0:256], AF.Square, accum_out=stt[:, 4:5])
                k.act(junk[:, 256:384], ml[:, 256:384], AF.Square, accum_out=stt[:, 5:6])
                k.act(stt[:, 6:7], stt[:, 4:5], AF.Ln, scale=1.0 / 256, bias=1e-6)
                k.act(stt[:, 7:8], stt[:, 5:6], AF.Ln, scale=1.0 / 128, bias=1e-6)
                k.act(stt[:, 4:6], stt[:, 6:8], AF.Exp, scale=-0.5)
                k.act(cn[:, 0:256], ml[:, 0:256], AF.Copy, scale=stt[:, 4:5])
                k.act(cn[:, 256:384], ml[:, 256:384], AF.Copy, scale=stt[:, 5:6])
                for j in range(3):
                    k.tr(pT[:, j * 128:(j + 1) * 128], cn[:, j * 128:(j + 1) * 128], idb[:, :])
                k.copy("dve", cT[:, :], pT[:, 0:384])
                for half in range(2):
                    for c in range(2):
                        k.mm(pq[half][:, 0:384], cT[:, c * 128:(c + 1) * 128], wuq[:, c, half * 384:(half + 1) * 384],
                             start=(c == 0), stop=(c == 1))
                    k.mm(pk[half][:, :], cT[:, 256:384], wukv[:, 0, half * 512:(half + 1) * 512])
                qm_ = qm[0]
                km_ = qm[1]
                for half in range(2):
                    k.copy("act", qm_[:, half * 4:(half + 1) * 4, :], pq[half][:, 0:384].rearrange("p (h d) -> p h d", h=4))
                    k.copy("dve", km_[:, half * 4:(half + 1) * 4, 0:64],
                           pk[half][:, :].rearrange("p (h d) -> p h d", h=4)[:, :, 0:64])
                    k.copy("dve", vmb[ti % 2][:, half * 4:(half + 1) * 4, :],
                           pk[half][:, :].rearrange("p (h d) -> p h d", h=4)[:, :, 64:128])
                k.copy("pool", km_[:, :, 64:96], ml[:, 384:416].unsqueeze(1).bc([128, 8, 32]))
                k.dma("pool", S["v_m"][tok0:tok0 + 128, :], vmb[ti % 2][:, :, :].rearrange("p h d -> p (h d)"))
                normrope(qm_, mq, S["qT_m"], stn, 0, csb, tok0, 0)
                normrope(km_, mk, S["kT_m"], stn, 0, csb, tok0, 1)
                for n0 in range(C_RW, C_G, 512):
                    n1 = min(C_G, n0 + 512)
                    p = proj(xTt, n0, n1)
                    e_ = ev[cnt["ev"] % 4]
                    cnt["ev"] += 1
                    k.copy("dve", e_[:, 0:n1 - n0], p[:, 0:n1 - n0])
                    k.dma("pool", S["rw_raw"][tok0:tok0 + 128, n0 - C_RW:n1 - C_RW], e_[:, 0:n1 - n0])
                for n0 in range(C_G, DIN, 512):
                    p = proj(xTt, n0, n0 + 512)
                    e_ = ev[cnt["ev"] % 4]
                    cnt["ev"] += 1
                    k.tt("dve", e_[:, :], p[:, :], bg[:, n0 - C_G:n0 - C_G + 512], ALU.add)
                    k.act(e_[:, :], e_[:, :], AF.Sigmoid)
                    k.dma("pool", S["gates"][tok0:tok0 + 128, n0 - C_G:n0 - C_G + 512], e_[:, :])

    def pass_na(self, l):
        k, S = self.k, self.S
        with self.phase():
            Lmax = max(self.seqs)
            Rmax = Lmax // 64
            KT = k.rot("KT", [64, 2, Lmax], BF16, 1)
            QT = k.rot("QT", [64, 2, Lmax], BF16, 1)
            VE = k.rot("VE", [128, Rmax // 2, 2, 66], BF16, 2)
            VO = k.rot("VO", [128, Rmax // 2, 2, 66], BF16, 2)
            BT = k.rot("BT", [128, 8, 512], F32, 2)
            sbs = k.rot("sbs", [128, 512], F32, 3)
            pts = k.rot("pts", [128, 512], BF16, 3)
            rc = k.rot("rc", [64, 2, 1], F32, 2)
            ob = k.rot("ob", [64, 4, 2, 64], BF16, 2)
            pS = k.rot("pS", [128, 512], F32, 2, psum=True)
            pO = k.rot("pO", [128, 512], F32, 2, psum=True)
            it = 0
            u = 0
            for s, (t0, L) in enumerate(zip(self.seq_off, self.seqs)):
                R = L // 64
                for hp in range(4):
                    kt, qt, ve, vo, bt = KT[0], QT[0], VE[it % 2], VO[it % 2], BT[it % 2]
                    it += 1
                    for h2 in range(2):
                        hh = 2 * hp + h2
                        k.dma("sp", kt[:, h2, 0:L], S["kT_na"][hh * 64:(hh + 1) * 64, t0:t0 + L])
                        k.dma("sp", qt[:, h2, 0:L], S["qT_na"][hh * 64:(hh + 1) * 64, t0:t0 + L])
                    k.dma("sp", bt[:, :, :], self.c_nab[l, hp, :, :].rearrange("p (c n) -> p c n", c=8))
                    k._memset("pool", ve[:, :, :, :], 1.0)
                    k._memset("pool", vo[:, :, :, :], 1.0)
                    for h2 in range(2):
                        c0 = (2 * hp + h2) * 64
                        k.dma("sp", ve[:, 0:R // 2, h2, 0:64],
                              S["v_na"][t0:t0 + L, c0:c0 + 64].rearrange("(c p) d -> p c d", p=128))
                        k.dma("sp", vo[:, 0:R // 2 - 1, h2, 0:64],
                              S["v_na"][t0 + 64:t0 + L - 64, c0:c0 + 64].rearrange("(c p) d -> p c d", p=128))
                    for r in range(R):
                        rs = min(max(r - 4, 0), R - 8)
                        dcase = r - rs
                        ps_ = pS[u % 2]
                        po = pO[u % 2]
                        for h2 in range(2):
                            for kc in range(4):
                                col = (h2 * 4 + kc) * 64
                                k.mm(ps_[:, col:col + 64], kt[:, h2, rs * 64 + kc * 128:rs * 64 + (kc + 1) * 128],
                                     qt[:, h2, r * 64:(r + 1) * 64])
                        sb_ = sbs[u % 3]
                        k.stt("dve", sb_[:, :], ps_[:, :], 0.125, bt[:, dcase, :], ALU.mult, ALU.add)
                        pt = pts[u % 3]
                        k.act(pt[:, :], sb_[:, :], AF.Exp)
                        vbuf, cb = (ve, rs // 2) if rs % 2 == 0 else (vo, (rs - 1) // 2)
                        for h2 in range(2):
                            for kc in range(4):
                                col = (h2 * 4 + kc) * 64
                                k.mm(po[0:64, h2 * 128:h2 * 128 + 66], pt[:, col:col + 64], vbuf[:, cb + kc, h2, :],
                                     start=(kc == 0), stop=(kc == 3))
                        po3 = po[0:64, 0:256].rearrange("q (h d) -> q h d", h=2)
                        rc_ = rc[u % 2]
                        k.op("dve", "reciprocal", out=rc_[:, :, :], in_=po3[:, :, 64:65])
                        ob_ = ob[(r // 4) % 2]
                        k.tt("dve", ob_[:, r % 4, :, :], po3[:, :, 0:64], rc_[:, :, :].bc([64, 2, 64]), ALU.mult)
                        u += 1
                        if r % 4 == 3:
                            r0 = r - 3
                            k.dma("pool", S["na_out"][t0 + r0 * 64:t0 + r0 * 64 + 256, hp * 128:(hp + 1) * 128].rearrange("(r q) c -> q r c", q=64),
                                  ob_[:, :, :, :].rearrange("q r h d -> q r (h d)"))

    def pass_mla(self, l):
        k, S = self.k, self.S
        with self.phase():
            Lmax = max(self.seqs)
            QT = k.rot("QT", [96, Lmax], BF16, 2)
            KT = k.rot("KT", [96, Lmax], BF16, 2)
            VA = k.rot("VA", [128, Lmax // 128, 128], BF16, 2)
            pts = k.rot("pts", [128, 512], BF16, 4)
            rcs = k.rot("rcs", [128, 512], F32, 2)
            ots = k.rot("ots", [64, 512], BF16, 2)
            pS = k.rot("pS", [128, 512], F32, 3, psum=True)
            pO = k.rot("pO", [128, 512], F32, 2, psum=True)
            scale = 96 ** -0.5
            it = 0
            u = 0
            g = 0
            for s, (t0, L) in enumerate(zip(self.seq_off, self.seqs)):
                nkt = L // 128
                for h in range(8):
                    qt, kt, va = QT[it % 2], KT[it % 2], VA[it % 2]
                    it += 1
                    k.dma("sp", qt[:, 0:L], S["qT_m"][h, :, t0:t0 + L])
                    k.dma("sp", kt[:, 0:L], S["kT_m"][h, :, t0:t0 + L])
                    k._memset("pool", va[:, :, 64:128], 1.0)
                    k.dma("sp", va[:, 0:nkt, 0:64], S["v_m"][t0:t0 + L, h * 64:(h + 1) * 64].rearrange("(c p) d -> p c d", p=128))
                    for qc in range(L // 512):
                        po = pO[g % 2]
                        q_ = qt[:, qc * 512:(qc + 1) * 512]
                        LOOK = 2
                        pend = []
                        for j in range(nkt + LOOK):
                            if j < nkt:
                                ps_ = pS[u % 3]
                                u += 1
                                k.mm(ps_[:, :], kt[:, j * 128:(j + 1) * 128], q_)
                                pend.append(ps_)
                            if j >= LOOK:
                                jj = j - LOOK
                                ps_ = pend[jj]
                                pt = pts[jj % 4]
                                k.act(pt[:, :], ps_[:, :], AF.Exp, scale=scale)
                                k.mm(po[:, :], va[:, jj, :], pt[:, :], start=(jj == 0), stop=(jj == nkt - 1))
                        rc_ = rcs[g % 2]
                        k.op("dve", "reciprocal", out=rc_[64:128, :], in_=po[64:128, :])
                        ot = ots[g % 2]
                        k.tt("dve", ot[:, :], po[0:64, :], rc_[64:128, :], ALU.mult)
                        k.dma("pool", S["oT_b"][h * 64:(h + 1) * 64, t0 + qc * 512:t0 + (qc + 1) * 512], ot[:, :])
                        g += 1

    def pass_rwprep(self, l):
        k, W, S = self.k, self.W, self.S
        with self.phase():
            mu = k.sb("mu", [128, 1920], F32)
            kkb = k.sb("kkb", [128, 512], F32)
            kab = k.sb("kab", [128, 512], F32)
            rkb = k.sb("rkb", [128, 512], F32)
            w0b = k.sb("w0b", [128, 2, 512], F32)
            a0b = k.sb("a0b", [128, 2, 512], F32)
            wup = k.sb("wup", [64, 2, 512], F32)
            aup = k.sb("aup", [64, 2, 512], F32)
            gup = k.sb("gup", [128, 512], F32)
            idf = k.sb("idf", [128, 128], F32)
            self.load_bc(mu[:, :], W["rw_mu"][l, :])
            self.load_bc(kkb[:, :], W["rw_k_k"][l, :])
            self.load_bc(kab[:, :], W["rw_k_a"][l, :])
            self.load_bc(rkb[:, :], W["rw_r_k"][l, :, :].rearrange("h d -> (h d)"))
            for d_ in range(2):
                self.load_bc(w0b[:, d_, :], W["rw_w0"][l, d_, :])
                self.load_bc(a0b[:, d_, :], W["rw_a0"][l, d_, :])
            k.dma("sp", wup[:, :, :], W["rw_w_up"][l, :, :, :].rearrange("d r c -> r d c"))
            k.dma("sp", aup[:, :, :], W["rw_a_up"][l, :, :, :].rearrange("d r c -> r d c"))
            k.dma("sp", gup[:, :], W["rw_g_up"][l, :, :])
            k.dma("sp", idf[:, :], self.c_ident[:, :])
            cur = k.rot("cur", [128, 1920], F32, 2)
            prv = k.rot("prv", [128, 1920], F32, 2)
            nxt = k.rot("nxt", [128, 1920], F32, 2)
            pp = k.sb("pp", [128, 1920], F32)
            dd = k.sb("dd", [128, 1920], F32)
            O = k.rot("O", [128, RWC], F32, 2)
            t1 = k.sb("t1", [128, 512], F32)
            t2 = k.sb("t2", [128, 512], F32)
            t3 = k.sb("t3", [128, 512], F32)
            a_ = k.sb("a_", [128, 512], F32)
            st = k.rot("st", [128, 24], F32, 2)
            sm = k.sb("sm", [128, 384], F32)
            tT = k.sb("tT", [64, 4, 128], F32)
            gT = k.sb("gT", [128, 128], F32)
            pT = k.ps("pT", [128, 512], F32)
            pT2 = k.ps("pT2", [128, 512], F32)
            pw = k.rot("pw", [128, 512], F32, 2, psum=True)
            npw = 0
            seq_starts = set(self.seq_off)
            seq_ends = set(o + L for o, L in zip(self.seq_off, self.seqs))

            def col(o, i):
                return o[:, i * 512:(i + 1) * 512]

            for ti in range(self.nt):
                tok0 = ti * 128
                c_, p_, n_, o = cur[ti % 2], prv[ti % 2], nxt[ti % 2], O[ti % 2]
                stt = st[ti % 2]
                k.dma("sp", c_[:, :], S["rw_raw"][tok0:tok0 + 128, :])
                if tok0 in seq_starts:
                    k._memset("pool", p_[0:32, :], 0.0)
                    k.dma("sp", p_[1:128, :], S["rw_raw"][tok0:tok0 + 127, :])
                else:
                    k.dma("sp", p_[:, :], S["rw_raw"][tok0 - 1:tok0 + 127, :])
                if tok0 + 128 in seq_ends:
                    k._memset("pool", n_[96:128, :], 0.0)
                    k.dma("sp", n_[0:127, :], S["rw_raw"][tok0 + 1:tok0 + 128, :])
                else:
                    k.dma("sp", n_[:, :], S["rw_raw"][tok0 + 1:tok0 + 129, :])
                k.tt("pool", dd[:, :], p_[:, :], n_[:, :], ALU.add)
                k.stt("dve", dd[:, :], dd[:, :], 0.5, c_[:, :], ALU.mult, ALU.subtract)
                k.tt("pool", dd[:, :], dd[:, :], mu[:, :], ALU.mult)
                k.tt("dve", pp[:, :], dd[:, :], c_[:, :], ALU.add)
                rr, kk_, vv = pp[:, 0:512], pp[:, 512:1024], pp[:, 1024:1536]
                k.copy("act", col(o, 0), rr)
                k.copy("act", col(o, 2), vv)
                k.tt("pool", t1[:, :], kk_, kkb[:, :], ALU.mult)
                k.act(t2[:, :], t1[:, :], AF.Square)
                k.reduce("dve", stt[:, 0:8], t2[:, :].rearrange("p (h d) -> p h d", h=8))
                self.rstd(stt[:, 0:8], stt[:, 0:8], stt[:, 8:16], 1.0, 1e-12)
                k.tt("dve", col(o, 1).rearrange("p (h d) -> p h d", h=8), t1[:, :].rearrange("p (h d) -> p h d", h=8),
                     stt[:, 0:8].unsqueeze(2).bc([128, 8, 64]), ALU.mult)
                k.act(sm[:, 0:128], pp[:, 1536:1664], AF.Tanh)
                k.copy("dve", sm[:, 128:256], pp[:, 1664:1792])
                k.act(sm[:, 256:384], pp[:, 1792:1920], AF.Sigmoid)
                for j in range(4):
                    k.tr(pT[0:64, j * 128:(j + 1) * 128], sm[:, j * 64:(j + 1) * 64], idf[:, :])
                k.tr(pT2[:, 0:128], sm[:, 256:384], idf[:, :])
                k.copy("dve", tT[:, :, :], pT[0:64, :].rearrange("p (j t) -> p j t", j=4))
                k.copy("dve", gT[:, :], pT2[:, 0:128])
                for d_ in range(2):
                    p = pw[npw % 2]
                    npw += 1
                    k.mm(p[:, :], tT[:, d_, :], wup[:, d_, :])
                    k.tt("dve", t2[:, :], p[:, :], w0b[:, d_, :], ALU.add)
                    k.act(t2[:, :], t2[:, :], AF.Sigmoid)
                    k.ts("pool", col(o, 4 + 3 * d_), t2[:, :], -0.6065306597126334, None, ALU.mult)
                    p = pw[npw % 2]
                    npw += 1
                    k.mm(p[:, :], tT[:, 2 + d_, :], aup[:, d_, :])
                    k.tt("dve", a_[:, :], p[:, :], a0b[:, d_, :], ALU.add)
                    k.act(a_[:, :], a_[:, :], AF.Sigmoid)
                    k.tt("pool", col(o, 6 + 3 * d_), a_[:, :], col(o, 1), ALU.mult)
                    k.stt("dve", t3[:, :], a_[:, :], -1.0, kab[:, :], ALU.add, ALU.mult)
                    k.stt("dve", col(o, 5 + 3 * d_), t3[:, :], 1.0, kk_, ALU.add, ALU.mult)
                p = pw[npw % 2]
                npw += 1
                k.mm(p[:, :], gT[:, :], gup[:, :])
                k.copy("act", col(o, 3), p[:, :])
                k.tt("pool", t1[:, :], col(o, 5), col(o, 8), ALU.add)
                k.tt("pool", t1[:, :], t1[:, :], rr, ALU.mult)
                k.tt("pool", t1[:, :], t1[:, :], rkb[:, :], ALU.mult)
                k.reduce("dve", o[:, 5120:5128], t1[:, :].rearrange("p (h d) -> p h d", h=8))
                k.dma("pool", S["rwpA"][tok0:tok0 + 128, :], o[:, 0:2048])
                k.dma("pool", S["rwpB"][tok0:tok0 + 128, :], o[:, 2048:RWC])

    def pass_rwscan(self, l):
        k, S = self.k, self.S
        with self.phase():
            idf = k.sb("idf", [128, 128], F32)
            eblk = k.sb("eblk", [128, 512], F32)
            k.dma("sp", idf[:, :], self.c_ident[:, :])
            idb = k.sb("idb", [128, 128], BF16)
            k.copy("dve", idb[:, :], idf[:, :])
            Vb = k.sb("Vb", [128, 512], BF16)
            k.dma("sp", eblk[:, :], self.c_eblk[:, :])
            M4 = k.sb("M4", [128, 4, 128], F32)
            MI = k.sb("MI", [128, 128], F32)
            BLK = k.sb("BLK", [128, 128], F32)
            A = k.rot("A", [128, 1536], F32, 2)
            Bd = k.rot("Bd", [128, 1536], F32, 2)
            cumS = k.sb("cumS", [128, 512], F32)
            tmp = k.sb("tmp", [128, 512], F32)
            e1 = k.sb("e1", [128, 512], F32)
            e2 = k.sb("e2", [128, 512], F32)
            e3 = k.sb("e3", [128, 512], F32)
            e4 = k.sb("e4", [128, 512], F32)
            e5 = k.sb("e5", [128, 512], F32)
            Abar = k.sb("Abar", [128, 512], BF16)
            Rbar = k.sb("Rbar", [128, 512], BF16)
            Kt = k.sb("Kt", [128, 512], BF16)
            Bt = k.sb("Bt", [128, 512], BF16)
            Kcs = [k.sb("Kc%d" % i, [128, 512], BF16) for i in range(2)]
            Bcs = [k.sb("Bc%d" % i, [128, 512], BF16) for i in range(2)]
            e4c = [k.sb("e4c%d" % i, [128, 512], F32) for i in range(2)]
            Yd = k.sb("Yd", [128, 512], BF16)
            XTs = [k.sb("XT%d" % i, [64, 8, 128], BF16) for i in range(4)]
            PM = k.sb("PM", [128, 8, 4, 128], BF16)
            MRK = k.sb("MRK", [128, 8, 128], BF16)
            Xs = k.rot("Xs", [128, 8, 128], BF16, 2)
            XsT = k.rot("XsT", [128, 8, 128], BF16, 2)
            Z = k.rot("Z", [128, 8, 128], BF16, 2)
            NZ = k.sb("NZ", [128, 8, 128], BF16)
            RTa = k.sb("RTa", [64, 8, 128], F32)
            RTb = k.sb("RTb", [64, 8, 128], F32)
            Y0 = k.sb("Y0", [128, 512], F32)
            GT = k.sb("GT", [64, 2, 8, 64], F32)
            HS = k.sb("HS", [64, 2, 8, 64], F32)
            ST = k.rot("ST", [64, 8, 64], F32, 2)
            yo = k.rot("yo", [128, 512], F32, 2)
            pA = k.rot("pA", [128, 512], F32, 6, psum=True)
            pYs = k.rot("pY", [128, 512], F32, 2, psum=True)
            k._memset("pool", RTa[:, :, :], 0.0)
            k._memset("pool", RTb[:, :, :], 0.0)
            npa = [0]

            def bank():
                b = pA[npa[0] % 6]
                npa[0] += 1
                return b

            for dr in range(2):
                k.dma("sp", M4[:, 0, :], self.c_rwm[dr, 0])
                k.dma("sp", M4[:, 1, :], self.c_rwm[dr, 1])
                k.dma("sp", M4[:, 2, :], self.c_rwm[dr, 0])
                k.dma("sp", M4[:, 3, :], self.c_rwm[dr, 2])
                k.dma("sp", MI[:, :], self.c_rwm[dr, 2])
                k.dma("sp", BLK[:, :], self.c_rwm[dr, 3])
                ydst = S["yd%d" % dr]
                sti = 0
                for s, (t0, L) in enumerate(zip(self.seq_off, self.seqs)):
                    ntl = L // 128
                    tiles = range(ntl) if dr == 0 else range(ntl - 1, -1, -1)
                    st_cur = ST[sti % 2]
                    k._memset("pool", st_cur[:, :, :], 0.0)
                    for tl in tiles:
                        tok0 = t0 + tl * 128
                        a, b = A[tl % 2], Bd[tl % 2]
                        k.dma("sp", a[:, :], S["rwpA"][tok0:tok0 + 128, 0:1536])
                        k.dma("sp", b[:, :], S["rwpB"][tok0:tok0 + 128, dr * 1536:(dr + 1) * 1536])
                        Rr, KK, Vv = a[:, 0:512], a[:, 512:1024], a[:, 1024:1536]
                        LW, KD, BE = b[:, 0:512], b[:, 512:1024], b[:, 1024:1536]
                        pc = bank()
                        pcc = bank()
                        k.mm(pc[:, :], MI[:, :], LW)
                        k.mm(pcc[:, :], BLK[:, :], LW)
                        k.copy("act", cumS[:, :], pc[:, :])
                        k.act(e1[:, :], pc[:, :], AF.Exp)
                        k.act(e3[:, :], pc[:, :], AF.Exp, scale=-1.0)
                        k.tt("pool", tmp[:, :], cumS[:, :], LW, ALU.subtract)
                        k.act(e2[:, :], tmp[:, :], AF.Exp)
                        k.copy("dve", e5[:, :], pcc[:, :])
                        k.tt("pool", e4[:, :], e5[:, :], cumS[:, :], ALU.subtract)
                        k.act(e4[:, :], e4[:, :], AF.Exp)
                        k.act(e5[:, :], e5[:, :], AF.Exp)
                        k.tt("pool", Abar[:, :], KK, e2[:, :], ALU.mult)
                        k.tt("dve", Rbar[:, :], Rr, e1[:, :], ALU.mult)
                        k.tt("pool", Kt[:, :], KD, e3[:, :], ALU.mult)
                        k.tt("dve", Bt[:, :], BE, e3[:, :], ALU.mult)
                        for c in range(2):
                            k.ts("pool" if c else "dve", e4c[c][:, :], e4[:, :], BLK[:, c * 127:c * 127 + 1], None, ALU.mult)
                            k.tt("pool", Kcs[c][:, :], KD, e4c[c][:, :], ALU.mult)
                            k.tt("dve", Bcs[c][:, :], BE, e4c[c][:, :], ALU.mult)
                        k.tt("pool", Yd[:, :], eblk[:, :], e5[:, :], ALU.mult)
                        k.copy("act", Vb[:, :], Vv)
                        for i, X in enumerate((Abar, Bt, Kt, Rbar)):
                            for half in range(2):
                                p = bank()
                                pb = p[:, :].bitcast(BF16)
                                for hh in range(4):
                                    h = half * 4 + hh
                                    k.tr(pb[0:64, hh * 128:(hh + 1) * 128], X[:, h * 64:(h + 1) * 64], idb[:, :])
                                k.copy("act" if half else "dve", XTs[i][:, half * 4:(half + 1) * 4, :],
                                       pb[0:64, 0:512].rearrange("p (j t) -> p j t", j=4))
                        aT, bT, kT_, rT = XTs

                        def hv(X, h):
                            return X[:, h, :]
                        for h in range(8):
                            p = bank()
                            k.mm(p[:, 0:128], hv(bT, h), hv(aT, h))
                            k.mm(p[:, 128:256], hv(aT, h), hv(bT, h))
                            k.mm(p[:, 256:384], hv(kT_, h), hv(aT, h))
                            k.mm(p[:, 384:512], hv(bT, h), hv(rT, h))
                            k.tt("dve", PM[:, h, :, :], p[:, :].rearrange("p (j t) -> p j t", j=4), M4[:, :, :], ALU.mult)
                        for half in range(2):
                            p = bank()
                            for hh in range(4):
                                h = half * 4 + hh
                                k.mm(p[:, hh * 128:(hh + 1) * 128], hv(kT_, h), hv(rT, h))
                            k.tt("dve", MRK[:, half * 4:(half + 1) * 4, :], p[:, :].rearrange("p (j t) -> p j t", j=4),
                                 MI[:, :].unsqueeze(1).bc([128, 4, 128]), ALU.mult)
                        z = Z[0]
                        zi = 0
                        p = bank()
                        for h in range(8):
                            k.mm(p[:, h * 64:(h + 1) * 64], PM[:, h, 2, :], Vb[:, h * 64:(h + 1) * 64])
                        k.copy("act", z[:, :, 0:64], Abar[:, :].rearrange("p (h d) -> p h d", h=8))
                        k.copy("dve", z[:, :, 64:128], p[:, :].rearrange("p (h d) -> p h d", h=8))
                        curX = PM[:, :, 1, :]
                        curXT = PM[:, :, 0, :]
                        for lev in range(6):
                            znew = Z[(zi + 1) % 2]
                            for half in range(2):
                                p = bank()
                                for hh in range(4):
                                    h = half * 4 + hh
                                    k.mm(p[:, hh * 128:(hh + 1) * 128], curXT[:, h, :], z[:, h, :])
                                k.tt("dve", znew[:, half * 4:(half + 1) * 4, :], z[:, half * 4:(half + 1) * 4, :],
                                     p[:, :].rearrange("p (j t) -> p j t", j=4), ALU.subtract if lev == 0 else ALU.add)
                            z = znew
                            zi += 1
                            if lev == 5:
                                break
                            nX, nXT = Xs[lev % 2], XsT[lev % 2]
                            for half in range(2):
                                p = bank()
                                for hh in range(4):
                                    h = half * 4 + hh
                                    k.mm(p[:, hh * 128:(hh + 1) * 128], curX[:, h, :], curXT[:, h, :])
                                k.copy("act", nXT[:, half * 4:(half + 1) * 4, :], p[:, :].rearrange("p (j t) -> p j t", j=4))
                                if lev < 4:
                                    p = bank()
                                    for hh in range(4):
                                        h = half * 4 + hh
                                        k.mm(p[:, hh * 128:(hh + 1) * 128], curXT[:, h, :], curX[:, h, :])
                                    k.copy("act", nX[:, half * 4:(half + 1) * 4, :], p[:, :].rearrange("p (j t) -> p j t", j=4))
                            curX = nX[:, :, :]
                            curXT = nXT[:, :, :]
                        k.ts("pool", NZ[:, :, :], z[:, :, :], -1.0, None, ALU.mult)
                        for half in range(2):
                            p = bank()
                            for hh in range(4):
                                h = half * 4 + hh
                                k.mm(p[0:64, hh * 128:(hh + 1) * 128], Rbar[:, h * 64:(h + 1) * 64], idb[:, :], start=True, stop=False)
                                k.mm(p[0:64, hh * 128:(hh + 1) * 128], NZ[:, h, 0:64], PM[:, h, 3, :], start=False, stop=True)
                            p3 = p[0:64, :].rearrange("p (j t) -> p j t", j=4)
                            k.copy("dve", RTa[:, half * 4:(half + 1) * 4, 0:64], p3[:, :, 0:64])
                            k.copy("dve", RTb[:, half * 4:(half + 1) * 4, 64:128], p3[:, :, 64:128])
                        p = bank()
                        for h in range(8):
                            k.mm(p[:, h * 64:(h + 1) * 64], MRK[:, h, :], Vb[:, h * 64:(h + 1) * 64], start=True, stop=False)
                            k.mm(p[:, h * 64:(h + 1) * 64], PM[:, h, 3, :], NZ[:, h, 64:128], start=False, stop=True)
                        k.copy("act", Y0[:, :], p[:, :])
                        for c in range(2):
                            cs_ = slice(c * 64, (c + 1) * 64)
                            p = bank()
                            p2 = bank()
                            for h in range(8):
                                hs_ = slice(h * 64, (h + 1) * 64)
                                k.mm(p[0:64, hs_], NZ[:, h, 0:64], Bcs[c][:, hs_], start=True, stop=False)
                                k.mm(p[0:64, hs_], idb[:, cs_], Yd[:, hs_], start=False, stop=True)
                                k.mm(p2[0:64, hs_], Kcs[c][:, hs_], Vb[:, hs_], start=True, stop=False)
                                k.mm(p2[0:64, hs_], Bcs[c][:, hs_], NZ[:, h, 64:128], start=False, stop=True)
                            k.copy("act", GT[:, c, :, :], p[0:64, :].rearrange("p (h d) -> p h d", h=8))
                            k.copy("dve", HS[:, c, :, :], p2[0:64, :].rearrange("p (h d) -> p h d", h=8))
                        order = (0, 1) if dr == 0 else (1, 0)
                        for ci, c in enumerate(order):
                            RTx = RTa if c == 0 else RTb
                            for h in range(8):
                                k.mm(pYs[ci][:, h * 64:(h + 1) * 64], RTx[:, h, :], st_cur[:, h, :])
                            p = bank()
                            for h in range(8):
                                k.mm(p[0:64, h * 64:(h + 1) * 64], GT[:, c, h, :], st_cur[:, h, :])
                            sti += 1
                            st_new = ST[sti % 2]
                            k.tt("dve", st_new[:, :, :], p[0:64, :].rearrange("p (h d) -> p h d", h=8), HS[:, c, :, :], ALU.add)
                            st_cur = st_new
                        yo_ = yo[tl % 2]
                        k.tt("dve", yo_[:, :], pYs[0][:, :], Y0[:, :], ALU.add)
                        k.tt("dve", yo_[:, :], pYs[1][:, :], yo_[:, :], ALU.add)
                        k.dma("pool", ydst[tok0:tok0 + 128, :], yo_[:, :])

    def pass_rwfin(self, l):
        k, W, S = self.k, self.W, self.S
        with self.phase():
            lnw = k.sb("lnw", [128, 512], F32)
            lnb = k.sb("lnb", [128, 512], F32)
            self.load_bc(lnw[:, :], W["rw_ln_w"][l, :])
            self.load_bc(lnb[:, :], W["rw_ln_b"][l, :])
            yf = k.rot("yf", [128, 512], F32, 2)
            yb = k.rot("yb", [128, 512], F32, 2)
            vg = k.rot("vg", [128, 1024], F32, 2)
            bo = k.rot("bo", [128, 8], F32, 2)
            y = k.sb("y", [128, 512], F32)
            t = k.sb("t", [128, 512], F32)
            st = k.rot("st", [128, 32], F32, 2)
            ob = k.rot("ob", [128, 512], BF16, 2)

            def h3(v):
                return v.rearrange("p (h d) -> p h d", h=8)
            for ti in range(self.nt):
                tok0 = ti * 128
                a, b, vg_, bo_, stt = yf[ti % 2], yb[ti % 2], vg[ti % 2], bo[ti % 2], st[ti % 2]
                k.dma("sp", a[:, :], S["yd0"][tok0:tok0 + 128, :])
                k.dma("sp", b[:, :], S["yd1"][tok0:tok0 + 128, :])
                k.dma("sp", vg_[:, :], S["rwpA"][tok0:tok0 + 128, 1024:2048])
                k.dma("sp", bo_[:, :], S["rwpB"][tok0:tok0 + 128, 3072:3080])
                k.tt("pool", y[:, :], a[:, :], b[:, :], ALU.add)
                k.reduce("dve", stt[:, 0:8], h3(y[:, :]))
                k.ts("dve", stt[:, 0:8], stt[:, 0:8], 1.0 / 64, None, ALU.mult)
                k.tt("dve", h3(y[:, :]), h3(y[:, :]), stt[:, 0:8].unsqueeze(2).bc([128, 8, 64]), ALU.subtract)
                k.act(t[:, :], y[:, :], AF.Square)
                k.reduce("dve", stt[:, 8:16], h3(t[:, :]))
                self.rstd(stt[:, 8:16], stt[:, 8:16], stt[:, 16:24], 1.0 / 64, 64e-5)
                k.tt("dve", h3(y[:, :]), h3(y[:, :]), stt[:, 8:16].unsqueeze(2).bc([128, 8, 64]), ALU.mult)
                k.tt("pool", y[:, :], y[:, :], lnw[:, :], ALU.mult)
                k.tt("pool", y[:, :], y[:, :], lnb[:, :], ALU.add)
                k.tt("dve", h3(t[:, :]), h3(vg_[:, 0:512]), bo_[:, :].unsqueeze(2).bc([128, 8, 64]), ALU.mult)
                k.tt("pool", y[:, :], y[:, :], t[:, :], ALU.add)
                k.tt("dve", ob[ti % 2][:, :], y[:, :], vg_[:, 512:1024], ALU.mult)
                k.dma("pool", S["yc"][tok0:tok0 + 128, :], ob[ti % 2][:, :])

    def pass_merge(self, l, xsrc):
        k, W, S = self.k, self.W, self.S
        with self.phase():
            idf = k.sb("idf", [128, 128], F32)
            idb = k.sb("idb", [128, 128], BF16)
            k.dma("sp", idf[:, :], self.c_ident[:, :])
            k.copy("dve", idb[:, :], idf[:, :])
            wa = k.sb("wa", [128, 4, D], BF16)
            wb = k.sb("wb", [128, 4, D], BF16)
            wc = k.sb("wc", [128, 4, D], BF16)
            wo = k.sb("wo", [128, 8, D], BF16)
            self.load_w(wa, W["na_proj"][l], 4, D)
            self.load_w(wb, W["mla_proj"][l], 4, D)
            self.load_w(wc, W["rw_proj"][l], 4, D)
            self.load_w(wo, W["w_out"][l], 8, D)
            xa = k.rot("xa", [128, 512], BF16, 2)
            xc = k.rot("xc", [128, 512], BF16, 2)
            oTb = k.rot("oTb", [128, 4, 128], BF16, 2)
            gt = k.rot("gt", [128, 3072], F32, 2)
            xt = k.rot("xt", [128, D], F32, 2)
            aT = k.sb("aT", [128, 4, 128], BF16)
            cT = k.sb("cT", [128, 4, 128], BF16)
            mixed = k.sb("mixed", [128, D], F32)
            tmp = k.rot("tmp", [128, 512], F32, 2)
            mixb = k.sb("mixb", [128, D], BF16)
            mT = k.sb("mT", [128, 8, 128], BF16)
            xo = k.rot("xo", [128, D], F32, 2)
            pT = k.ps("pT", [128, 1024], BF16)
            pm = k.rot("pm", [128, 512], F32, 4, psum=True)
            npm = 0
            for ti in range(self.nt):
                tok0 = ti * 128
                i2 = ti % 2
                k.dma("sp", xa[i2][:, :], S["na_out"][tok0:tok0 + 128, :])
                k.dma("sp", xc[i2][:, :], S["yc"][tok0:tok0 + 128, :])
                k.dma("sp", oTb[i2][:, :, :], S["oT_b"][:, tok0:tok0 + 128].rearrange("(j p) t -> p j t", p=128))
                k.dma("sp", gt[i2][:, :], S["gates"][tok0:tok0 + 128, :])
                k.dma("sp", xt[i2][:, :], xsrc[tok0:tok0 + 128, :])
                for j in range(4):
                    k.tr(pT[:, j * 128:(j + 1) * 128], xa[i2][:, j * 128:(j + 1) * 128], idb[:, :])
                    k.tr(pT[:, 512 + j * 128:512 + (j + 1) * 128], xc[i2][:, j * 128:(j + 1) * 128], idb[:, :])
                k.copy("dve", aT[:, :, :], pT[:, 0:512].rearrange("p (j t) -> p j t", j=4))
                k.copy("dve", cT[:, :, :], pT[:, 512:1024].rearrange("p (j t) -> p j t", j=4))
                for bi, (T_, W_) in enumerate(((aT, wa), (oTb[i2], wb), (cT, wc))):
                    for nch in range(2):
                        p = pm[npm % 4]
                        npm += 1
                        for c in range(4):
                            k.mm(p[:, :], T_[:, c, :], W_[:, c, nch * 512:(nch + 1) * 512], start=(c == 0), stop=(c == 3))
                        gsl = gt[i2][:, bi * D + nch * 512:bi * D + (nch + 1) * 512]
                        if bi == 0:
                            k.tt("dve", mixed[:, nch * 512:(nch + 1) * 512], p[:, :], gsl, ALU.mult)
                        else:
                            t_ = tmp[npm % 2]
                            k.tt("dve", t_[:, :], p[:, :], gsl, ALU.mult)
                            k.tt("pool", mixed[:, nch * 512:(nch + 1) * 512], mixed[:, nch * 512:(nch + 1) * 512], t_[:, :], ALU.add)
                k.copy("act", mixb[:, :], mixed[:, :])
                for c in range(8):
                    k.tr(pT[:, c * 128:(c + 1) * 128], mixb[:, c * 128:(c + 1) * 128], idb[:, :])
                k.copy("dve", mT[:, :, :], pT[:, :].rearrange("p (j t) -> p j t", j=8))
                for nch in range(2):
                    p = pm[npm % 4]
                    npm += 1
                    for c in range(8):
                        k.mm(p[:, :], mT[:, c, :], wo[:, c, nch * 512:(nch + 1) * 512], start=(c == 0), stop=(c == 7))
                    k.tt("dve", xo[i2][:, nch * 512:(nch + 1) * 512], p[:, :], xt[i2][:, nch * 512:(nch + 1) * 512], ALU.add)
                k.dma("pool", S["xmid"][tok0:tok0 + 128, :], xo[i2][:, :])

    def pass_ffn(self, l, xdst):
        k, W, S = self.k, self.W, self.S
        NF = DFF // 128
        with self.phase():
            idf = k.sb("idf", [128, 128], F32)
            idb = k.sb("idb", [128, 128], BF16)
            g2 = k.sb("g2", [128, 8], F32)
            k.dma("sp", idf[:, :], self.c_ident[:, :])
            k.copy("dve", idb[:, :], idf[:, :])
            k.dma("sp", g2[:, :], W["norm2_g"][l, :].rearrange("(c p) -> p c", p=128), slow=True)
            wg = k.sb("wg", [128, 8, DFF], BF16)
            wu = k.sb("wu", [128, 8, DFF], BF16)
            wd = k.sb("wd", [128, NF, D], BF16)
            self.load_w(wg, W["ffn_w_gate"][l], 8, DFF, gcol=g2, chunk=256)
            self.load_w(wu, W["ffn_w_up"][l], 8, DFF, gcol=g2, chunk=256)
            self.load_w(wd, W["ffn_w_down"][l], NF, D, chunk=128)
            TS = 2
            xt = k.rot("xt", [128, TS, D], F32, 2)
            junk = k.sb("junk", [128, D], BF16)
            hb = k.sb("hb", [128, D], BF16)
            xT = k.sb("xT", [128, 8, TS * 128], BF16)
            st = k.rot("st", [128, 4], F32, 2)
            sg = k.rot("sg", [128, TS * 128], F32, 2)
            hT = k.sb("hT", [128, NF, TS * 128], BF16)
            xo = k.rot("xo", [128, D], F32, 2)
            pT = k.ps("pT", [128, 1024], BF16)
            pg = k.rot("pg", [128, 512], F32, 2, psum=True)
            pu = k.rot("pu", [128, 512], F32, 2, psum=True)
            pm = k.rot("pm", [128, 512], F32, 2, psum=True)
            npm = 0
            nx = 0
            for tb in range(self.nt // TS):
                x_ = xt[tb % 2]
                for s in range(TS):
                    tok0 = (tb * TS + s) * 128
                    stt = st[s % 2]
                    k.dma("sp", x_[:, s, :], S["xmid"][tok0:tok0 + 128, :])
                    k._memset("pool", stt[:, :], 0.0)
                    k.act(junk[:, :], x_[:, s, :], AF.Square, accum_out=stt[:, 0:1])
                    self.rstd(stt[:, 1:2], stt[:, 0:1], stt[:, 2:3], 1.0 / D, 1e-6)
                    k.act(hb[:, :], x_[:, s, :], AF.Copy, scale=stt[:, 1:2])
                    for c in range(8):
                        k.tr(pT[:, c * 128:(c + 1) * 128], hb[:, c * 128:(c + 1) * 128], idb[:, :])
                    k.copy("dve", xT[:, :, s * 128:(s + 1) * 128], pT[:, :].rearrange("p (c t) -> p c t", c=8))
                for f in range(NF):
                    pg_, pu_ = pg[f % 2], pu[f % 2]
                    for c in range(8):
                        k.mm(pg_[:, 0:TS * 128], wg[:, c, f * 128:(f + 1) * 128], xT[:, c, :], start=(c == 0), stop=(c == 7))
                    for c in range(8):
                        k.mm(pu_[:, 0:TS * 128], wu[:, c, f * 128:(f + 1) * 128], xT[:, c, :], start=(c == 0), stop=(c == 7))
                    sg_ = sg[f % 2]
                    k.act(sg_[:, :], pg_[:, 0:TS * 128], AF.Silu)
                    k.tt("dve", hT[:, f, :], sg_[:, :], pu_[:, 0:TS * 128], ALU.mult)
                for s in range(TS):
                    tok0 = (tb * TS + s) * 128
                    xo_ = xo[nx % 2]
                    nx += 1
                    for nch in range(2):
                        p = pm[npm % 2]
                        npm += 1
                        for f in range(NF):
                            k.mm(p[:, :], hT[:, f, s * 128:(s + 1) * 128], wd[:, f, nch * 512:(nch + 1) * 512],
                                 start=(f == 0), stop=(f == NF - 1))
                        k.tt("dve", xo_[:, nch * 512:(nch + 1) * 512], p[:, :], x_[:, s, nch * 512:(nch + 1) * 512], ALU.add)
                    k.dma("pool", xdst[tok0:tok0 + 128, :], xo_[:, :])


def make_consts(seqs, na_rpb):
    ntok = sum(seqs)
    c = {}
    c["c_ident"] = np.eye(128, dtype=np.float32)
    rope = np.zeros((ntok, 32), np.float32)
    o = 0
    inv = (10000.0 ** (-np.arange(8, dtype=np.float32) / 8)).astype(np.float32)
    for L in seqs:
        t = np.arange(L)
        row = (t // 64).astype(np.float32)
        colp = (t % 64).astype(np.float32)
        ang = np.concatenate([row[:, None] * inv, colp[:, None] * inv], axis=-1).astype(np.float32)
        rope[o:o + L, 0:16] = np.cos(ang)
        rope[o:o + L, 16:32] = np.sin(ang)
        o += L
    c["c_rope"] = rope
    p = np.arange(128)
    kc = np.arange(4)
    key = kc[None, :] * 128 + p[:, None]
    w = key // 64
    kcol = key % 64
    q = np.arange(64)
    win = np.clip(q - 8, 0, 48)
    valid = (kcol[:, :, None] >= win[None, None, :]) & (kcol[:, :, None] < win[None, None, :] + 16)
    dc = np.clip(kcol[:, :, None] - q[None, None, :] + 15, 0, 30)
    nab = np.full((DEPTH, 4, 128, 8, 2, 4, 64), -30000.0, np.float32)
    for d in range(8):
        dri = np.clip(w - d + 7, 0, 14)
        drb = np.broadcast_to(dri[:, :, None], dc.shape)
        for hp in range(4):
            for h2 in range(2):
                g = na_rpb[:, 2 * hp + h2][:, drb, dc]
                nab[:, hp, :, d, h2] = np.where(valid[None], g, np.float32(-30000.0))
    c["c_nab"] = nab.reshape(DEPTH, 4, 128, 8 * 512)
    j = np.arange(128)[:, None]
    t = np.arange(128)[None, :]
    same = (j // 64) == (t // 64)
    rwm = np.zeros((2, 4, 128, 128), np.float32)
    rwm[0, 0] = same & (j < t)
    rwm[0, 1] = rwm[0, 0].T
    rwm[0, 2] = same & (j <= t)
    rwm[0, 3] = same
    rwm[1, 0] = same & (j > t)
    rwm[1, 1] = rwm[1, 0].T
    rwm[1, 2] = same & (j >= t)
    rwm[1, 3] = same
    c["c_rwm"] = rwm
    eb = np.zeros((128, 512), np.float32)
    for tt in range(128):
        eb[tt, np.arange(8) * 64 + (tt % 64)] = 1.0
    c["c_eblk"] = eb
    return c


WNAMES = ["norm1_g", "w_in", "b_gate", "na_q_norm", "na_k_norm", "na_proj", "mla_cq_norm", "mla_ckv_norm", "mla_w_uq",
          "mla_w_ukv", "mla_q_norm", "mla_k_norm", "mla_proj", "rw_mu", "rw_w0", "rw_w_up", "rw_a0", "rw_a_up", "rw_g_up",
          "rw_k_k", "rw_k_a", "rw_r_k", "rw_ln_w", "rw_ln_b", "rw_proj", "w_out", "norm2_g", "ffn_w_gate", "ffn_w_up",
          "ffn_w_down"]

_PROG = {}


def get_prog(seqs, dbg=()):
    key = (tuple(seqs), tuple(sorted(dbg)))
    if key not in _PROG:
        _PROG[key] = Prog(seqs, dbg)
    return _PROG[key]


def kernel(**inputs):
    xp = np.asarray(inputs["x_prompt"], np.float32)
    xs = np.asarray(inputs["x_sample"], np.float32)
    n = 8
    prog = get_prog(FULL_SEQS)
    consts = make_consts(FULL_SEQS, np.asarray(inputs["na_rpb"], np.float32))
    shared = {nm: np.ascontiguousarray(np.asarray(inputs[nm], np.float32)) for nm in WNAMES}
    shared.update(consts)
    in_maps = []
    for c in range(n):
        m = dict(shared)
        m["x"] = np.ascontiguousarray(np.concatenate([xp[c], xs[2 * c], xs[2 * c + 1]], axis=0))
        in_maps.append(m)
    res = run_bass_kernel_spmd(prog.nc, in_maps, core_ids=list(range(n)))
    yp = np.empty_like(xp)
    ys = np.empty_like(xs)
    for c in range(n):
        y = np.asarray(res.results[c]["y"], np.float32)
        yp[c] = y[0:8192]
        ys[2 * c] = y[8192:12288]
        ys[2 * c + 1] = y[12288:16384]
    return (yp, ys)
```

```python
import numpy as np
import ml_dtypes
from contextlib import ExitStack
import concourse.bass as bass
import concourse.mybir as mybir
from concourse.bass_utils import run_bass_kernel_spmd

F32 = mybir.dt.float32
BF16 = mybir.dt.bfloat16
AF = mybir.ActivationFunctionType
ALU = mybir.AluOpType
AX = mybir.AxisListType

DEPTH = 2
D = 1024
DIN = 6944
DFF = 2816
NH = 8
C_Q, C_K, C_V, C_CQ, C_CKV, C_KR, C_RW, C_G = 0, 512, 1024, 1536, 1792, 1920, 1952, 3872
RWC = 5128
FULL_SEQS = (8192, 4096, 4096)


class V:
    __slots__ = ("ap", "buf")

    def __init__(self, ap, buf):
        self.ap = ap
        self.buf = buf

    def __getitem__(self, key):
        return V(self.ap[key], self.buf)

    def rearrange(self, pat, **kw):
        return V(self.ap.rearrange(pat, **kw), self.buf)

    def unsqueeze(self, a):
        return V(self.ap.unsqueeze(a), self.buf)

    def bc(self, shape):
        return V(self.ap.broadcast_to(list(shape)), self.buf)

    def bitcast(self, dt):
        return V(self.ap.bitcast(dt), self.buf)


class Buf:
    __slots__ = ("t", "w", "r", "name")

    def __init__(self, t, name):
        self.t = t
        self.w = None
        self.r = {}
        self.name = name

    def __getitem__(self, key):
        return V(self.t[key], self)


class K:
    NDMA = 16

    def __init__(self, nc, es):
        self.nc = nc
        self.es = es
        self.eng = {"pe": nc.tensor, "act": nc.scalar, "dve": nc.vector, "pool": nc.gpsimd, "sp": nc.sync}
        self.sem = {}
        self.cnt = {}
        self.seen = {n: {} for n in self.eng}
        for n in self.eng:
            self.sem[n] = es.enter_context(nc.semaphore("s_" + n))
            self.cnt[n] = 0
        self.dq = {"sp": 0, "pool": 0, "act": 0}
        for q in self.dq:
            for i in range(self.NDMA):
                n = "%s_d%d" % (q, i)
                self.sem[n] = es.enter_context(nc.semaphore("s_" + n))
                self.cnt[n] = 0
        self.uid = 0
        import os as _os
        self.limit = int(_os.environ.get("KLIMIT", "0")) or None
        self.nops = 0
        self.log = [] if _os.environ.get("KLOG") else None

    def _lim(self):
        self.nops += 1
        if self.log is not None:
            import traceback
            fr = traceback.extract_stack(limit=4)[0:2]
            self.log.append((self.nops, [(f.lineno) for f in fr]))
        return self.limit is not None and self.nops > self.limit

    def sb(self, name, shape, dt):
        self.uid += 1
        return Buf(self.es.enter_context(self.nc.sbuf_tensor("%s_%d" % (name, self.uid), list(shape), dt)), name)

    def ps(self, name, shape, dt):
        self.uid += 1
        return Buf(self.es.enter_context(self.nc.psum_tensor("%s_%d" % (name, self.uid), list(shape), dt)), name)

    def rot(self, name, shape, dt, n, psum=False):
        return [(self.ps if psum else self.sb)("%s%d" % (name, i), shape, dt) for i in range(n)]

    def _wait(self, e, deps):
        seen = self.seen[e]
        for key, v in deps.items():
            if key == e and e == "pe":
                continue
            if seen.get(key, 0) >= v:
                continue
            self.eng[e].wait_ge(self.sem[key], v)
            seen[key] = v

    @staticmethod
    def _add(deps, ev):
        if ev is None:
            return
        key, v = ev
        if deps.get(key, 0) < v:
            deps[key] = v

    def _deps(self, reads, writes):
        deps = {}
        for b in reads:
            self._add(deps, b.w)
        for b in writes:
            self._add(deps, b.w)
            for key, v in b.r.items():
                self._add(deps, (key, v))
        return deps

    def _mark(self, ev, reads, writes):
        key, v = ev
        for b in reads:
            if b.r.get(key, 0) < v:
                b.r[key] = v
        for b in writes:
            b.w = ev
            b.r = {}

    def op(self, e, meth, **kw):
        if self._lim():
            return None
        reads, writes = [], []
        args = {}
        for name, val in kw.items():
            if isinstance(val, V):
                (writes if name in ("out", "accum_out") else reads).append(val.buf)
                args[name] = val.ap
            else:
                args[name] = val
        self._wait(e, self._deps(reads, writes))
        ins = getattr(self.eng[e], meth)(**args)
        self.cnt[e] += 1
        ins.then_inc(self.sem[e], 1)
        self._mark((e, self.cnt[e]), reads, writes)
        return ins

    def dma(self, q, out, in_, slow=False):
        if self._lim():
            return None
        reads, writes = [], []
        if isinstance(out, V):
            writes.append(out.buf)
            out = out.ap
        if isinstance(in_, V):
            reads.append(in_.buf)
            in_ = in_.ap
        i = self.dq[q] % self.NDMA
        self.dq[q] += 1
        key = "%s_d%d" % (q, i)
        deps = self._deps(reads, writes)
        if self.cnt[key] > 0:
            self._add(deps, (key, self.cnt[key]))
        self._wait(q, deps)
        if slow:
            ins = self.eng[q].dma_start(out=out, in_=in_, allow_slow_non_contiguous=True)
        else:
            ins = self.eng[q].dma_start(out=out, in_=in_)
        self.cnt[key] += 16
        ins.then_inc(self.sem[key], 16)
        self._mark((key, self.cnt[key]), reads, writes)
        return ins

    def barrier(self):
        allev = {key: v for key, v in self.cnt.items() if v > 0}
        for e in self.eng:
            self._wait(e, dict(allev))

    def act(self, out, in_, func, **kw):
        return self.op("act", "activation", out=out, in_=in_, func=func, **kw)

    def tt(self, e, out, in0, in1, op):
        return self.op(e, "tensor_tensor", out=out, in0=in0, in1=in1, op=op)

    def stt(self, e, out, in0, scalar, in1, op0, op1):
        return self.op(e, "scalar_tensor_tensor", out=out, in0=in0, scalar=scalar, in1=in1, op0=op0, op1=op1)

    def ts(self, e, out, in0, s1, s2, op0, op1=None):
        if op1 is None:
            return self.op(e, "tensor_scalar", out=out, in0=in0, scalar1=s1, scalar2=None, op0=op0)
        return self.op(e, "tensor_scalar", out=out, in0=in0, scalar1=s1, scalar2=s2, op0=op0, op1=op1)

    def copy(self, e, out, in_):
        if e == "act":
            return self.act(out, in_, AF.Copy)
        return self.op(e, "tensor_copy", out=out, in_=in_)

    def memset(self, e, out, val):
        return self.op(e, "memset", ap=out.ap, **{}) if False else self._memset(e, out, val)

    def _memset(self, e, out, val):
        if self._lim():
            return None
        self._wait(e, self._deps([], [out.buf]))
        ins = self.eng[e].memset(out.ap, val)
        self.cnt[e] += 1
        ins.then_inc(self.sem[e], 1)
        self._mark((e, self.cnt[e]), [], [out.buf])
        return ins

    def mm(self, out, lhsT, rhs, start=True, stop=True):
        return self.op("pe", "matmul", out=out, lhsT=lhsT, rhs=rhs, start=start, stop=stop)

    def tr(self, out, in_, ident):
        return self.op("pe", "transpose", out=out, in_=in_, identity=ident)

    def reduce(self, e, out, in_, op=ALU.add):
        return self.op(e, "tensor_reduce", out=out, in_=in_, axis=AX.X, op=op)


class Prog:
    def __init__(self, seqs, dbg=()):
        self.seqs = tuple(seqs)
        self.ntok = sum(seqs)
        self.nt = self.ntok // 128
        self.dbg = set(dbg)
        self.seq_off = [sum(seqs[:i]) for i in range(len(seqs))]
        self.nc = bass.Bass("TRN2", target_bir_lowering=False)
        self.build()

    def din(self, name, shape, dt=F32):
        return self.nc.dram_tensor(name, list(shape), dt, kind="ExternalInput").ap()

    def dscr(self, name, shape, dt=F32):
        kind = "ExternalOutput" if name in self.dbg else "Internal"
        return self.nc.dram_tensor(name, list(shape), dt, kind=kind).ap()

    def seq_of_tile(self, ti):
        tok = ti * 128
        for s, (o, L) in enumerate(zip(self.seq_off, self.seqs)):
            if o <= tok < o + L:
                return s, o, L
        raise AssertionError

    def build(self):
        nc = self.nc
        NT = self.ntok
        W = {}
        self.W = W
        self.x = self.din("x", [NT, D])
        shapes = dict(
            norm1_g=[DEPTH, D], w_in=[DEPTH, D, DIN], b_gate=[DEPTH, 3 * D], na_q_norm=[DEPTH, 64], na_k_norm=[DEPTH, 64],
            na_proj=[DEPTH, 512, D], mla_cq_norm=[DEPTH, 256], mla_ckv_norm=[DEPTH, 128], mla_w_uq=[DEPTH, 256, 768],
            mla_w_ukv=[DEPTH, 128, 1024], mla_q_norm=[DEPTH, 96], mla_k_norm=[DEPTH, 96], mla_proj=[DEPTH, 512, D],
            rw_mu=[DEPTH, 1920], rw_w0=[DEPTH, 2, 512], rw_w_up=[DEPTH, 2, 64, 512], rw_a0=[DEPTH, 2, 512],
            rw_a_up=[DEPTH, 2, 64, 512], rw_g_up=[DEPTH, 128, 512], rw_k_k=[DEPTH, 512], rw_k_a=[DEPTH, 512],
            rw_r_k=[DEPTH, 8, 64], rw_ln_w=[DEPTH, 512], rw_ln_b=[DEPTH, 512], rw_proj=[DEPTH, 512, D],
            w_out=[DEPTH, D, D], norm2_g=[DEPTH, D], ffn_w_gate=[DEPTH, D, DFF], ffn_w_up=[DEPTH, D, DFF],
            ffn_w_down=[DEPTH, DFF, D])
        for n, s in shapes.items():
            W[n] = self.din(n, s)
        self.c_ident = self.din("c_ident", [128, 128])
        self.c_rope = self.din("c_rope", [NT, 32])
        self.c_nab = self.din("c_nab", [DEPTH, 4, 128, 8 * 512])
        self.c_rwm = self.din("c_rwm", [2, 4, 128, 128])
        self.c_eblk = self.din("c_eblk", [128, 512])
        self.y = nc.dram_tensor("y", [NT, D], F32, kind="ExternalOutput").ap()
        S = {}
        self.S = S
        S["xmid"] = self.dscr("xmid", [NT, D])
        S["xres"] = self.dscr("xres", [NT, D])
        S["qT_na"] = self.dscr("qT_na", [512, NT], BF16)
        S["kT_na"] = self.dscr("kT_na", [512, NT], BF16)
        S["v_na"] = self.dscr("v_na", [NT, 512], BF16)
        S["qT_m"] = self.dscr("qT_m", [8, 96, NT], BF16)
        S["kT_m"] = self.dscr("kT_m", [8, 96, NT], BF16)
        S["v_m"] = self.dscr("v_m", [NT, 512], BF16)
        S["rw_raw"] = self.dscr("rw_raw", [NT, 1920])
        S["gates"] = self.dscr("gates", [NT, 3072], BF16)
        S["na_out"] = self.dscr("na_out", [NT, 512], BF16)
        S["oT_b"] = self.dscr("oT_b", [512, NT], BF16)
        S["rwpA"] = self.dscr("rwpA", [NT, 2048])
        S["rwpB"] = self.dscr("rwpB", [NT, RWC - 2048])
        S["yd0"] = self.dscr("yd0", [NT, 512])
        S["yd1"] = self.dscr("yd1", [NT, 512])
        S["yc"] = self.dscr("yc", [NT, 512], BF16)

        with ExitStack() as es:
            k = K(nc, es)
            self.k = k
            stop_after = None
            for d in self.dbg:
                if d.startswith("stop:"):
                    stop_after = d[5:]
            done = False
            for l in range(DEPTH):
                xsrc = self.x if l == 0 else S["xres"]
                xdst = self.y if l == DEPTH - 1 else S["xres"]
                for name, fn in (("inproj", lambda: self.pass_inproj(l, xsrc)),
                                 ("na", lambda: self.pass_na(l)),
                                 ("mla", lambda: self.pass_mla(l)),
                                 ("rwprep", lambda: self.pass_rwprep(l)),
                                 ("rwscan", lambda: self.pass_rwscan(l)),
                                 ("rwfin", lambda: self.pass_rwfin(l)),
                                 ("merge", lambda: self.pass_merge(l, xsrc)),
                                 ("ffn", lambda: self.pass_ffn(l, xdst))):
                    if "skip:" + name in self.dbg:
                        continue
                    fn()
                    if stop_after == "%s%d" % (name, l):
                        done = True
                        break
                if done:
                    break
            k.barrier()

    def phase(self):
        prog = self

        class _P:
            def __enter__(s):
                s.st = ExitStack()
                s.old = prog.k.es
                prog.k.es = s.st
                s.st.__enter__()
                return s

            def __exit__(s, *a):
                prog.k.barrier()
                prog.k.es = s.old
                return s.st.__exit__(*a)
        return _P()

    def load_bc(self, dst, src_row):
        self.k.dma("sp", dst, src_row.partition_broadcast(128))

    def load_w(self, dst, src, kc, ncols, gcol=None, chunk=512, f32=False):
        k = self.k
        with ExitStack() as st:
            old = k.es
            k.es = st
            stg = k.rot("stg", [128, kc, chunk], F32, 2)
            i = 0
            for n0 in range(0, ncols, chunk):
                n1 = min(ncols, n0 + chunk)
                s = stg[i % 2]
                i += 1
                k.dma("sp", s[:, :, 0:n1 - n0], src[:, n0:n1].rearrange("(c p) n -> p c n", p=128))
                for c in range(kc):
                    if gcol is not None:
                        k.act(dst[:, c, n0:n1], s[:, c, 0:n1 - n0], AF.Copy, scale=gcol[:, c:c + 1])
                    else:
                        k.copy("dve" if c % 2 else "act", dst[:, c, n0:n1], s[:, c, 0:n1 - n0])
            k.barrier()
            k.es = old

    @staticmethod
    def zip_gens(gens):
        gens = list(gens)
        while gens:
            for g in list(gens):
                try:
                    next(g)
                except StopIteration:
                    gens.remove(g)
            yield

    @staticmethod
    def run_pipeline(gens, depth=2):
        active = []
        it = iter(gens)
        exhausted = False
        while True:
            while len(active) < depth and not exhausted:
                try:
                    active.append(next(it))
                except StopIteration:
                    exhausted = True
            if not active:
                break
            for g in list(active):
                try:
                    next(g)
                except StopIteration:
                    active.remove(g)

    def rstd(self, out, ss, tmp, scale, eps):
        k = self.k
        k.act(tmp, ss, AF.Ln, scale=scale, bias=eps)
        k.act(out, tmp, AF.Exp, scale=-0.5)

    def pass_inproj(self, l, xsrc):
        k, W, S = self.k, self.W, self.S
        with self.phase():
            Wb = k.sb("Wb", [128, 8, DIN], BF16)
            wuq = k.sb("wuq", [128, 2, 768], BF16)
            wukv = k.sb("wukv", [128, 1, 1024], BF16)
            g1 = k.sb("g1", [128, 8], F32)
            gcq = k.sb("gcq", [128, 2], F32)
            gckv = k.sb("gckv", [128, 1], F32)
            gq = k.sb("gq", [128, 64], F32)
            gk = k.sb("gk", [128, 64], F32)
            mq = k.sb("mq", [128, 96], F32)
            mk = k.sb("mk", [128, 96], F32)
            bg = k.sb("bg", [128, 3072], F32)
            idf = k.sb("idf", [128, 128], F32)
            idb = k.sb("idb", [128, 128], BF16)
            k.dma("sp", g1[:, :], W["norm1_g"][l, :].rearrange("(c p) -> p c", p=128), slow=True)
            k.dma("sp", gcq[:, :], W["mla_cq_norm"][l, :].rearrange("(c p) -> p c", p=128), slow=True)
            k.dma("sp", gckv[:, :], W["mla_ckv_norm"][l, :].rearrange("(c p) -> p c", p=128), slow=True)
            self.load_bc(gq[:, :], W["na_q_norm"][l, :])
            self.load_bc(gk[:, :], W["na_k_norm"][l, :])
            self.load_bc(mq[:, :], W["mla_q_norm"][l, :])
            self.load_bc(mk[:, :], W["mla_k_norm"][l, :])
            self.load_bc(bg[:, :], W["b_gate"][l, :])
            k.dma("sp", idf[:, :], self.c_ident[:, :])
            k.copy("dve", idb[:, :], idf[:, :])
            self.load_w(Wb, W["w_in"][l], 8, DIN, gcol=g1)
            self.load_w(wuq, W["mla_w_uq"][l], 2, 768, gcol=gcq)
            self.load_w(wukv, W["mla_w_ukv"][l], 1, 1024, gcol=gckv)

            XT = k.rot("xt", [128, D], F32, 2)
            junk = k.rot("junk", [128, D], BF16, 2)
            hb = k.rot("hb", [128, D], BF16, 2)
            xT = k.rot("xT", [128, D], BF16, 2)
            st = k.rot("st", [128, 16], F32, 2)
            st2 = k.rot("st2", [128, 16], F32, 4)
            sq = k.rot("sq", [128, 768], F32, 2)
            qf = k.rot("qf", [128, 512], F32, 2)
            qn = k.rot("qn", [128, 512], BF16, 2)
            qTs = k.rot("qTs", [128, 512], BF16, 2)
            vb = k.rot("vb", [128, 512], BF16, 2)
            ml = k.rot("ml", [128, 416], F32, 2)
            cn = k.rot("cn", [128, 384], BF16, 2)
            cT = k.rot("cT", [128, 384], BF16, 2)
            qm = k.rot("qm", [128, 8, 96], F32, 4)
            rp = k.rot("rp", [128, 4, 8, 16], F32, 2)
            qmb = k.rot("qmb", [128, 8, 96], BF16, 2)
            qmT = k.rot("qmT", [96, 1024], BF16, 2)
            vmb = k.rot("vmb", [128, 8, 64], BF16, 2)
            cs = k.rot("cs", [128, 32], F32, 2)
            ev = [k.rot("ev%d" % i, [128, 512], F32, 2) for i in range(2)]
            evb = [k.rot("evb%d" % i, [128, 512], BF16, 2) for i in range(2)]
            pT = k.rot("pT", [128, 1024], BF16, 2, psum=True)
            pmP = [k.rot("pm%d" % i, [128, 512], F32, 2, psum=True) for i in range(2)]
            cnt = {"pm0": 0, "pm1": 0, "ev0": 0, "ev1": 0}

            def proj(xTt, n0, n1, par):
                p = pmP[par][cnt["pm%d" % par] % 2]
                cnt["pm%d" % par] += 1
                for c in range(8):
                    k.mm(p[:, 0:n1 - n0], xTt[:, c * 128:(c + 1) * 128], Wb[:, c, n0:n1], start=(c == 0), stop=(c == 7))
                return p

            def headnorm(src3, dst3, nh, hd, stt, col, eps, scale, sq_):
                k.act(sq_[:, 0:nh * hd].rearrange("p (h d) -> p h d", h=nh), src3, AF.Square)
                k.reduce("dve", stt[:, col:col + nh], sq_[:, 0:nh * hd].rearrange("p (h d) -> p h d", h=nh))
                yield
                self.rstd(stt[:, col:col + nh], stt[:, col:col + nh], stt[:, col + 8:col + 8 + nh], scale, eps)
                yield
                k.tt("dve", dst3, src3, stt[:, col:col + nh].unsqueeze(2).bc([128, nh, hd]), ALU.mult)

            def na_qk(p, gain, dst, stt, par, tok0, pT_):
                qf_ = qf[par]
                yield from headnorm(p[:, :].rearrange("p (h d) -> p h d", h=8), qf_[:, :].rearrange("p (h d) -> p h d", h=8),
                                    8, 64, stt, 0, 1e-6, 1.0 / 64, sq[par])
                yield
                q_ = qn[par]
                qs = qTs[par]
                k.tt("pool", q_[:, :].rearrange("p (h d) -> p h d", h=8), qf_[:, :].rearrange("p (h d) -> p h d", h=8),
                     gain[:, :].unsqueeze(1).bc([128, 8, 64]), ALU.mult)
                yield
                for j in range(4):
                    k.tr(pT_[:, j * 128:(j + 1) * 128], q_[:, j * 128:(j + 1) * 128], idb[:, :])
                k.copy("dve", qs[:, :], pT_[:, 0:512])
                k.dma("pool", dst[:, tok0:tok0 + 128].rearrange("(j p) t -> p j t", p=128), qs[:, :].rearrange("p (j t) -> p j t", j=4))

            def normrope(src, gain, dst, stt, col, csb, tok0, par, pT_):
                yield from headnorm(src[:, :, :], src[:, :, :], 8, 96, stt, col, 1e-6, 1.0 / 96, sq[par])
                yield
                k.tt("pool", src[:, :, :], src[:, :, :], gain[:, :].unsqueeze(1).bc([128, 8, 96]), ALU.mult)
                yield
                x1 = src[:, :, 64:80]
                x2 = src[:, :, 80:96]
                cc = csb[:, 0:16].unsqueeze(1).bc([128, 8, 16])
                ss_ = csb[:, 16:32].unsqueeze(1).bc([128, 8, 16])
                rp_ = rp[par]
                k.tt("dve", rp_[:, 0], x1, cc, ALU.mult)
                k.tt("pool", rp_[:, 1], x2, ss_, ALU.mult)
                k.tt("dve", rp_[:, 2], x1, ss_, ALU.mult)
                k.tt("pool", rp_[:, 3], x2, cc, ALU.mult)
                qb = qmb[par]
                k.copy("act", qb[:, :, 0:64], src[:, :, 0:64])
                yield
                k.tt("dve", qb[:, :, 64:80], rp_[:, 0], rp_[:, 1], ALU.subtract)
                k.tt("dve", qb[:, :, 80:96], rp_[:, 2], rp_[:, 3], ALU.add)
                yield
                for h in range(8):
                    k.tr(pT_[0:96, h * 128:(h + 1) * 128], qb[:, h, :], idb[:, :])
                qt = qmT[par]
                k.copy("dve", qt[:, :], pT_[0:96, :])
                k.dma("pool", dst[:, :, tok0:tok0 + 128].rearrange("h d t -> d h t"), qt[:, :].rearrange("d (h t) -> d h t", h=8))

            def tile(ti):
                tok0 = ti * 128
                par = ti % 2
                xt = XT[par]
                stt = st[par]
                csb = cs[par]
                pT_ = pT[par]
                junk_, hb_, ml_, cn_, cT_ = junk[par], hb[par], ml[par], cn[par], cT[par]
                k.dma("sp", xt[:, :], xsrc[tok0:tok0 + 128, :])
                k.dma("sp", csb[:, :], self.c_rope[tok0:tok0 + 128, :])
                k._memset("pool", stt[:, :], 0.0)
                k.act(junk_[:, :], xt[:, :], AF.Square, accum_out=stt[:, 0:1])
                yield
                self.rstd(stt[:, 1:2], stt[:, 0:1], stt[:, 2:3], 1.0 / D, 1e-6)
                yield
                k.act(hb_[:, :], xt[:, :], AF.Copy, scale=stt[:, 1:2])
                yield
                for c in range(8):
                    k.tr(pT_[:, c * 128:(c + 1) * 128], hb_[:, c * 128:(c + 1) * 128], idb[:, :])
                xTt = xT[par]
                k.copy("dve", xTt[:, :], pT_[:, :])
                yield
                p = proj(xTt, C_Q, C_Q + 512, par)
                yield
                yield from na_qk(p, gq, S["qT_na"], st2[2 * par], par, tok0, pT_)
                p = proj(xTt, C_K, C_K + 512, par)
                yield
                yield from na_qk(p, gk, S["kT_na"], st2[2 * par + 1], par, tok0, pT_)
                p = proj(xTt, C_V, C_V + 512, par)
                v_ = vb[par]
                yield
                k.copy("act", v_[:, :], p[:, :])
                k.dma("pool", S["v_na"][tok0:tok0 + 128, :], v_[:, :])
                p = proj(xTt, C_CQ, C_CQ + 416, par)
                yield
                k.copy("act", ml_[:, :], p[:, 0:416])
                k._memset("pool", stt[:, 4:6], 0.0)
                yield
                k.act(junk_[:, 0:256], ml_[:, 0:256], AF.Square, accum_out=stt[:, 4:5])
                k.act(junk_[:, 256:384], ml_[:, 256:384], AF.Square, accum_out=stt[:, 5:6])
                yield
                k.act(stt[:, 6:7], stt[:, 4:5], AF.Ln, scale=1.0 / 256, bias=1e-6)
                k.act(stt[:, 7:8], stt[:, 5:6], AF.Ln, scale=1.0 / 128, bias=1e-6)
                yield
                k.act(stt[:, 4:6], stt[:, 6:8], AF.Exp, scale=-0.5)
                yield
                k.act(cn_[:, 0:256], ml_[:, 0:256], AF.Copy, scale=stt[:, 4:5])
                k.act(cn_[:, 256:384], ml_[:, 256:384], AF.Copy, scale=stt[:, 5:6])
                yield
                for j in range(3):
                    k.tr(pT_[:, j * 128:(j + 1) * 128], cn_[:, j * 128:(j + 1) * 128], idb[:, :])
                k.copy("dve", cT_[:, :], pT_[:, 0:384])
                yield
                qm_ = qm[2 * par]
                km_ = qm[2 * par + 1]
                pq = pmP[par]
                for half in range(2):
                    for c in range(2):
                        k.mm(pq[half][:, 0:384], cT_[:, c * 128:(c + 1) * 128], wuq[:, c, half * 384:(half + 1) * 384],
                             start=(c == 0), stop=(c == 1))
                    k.copy("act", qm_[:, half * 4:(half + 1) * 4, :], pq[half][:, 0:384].rearrange("p (h d) -> p h d", h=4))
                yield
                pk = pmP[par]
                for half in range(2):
                    k.mm(pk[half][:, :], cT_[:, 256:384], wukv[:, 0, half * 512:(half + 1) * 512])
                    k.copy("dve", km_[:, half * 4:(half + 1) * 4, 0:64],
                           pk[half][:, :].rearrange("p (h d) -> p h d", h=4)[:, :, 0:64])
                    k.copy("dve", vmb[par][:, half * 4:(half + 1) * 4, :],
                           pk[half][:, :].rearrange("p (h d) -> p h d", h=4)[:, :, 64:128])
                k.copy("pool", km_[:, :, 64:96], ml_[:, 384:416].unsqueeze(1).bc([128, 8, 32]))
                k.dma("pool", S["v_m"][tok0:tok0 + 128, :], vmb[par][:, :, :].rearrange("p h d -> p (h d)"))
                yield
                chunks = [(n0, min(C_G, n0 + 512), "rw") for n0 in range(C_RW, C_G, 512)] + \
                         [(n0, n0 + 512, "g") for n0 in range(C_G, DIN, 512)]

                def dense():
                    for (n0, n1, kind) in chunks:
                        p = proj(xTt, n0, n1, par)
                        yield
                        e_ = ev[par][cnt["ev%d" % par] % 2]
                        eb_ = evb[par][cnt["ev%d" % par] % 2]
                        cnt["ev%d" % par] += 1
                        if kind == "rw":
                            k.copy("dve", e_[:, 0:n1 - n0], p[:, 0:n1 - n0])
                            k.dma("pool", S["rw_raw"][tok0:tok0 + 128, n0 - C_RW:n1 - C_RW], e_[:, 0:n1 - n0])
                        else:
                            k.tt("dve", e_[:, :], p[:, :], bg[:, n0 - C_G:n0 - C_G + 512], ALU.add)
                            yield
                            k.act(eb_[:, :], e_[:, :], AF.Sigmoid)
                            k.dma("pool", S["gates"][tok0:tok0 + 128, n0 - C_G:n0 - C_G + 512], eb_[:, :])
                        yield

                def mla_chain():
                    yield from normrope(qm_, mq, S["qT_m"], st2[2 * par], 0, csb, tok0, par, pT_)
                    yield
                    yield from normrope(km_, mk, S["kT_m"], st2[2 * par + 1], 0, csb, tok0, par, pT_)
                yield from self.zip_gens([dense(), mla_chain()])

            self.run_pipeline([tile(ti) for ti in range(self.nt)], depth=2)

    def pass_na(self, l):
        k, S = self.k, self.S
        with self.phase():
            Lmax = max(self.seqs)
            Rmax = Lmax // 64
            KT = k.rot("KT", [64, 2, Lmax], BF16, 1)
            QT = k.rot("QT", [64, 2, Lmax], BF16, 1)
            VE = k.rot("VE", [128, Rmax // 2, 2, 66], BF16, 2)
            VO = k.rot("VO", [128, Rmax // 2, 2, 66], BF16, 2)
            BT = k.rot("BT", [128, 8, 512], F32, 2)
            sbs = k.rot("sbs", [128, 512], F32, 3)
            pts = k.rot("pts", [128, 512], BF16, 3)
            rc = k.rot("rc", [64, 2, 1], F32, 2)
            ob = k.rot("ob", [64, 4, 2, 64], BF16, 2)
            pS = k.rot("pS", [128, 512], F32, 2, psum=True)
            pO = k.rot("pO", [128, 512], F32, 2, psum=True)
            it = 0
            u = 0
            for s, (t0, L) in enumerate(zip(self.seq_off, self.seqs)):
                R = L // 64
                for hp in range(4):
                    kt, qt, ve, vo, bt = KT[0], QT[0], VE[it % 2], VO[it % 2], BT[it % 2]
                    it += 1
                    for h2 in range(2):
                        hh = 2 * hp + h2
                        k.dma("sp", kt[:, h2, 0:L], S["kT_na"][hh * 64:(hh + 1) * 64, t0:t0 + L])
                        k.dma("sp", qt[:, h2, 0:L], S["qT_na"][hh * 64:(hh + 1) * 64, t0:t0 + L])
                    k.dma("sp", bt[:, :, :], self.c_nab[l, hp, :, :].rearrange("p (c n) -> p c n", c=8))
                    k._memset("pool", ve[:, :, :, :], 1.0)
                    k._memset("pool", vo[:, :, :, :], 1.0)
                    for h2 in range(2):
                        c0 = (2 * hp + h2) * 64
                        k.dma("sp", ve[:, 0:R // 2, h2, 0:64],
                              S["v_na"][t0:t0 + L, c0:c0 + 64].rearrange("(c p) d -> p c d", p=128))
                        k.dma("sp", vo[:, 0:R // 2 - 1, h2, 0:64],
                              S["v_na"][t0 + 64:t0 + L - 64, c0:c0 + 64].rearrange("(c p) d -> p c d", p=128))
                    for r in range(R):
                        rs = min(max(r - 4, 0), R - 8)
                        dcase = r - rs
                        ps_ = pS[u % 2]
                        po = pO[u % 2]
                        for h2 in range(2):
                            for kc in range(4):
                                col = (h2 * 4 + kc) * 64
                                k.mm(ps_[:, col:col + 64], kt[:, h2, rs * 64 + kc * 128:rs * 64 + (kc + 1) * 128],
                                     qt[:, h2, r * 64:(r + 1) * 64])
                        sb_ = sbs[u % 3]
                        k.stt("dve", sb_[:, :], ps_[:, :], 0.125, bt[:, dcase, :], ALU.mult, ALU.add)
                        pt = pts[u % 3]
                        k.act(pt[:, :], sb_[:, :], AF.Exp)
                        vbuf, cb = (ve, rs // 2) if rs % 2 == 0 else (vo, (rs - 1) // 2)
                        for h2 in range(2):
                            for kc in range(4):
                                col = (h2 * 4 + kc) * 64
                                k.mm(po[0:64, h2 * 128:h2 * 128 + 66], pt[:, col:col + 64], vbuf[:, cb + kc, h2, :],
                                     start=(kc == 0), stop=(kc == 3))
                        po3 = po[0:64, 0:256].rearrange("q (h d) -> q h d", h=2)
                        rc_ = rc[u % 2]
                        k.op("dve", "reciprocal", out=rc_[:, :, :], in_=po3[:, :, 64:65])
                        ob_ = ob[(r // 4) % 2]
                        k.tt("dve", ob_[:, r % 4, :, :], po3[:, :, 0:64], rc_[:, :, :].bc([64, 2, 64]), ALU.mult)
                        u += 1
                        if r % 4 == 3:
                            r0 = r - 3
                            k.dma("pool", S["na_out"][t0 + r0 * 64:t0 + r0 * 64 + 256, hp * 128:(hp + 1) * 128].rearrange("(r q) c -> q r c", q=64),
                                  ob_[:, :, :, :].rearrange("q r h d -> q r (h d)"))

    def pass_mla(self, l):
        k, S = self.k, self.S
        with self.phase():
            Lmax = max(self.seqs)
            QT = k.rot("QT", [96, Lmax], BF16, 2)
            KT = k.rot("KT", [96, Lmax], BF16, 2)
            VA = k.rot("VA", [128, Lmax // 128, 128], BF16, 2)
            pts = k.rot("pts", [128, 512], BF16, 4)
            rcs = k.rot("rcs", [128, 512], F32, 2)
            ots = k.rot("ots", [64, 512], BF16, 2)
            pS = k.rot("pS", [128, 512], F32, 3, psum=True)
            pO = k.rot("pO", [128, 512], F32, 2, psum=True)
            scale = 96 ** -0.5
            it = 0
            u = 0
            g = 0
            for s, (t0, L) in enumerate(zip(self.seq_off, self.seqs)):
                nkt = L // 128
                for h in range(8):
                    qt, kt, va = QT[it % 2], KT[it % 2], VA[it % 2]
                    it += 1
                    k.dma("sp", qt[:, 0:L], S["qT_m"][h, :, t0:t0 + L])
                    k.dma("sp", kt[:, 0:L], S["kT_m"][h, :, t0:t0 + L])
                    k._memset("pool", va[:, :, 64:128], 1.0)
                    k.dma("sp", va[:, 0:nkt, 0:64], S["v_m"][t0:t0 + L, h * 64:(h + 1) * 64].rearrange("(c p) d -> p c d", p=128))
                    for qc in range(L // 512):
                        po = pO[g % 2]
                        q_ = qt[:, qc * 512:(qc + 1) * 512]
                        LOOK = 2
                        pend = []
                        for j in range(nkt + LOOK):
                            if j < nkt:
                                ps_ = pS[u % 3]
                                u += 1
                                k.mm(ps_[:, :], kt[:, j * 128:(j + 1) * 128], q_)
                                pend.append(ps_)
                            if j >= LOOK:
                                jj = j - LOOK
                                ps_ = pend[jj]
                                pt = pts[jj % 4]
                                k.act(pt[:, :], ps_[:, :], AF.Exp, scale=scale)
                                k.mm(po[:, :], va[:, jj, :], pt[:, :], start=(jj == 0), stop=(jj == nkt - 1))
                        rc_ = rcs[g % 2]
                        k.op("dve", "reciprocal", out=rc_[64:128, :], in_=po[64:128, :])
                        ot = ots[g % 2]
                        k.tt("dve", ot[:, :], po[0:64, :], rc_[64:128, :], ALU.mult)
                        k.dma("pool", S["oT_b"][h * 64:(h + 1) * 64, t0 + qc * 512:t0 + (qc + 1) * 512], ot[:, :])
                        g += 1

    def pass_rwprep(self, l):
        k, W, S = self.k, self.W, self.S
        with self.phase():
            mu = k.sb("mu", [128, 1920], F32)
            kkb = k.sb("kkb", [128, 512], F32)
            kab = k.sb("kab", [128, 512], F32)
            rkb = k.sb("rkb", [128, 512], F32)
            w0b = k.sb("w0b", [128, 2, 512], F32)
            a0b = k.sb("a0b", [128, 2, 512], F32)
            wup = k.sb("wup", [64, 2, 512], F32)
            aup = k.sb("aup", [64, 2, 512], F32)
            gup = k.sb("gup", [128, 512], F32)
            idf = k.sb("idf", [128, 128], F32)
            self.load_bc(mu[:, :], W["rw_mu"][l, :])
            self.load_bc(kkb[:, :], W["rw_k_k"][l, :])
            self.load_bc(kab[:, :], W["rw_k_a"][l, :])
            self.load_bc(rkb[:, :], W["rw_r_k"][l, :, :].rearrange("h d -> (h d)"))
            for d_ in range(2):
                self.load_bc(w0b[:, d_, :], W["rw_w0"][l, d_, :])
                self.load_bc(a0b[:, d_, :], W["rw_a0"][l, d_, :])
            k.dma("sp", wup[:, :, :], W["rw_w_up"][l, :, :, :].rearrange("d r c -> r d c"))
            k.dma("sp", aup[:, :, :], W["rw_a_up"][l, :, :, :].rearrange("d r c -> r d c"))
            k.dma("sp", gup[:, :], W["rw_g_up"][l, :, :])
            k.dma("sp", idf[:, :], self.c_ident[:, :])
            cur = k.rot("cur", [128, 1920], F32, 2)
            prv = k.rot("prv", [128, 1920], F32, 2)
            nxt = k.rot("nxt", [128, 1920], F32, 2)
            pp_ = k.rot("pp", [128, 1920], F32, 2)
            dd_ = k.rot("dd", [128, 1920], F32, 2)
            O = k.rot("O", [128, RWC], F32, 2)
            t1_ = k.rot("t1", [128, 512], F32, 2)
            t2_ = k.rot("t2", [128, 512], F32, 2)
            t3_ = k.rot("t3", [128, 512], F32, 2)
            a__ = k.rot("a_", [128, 512], F32, 2)
            st = k.rot("st", [128, 24], F32, 2)
            sm_ = k.rot("sm", [128, 384], F32, 2)
            tT_ = k.rot("tT", [64, 4, 128], F32, 2)
            gT_ = k.rot("gT", [128, 128], F32, 2)
            pT_ = k.rot("pT", [128, 512], F32, 2, psum=True)
            pT2_ = k.rot("pT2", [128, 512], F32, 2, psum=True)
            pw_ = [k.rot("pw%d" % i, [128, 512], F32, 2, psum=True) for i in range(2)]
            seq_starts = set(self.seq_off)
            seq_ends = set(o + L for o, L in zip(self.seq_off, self.seqs))

            def col(o, i):
                return o[:, i * 512:(i + 1) * 512]

            def tile(ti):
                tok0 = ti * 128
                par = ti % 2
                c_, p_, n_, o = cur[par], prv[par], nxt[par], O[par]
                pp, dd, t1, t2, t3, a_ = pp_[par], dd_[par], t1_[par], t2_[par], t3_[par], a__[par]
                sm, tT, gT, pT, pT2, pw = sm_[par], tT_[par], gT_[par], pT_[par], pT2_[par], pw_[par]
                stt = st[par]
                npw = 0
                k.dma("sp", c_[:, :], S["rw_raw"][tok0:tok0 + 128, :])
                if tok0 in seq_starts:
                    k._memset("pool", p_[0:32, :], 0.0)
                    k.dma("sp", p_[1:128, :], S["rw_raw"][tok0:tok0 + 127, :])
                else:
                    k.dma("sp", p_[:, :], S["rw_raw"][tok0 - 1:tok0 + 127, :])
                if tok0 + 128 in seq_ends:
                    k._memset("pool", n_[96:128, :], 0.0)
                    k.dma("sp", n_[0:127, :], S["rw_raw"][tok0 + 1:tok0 + 128, :])
                else:
                    k.dma("sp", n_[:, :], S["rw_raw"][tok0 + 1:tok0 + 129, :])
                yield
                k.tt("pool", dd[:, :], p_[:, :], n_[:, :], ALU.add)
                yield
                k.stt("dve", dd[:, :], dd[:, :], 0.5, c_[:, :], ALU.mult, ALU.subtract)
                yield
                k.tt("pool", dd[:, :], dd[:, :], mu[:, :], ALU.mult)
                yield
                k.tt("dve", pp[:, :], dd[:, :], c_[:, :], ALU.add)
                yield
                rr, kk_, vv = pp[:, 0:512], pp[:, 512:1024], pp[:, 1024:1536]
                k.copy("act", col(o, 0), rr)
                k.copy("act", col(o, 2), vv)
                k.tt("pool", t1[:, :], kk_, kkb[:, :], ALU.mult)
                k.act(sm[:, 0:128], pp[:, 1536:1664], AF.Tanh)
                k.copy("dve", sm[:, 128:256], pp[:, 1664:1792])
                k.act(sm[:, 256:384], pp[:, 1792:1920], AF.Sigmoid)
                yield
                k.act(t2[:, :], t1[:, :], AF.Square)
                for j in range(4):
                    k.tr(pT[0:64, j * 128:(j + 1) * 128], sm[:, j * 64:(j + 1) * 64], idf[:, :])
                k.tr(pT2[:, 0:128], sm[:, 256:384], idf[:, :])
                yield
                k.reduce("dve", stt[:, 0:8], t2[:, :].rearrange("p (h d) -> p h d", h=8))
                k.copy("dve", tT[:, :, :], pT[0:64, :].rearrange("p (j t) -> p j t", j=4))
                k.copy("dve", gT[:, :], pT2[:, 0:128])
                yield
                k.act(stt[:, 8:16], stt[:, 0:8], AF.Ln, scale=1.0, bias=1e-12)
                for d_ in range(2):
                    k.mm(pw[d_][:, :], tT[:, d_, :], wup[:, d_, :])
                yield
                k.act(stt[:, 0:8], stt[:, 8:16], AF.Exp, scale=-0.5)
                for d_ in range(2):
                    k.tt("dve", (t2 if d_ == 0 else t3)[:, :], pw[d_][:, :], w0b[:, d_, :], ALU.add)
                yield
                k.tt("dve", col(o, 1).rearrange("p (h d) -> p h d", h=8), t1[:, :].rearrange("p (h d) -> p h d", h=8),
                     stt[:, 0:8].unsqueeze(2).bc([128, 8, 64]), ALU.mult)
                k.act(t2[:, :], t2[:, :], AF.Sigmoid)
                k.act(t3[:, :], t3[:, :], AF.Sigmoid)
                for d_ in range(2):
                    k.mm(pw[d_][:, :], tT[:, 2 + d_, :], aup[:, d_, :])
                yield
                k.ts("pool", col(o, 4), t2[:, :], -0.6065306597126334, None, ALU.mult)
                k.ts("pool", col(o, 7), t3[:, :], -0.6065306597126334, None, ALU.mult)
                k.tt("dve", a_[:, :], pw[0][:, :], a0b[:, 0, :], ALU.add)
                k.tt("dve", t1[:, :], pw[1][:, :], a0b[:, 1, :], ALU.add)
                yield
                k.act(a_[:, :], a_[:, :], AF.Sigmoid)
                k.act(t1[:, :], t1[:, :], AF.Sigmoid)
                k.mm(pw[0][:, :], gT[:, :], gup[:, :])
                yield
                for d_, av in ((0, a_), (1, t1)):
                    k.tt("pool", col(o, 6 + 3 * d_), av[:, :], col(o, 1), ALU.mult)
                    k.stt("dve", (t2 if d_ == 0 else t3)[:, :], av[:, :], -1.0, kab[:, :], ALU.add, ALU.mult)
                k.copy("act", col(o, 3), pw[0][:, :])
                yield
                k.stt("dve", col(o, 5), t2[:, :], 1.0, kk_, ALU.add, ALU.mult)
                k.stt("dve", col(o, 8), t3[:, :], 1.0, kk_, ALU.add, ALU.mult)
                yield
                k.tt("pool", t2[:, :], col(o, 5), col(o, 8), ALU.add)
                yield
                k.tt("dve", t2[:, :], t2[:, :], rr, ALU.mult)
                yield
                k.tt("pool", t2[:, :], t2[:, :], rkb[:, :], ALU.mult)
                yield
                k.reduce("dve", o[:, 5120:5128], t2[:, :].rearrange("p (h d) -> p h d", h=8))
                k.dma("pool", S["rwpA"][tok0:tok0 + 128, :], o[:, 0:2048])
                k.dma("pool", S["rwpB"][tok0:tok0 + 128, :], o[:, 2048:RWC])

            self.run_pipeline([tile(ti) for ti in range(self.nt)], depth=2)

    def pass_rwscan(self, l):
        k, S = self.k, self.S
        with self.phase():
            idf = k.sb("idf", [128, 128], F32)
            eblk = k.sb("eblk", [128, 512], F32)
            k.dma("sp", idf[:, :], self.c_ident[:, :])
            idb = k.sb("idb", [128, 128], BF16)
            k.copy("dve", idb[:, :], idf[:, :])
            Vb = k.sb("Vb", [128, 512], BF16)
            k.dma("sp", eblk[:, :], self.c_eblk[:, :])
            M4 = k.sb("M4", [128, 4, 128], F32)
            MI = k.sb("MI", [128, 128], F32)
            BLK = k.sb("BLK", [128, 128], F32)
            A = k.rot("A", [128, 1536], F32, 2)
            Bd = k.rot("Bd", [128, 1536], F32, 2)
            cumS = k.sb("cumS", [128, 512], F32)
            tmp = k.sb("tmp", [128, 512], F32)
            e1 = k.sb("e1", [128, 512], F32)
            e2 = k.sb("e2", [128, 512], F32)
            e3 = k.sb("e3", [128, 512], F32)
            e4 = k.sb("e4", [128, 512], F32)
            e5 = k.sb("e5", [128, 512], F32)
            Abar = k.sb("Abar", [128, 512], BF16)
            Rbar = k.sb("Rbar", [128, 512], BF16)
            Kt = k.sb("Kt", [128, 512], BF16)
            Bt = k.sb("Bt", [128, 512], BF16)
            Kcs = [k.sb("Kc%d" % i, [128, 512], BF16) for i in range(2)]
            Bcs = [k.sb("Bc%d" % i, [128, 512], BF16) for i in range(2)]
            e4c = [k.sb("e4c%d" % i, [128, 512], F32) for i in range(2)]
            Yd = k.sb("Yd", [128, 512], BF16)
            XTs = [k.sb("XT%d" % i, [64, 8, 128], BF16) for i in range(4)]
            PM = k.sb("PM", [128, 8, 4, 128], BF16)
            MRK = k.sb("MRK", [128, 8, 128], BF16)
            Xs = k.rot("Xs", [128, 8, 128], BF16, 2)
            XsT = k.rot("XsT", [128, 8, 128], BF16, 2)
            Z = k.rot("Z", [128, 8, 128], BF16, 2)
            NZ = k.sb("NZ", [128, 8, 128], BF16)
            RTa = k.sb("RTa", [64, 8, 128], F32)
            RTb = k.sb("RTb", [64, 8, 128], F32)
            Y0 = k.sb("Y0", [128, 512], F32)
            GT = k.sb("GT", [64, 2, 8, 64], F32)
            HS = k.sb("HS", [64, 2, 8, 64], F32)
            ST = k.rot("ST", [64, 8, 64], F32, 2)
            yo = k.rot("yo", [128, 512], F32, 2)
            pA = k.rot("pA", [128, 512], F32, 6, psum=True)
            pYs = k.rot("pY", [128, 512], F32, 2, psum=True)
            k._memset("pool", RTa[:, :, :], 0.0)
            k._memset("pool", RTb[:, :, :], 0.0)
            npa = [0]

            def bank():
                b = pA[npa[0] % 6]
                npa[0] += 1
                return b

            for dr in range(2):
                k.dma("sp", M4[:, 0, :], self.c_rwm[dr, 0])
                k.dma("sp", M4[:, 1, :], self.c_rwm[dr, 1])
                k.dma("sp", M4[:, 2, :], self.c_rwm[dr, 0])
                k.dma("sp", M4[:, 3, :], self.c_rwm[dr, 2])
                k.dma("sp", MI[:, :], self.c_rwm[dr, 2])
                k.dma("sp", BLK[:, :], self.c_rwm[dr, 3])
                ydst = S["yd%d" % dr]
                sti = 0
                for s, (t0, L) in enumerate(zip(self.seq_off, self.seqs)):
                    ntl = L // 128
                    tiles = range(ntl) if dr == 0 else range(ntl - 1, -1, -1)
                    st_cur = ST[sti % 2]
                    k._memset("pool", st_cur[:, :, :], 0.0)
                    for tl in tiles:
                        tok0 = t0 + tl * 128
                        a, b = A[tl % 2], Bd[tl % 2]
                        k.dma("sp", a[:, :], S["rwpA"][tok0:tok0 + 128, 0:1536])
                        k.dma("sp", b[:, :], S["rwpB"][tok0:tok0 + 128, dr * 1536:(dr + 1) * 1536])
                        Rr, KK, Vv = a[:, 0:512], a[:, 512:1024], a[:, 1024:1536]
                        LW, KD, BE = b[:, 0:512], b[:, 512:1024], b[:, 1024:1536]
                        pc = bank()
                        pcc = bank()
                        k.mm(pc[:, :], MI[:, :], LW)
                        k.mm(pcc[:, :], BLK[:, :], LW)
                        k.copy("act", cumS[:, :], pc[:, :])
                        k.act(e1[:, :], pc[:, :], AF.Exp)
                        k.act(e3[:, :], pc[:, :], AF.Exp, scale=-1.0)
                        k.tt("pool", tmp[:, :], cumS[:, :], LW, ALU.subtract)
                        k.act(e2[:, :], tmp[:, :], AF.Exp)
                        k.copy("dve", e5[:, :], pcc[:, :])
                        k.tt("pool", e4[:, :], e5[:, :], cumS[:, :], ALU.subtract)
                        k.act(e4[:, :], e4[:, :], AF.Exp)
                        k.act(e5[:, :], e5[:, :], AF.Exp)
                        k.tt("pool", Abar[:, :], KK, e2[:, :], ALU.mult)
                        k.tt("dve", Rbar[:, :], Rr, e1[:, :], ALU.mult)
                        k.tt("pool", Kt[:, :], KD, e3[:, :], ALU.mult)
                        k.tt("dve", Bt[:, :], BE, e3[:, :], ALU.mult)
                        for c in range(2):
                            k.ts("pool" if c else "dve", e4c[c][:, :], e4[:, :], BLK[:, c * 127:c * 127 + 1], None, ALU.mult)
                            k.tt("pool", Kcs[c][:, :], KD, e4c[c][:, :], ALU.mult)
                            k.tt("dve", Bcs[c][:, :], BE, e4c[c][:, :], ALU.mult)
                        k.tt("pool", Yd[:, :], eblk[:, :], e5[:, :], ALU.mult)
                        k.copy("act", Vb[:, :], Vv)
                        for i, X in enumerate((Abar, Bt, Kt, Rbar)):
                            for half in range(2):
                                p = bank()
                                pb = p[:, :].bitcast(BF16)
                                for hh in range(4):
                                    h = half * 4 + hh
                                    k.tr(pb[0:64, hh * 128:(hh + 1) * 128], X[:, h * 64:(h + 1) * 64], idb[:, :])
                                k.copy("act" if half else "dve", XTs[i][:, half * 4:(half + 1) * 4, :],
                                       pb[0:64, 0:512].rearrange("p (j t) -> p j t", j=4))
                        aT, bT, kT_, rT = XTs

                        def hv(X, h):
                            return X[:, h, :]
                        for h in range(8):
                            p = bank()
                            k.mm(p[:, 0:128], hv(bT, h), hv(aT, h))
                            k.mm(p[:, 128:256], hv(aT, h), hv(bT, h))
                            k.mm(p[:, 256:384], hv(kT_, h), hv(aT, h))
                            k.mm(p[:, 384:512], hv(bT, h), hv(rT, h))
                            k.tt("dve", PM[:, h, :, :], p[:, :].rearrange("p (j t) -> p j t", j=4), M4[:, :, :], ALU.mult)
                        for half in range(2):
                            p = bank()
                            for hh in range(4):
                                h = half * 4 + hh
                                k.mm(p[:, hh * 128:(hh + 1) * 128], hv(kT_, h), hv(rT, h))
                            k.tt("dve", MRK[:, half * 4:(half + 1) * 4, :], p[:, :].rearrange("p (j t) -> p j t", j=4),
                                 MI[:, :].unsqueeze(1).bc([128, 4, 128]), ALU.mult)
                        z = Z[0]
                        zi = 0
                        p = bank()
                        for h in range(8):
                            k.mm(p[:, h * 64:(h + 1) * 64], PM[:, h, 2, :], Vb[:, h * 64:(h + 1) * 64])
                        k.copy("act", z[:, :, 0:64], Abar[:, :].rearrange("p (h d) -> p h d", h=8))
                        k.copy("dve", z[:, :, 64:128], p[:, :].rearrange("p (h d) -> p h d", h=8))
                        curX = PM[:, :, 1, :]
                        curXT = PM[:, :, 0, :]
                        for lev in range(6):
                            znew = Z[(zi + 1) % 2]
                            for half in range(2):
                                p = bank()
                                for hh in range(4):
                                    h = half * 4 + hh
                                    k.mm(p[:, hh * 128:(hh + 1) * 128], curXT[:, h, :], z[:, h, :])
                                k.tt("dve", znew[:, half * 4:(half + 1) * 4, :], z[:, half * 4:(half + 1) * 4, :],
                                     p[:, :].rearrange("p (j t) -> p j t", j=4), ALU.subtract if lev == 0 else ALU.add)
                            z = znew
                            zi += 1
                            if lev == 5:
                                break
                            nX, nXT = Xs[lev % 2], XsT[lev % 2]
                            for half in range(2):
                                p = bank()
                                for hh in range(4):
                                    h = half * 4 + hh
                                    k.mm(p[:, hh * 128:(hh + 1) * 128], curX[:, h, :], curXT[:, h, :])
                                k.copy("act", nXT[:, half * 4:(half + 1) * 4, :], p[:, :].rearrange("p (j t) -> p j t", j=4))
                                if lev < 4:
                                    p = bank()
                                    for hh in range(4):
                                        h = half * 4 + hh
                                        k.mm(p[:, hh * 128:(hh + 1) * 128], curXT[:, h, :], curX[:, h, :])
                                    k.copy("act", nX[:, half * 4:(half + 1) * 4, :], p[:, :].rearrange("p (j t) -> p j t", j=4))
                            curX = nX[:, :, :]
                            curXT = nXT[:, :, :]
                        k.ts("pool", NZ[:, :, :], z[:, :, :], -1.0, None, ALU.mult)
                        for half in range(2):
                            p = bank()
                            for hh in range(4):
                                h = half * 4 + hh
                                k.mm(p[0:64, hh * 128:(hh + 1) * 128], Rbar[:, h * 64:(h + 1) * 64], idb[:, :], start=True, stop=False)
                                k.mm(p[0:64, hh * 128:(hh + 1) * 128], NZ[:, h, 0:64], PM[:, h, 3, :], start=False, stop=True)
                            p3 = p[0:64, :].rearrange("p (j t) -> p j t", j=4)
                            k.copy("dve", RTa[:, half * 4:(half + 1) * 4, 0:64], p3[:, :, 0:64])
                            k.copy("dve", RTb[:, half * 4:(half + 1) * 4, 64:128], p3[:, :, 64:128])
                        p = bank()
                        for h in range(8):
                            k.mm(p[:, h * 64:(h + 1) * 64], MRK[:, h, :], Vb[:, h * 64:(h + 1) * 64], start=True, stop=False)
                            k.mm(p[:, h * 64:(h + 1) * 64], PM[:, h, 3, :], NZ[:, h, 64:128], start=False, stop=True)
                        k.copy("act", Y0[:, :], p[:, :])
                        for c in range(2):
                            cs_ = slice(c * 64, (c + 1) * 64)
                            p = bank()
                            p2 = bank()
                            for h in range(8):
                                hs_ = slice(h * 64, (h + 1) * 64)
                                k.mm(p[0:64, hs_], NZ[:, h, 0:64], Bcs[c][:, hs_], start=True, stop=False)
                                k.mm(p[0:64, hs_], idb[:, cs_], Yd[:, hs_], start=False, stop=True)
                                k.mm(p2[0:64, hs_], Kcs[c][:, hs_], Vb[:, hs_], start=True, stop=False)
                                k.mm(p2[0:64, hs_], Bcs[c][:, hs_], NZ[:, h, 64:128], start=False, stop=True)
                            k.copy("act", GT[:, c, :, :], p[0:64, :].rearrange("p (h d) -> p h d", h=8))
                            k.copy("dve", HS[:, c, :, :], p2[0:64, :].rearrange("p (h d) -> p h d", h=8))
                        order = (0, 1) if dr == 0 else (1, 0)
                        for ci, c in enumerate(order):
                            RTx = RTa if c == 0 else RTb
                            for h in range(8):
                                k.mm(pYs[ci][:, h * 64:(h + 1) * 64], RTx[:, h, :], st_cur[:, h, :])
                            p = bank()
                            for h in range(8):
                                k.mm(p[0:64, h * 64:(h + 1) * 64], GT[:, c, h, :], st_cur[:, h, :])
                            sti += 1
                            st_new = ST[sti % 2]
                            k.tt("dve", st_new[:, :, :], p[0:64, :].rearrange("p (h d) -> p h d", h=8), HS[:, c, :, :], ALU.add)
                            st_cur = st_new
                        yo_ = yo[tl % 2]
                        k.tt("dve", yo_[:, :], pYs[0][:, :], Y0[:, :], ALU.add)
                        k.tt("dve", yo_[:, :], pYs[1][:, :], yo_[:, :], ALU.add)
                        k.dma("pool", ydst[tok0:tok0 + 128, :], yo_[:, :])

    def pass_rwfin(self, l):
        k, W, S = self.k, self.W, self.S
        with self.phase():
            lnw = k.sb("lnw", [128, 512], F32)
            lnb = k.sb("lnb", [128, 512], F32)
            self.load_bc(lnw[:, :], W["rw_ln_w"][l, :])
            self.load_bc(lnb[:, :], W["rw_ln_b"][l, :])
            yf = k.rot("yf", [128, 512], F32, 2)
            yb = k.rot("yb", [128, 512], F32, 2)
            vg = k.rot("vg", [128, 1024], F32, 2)
            bo = k.rot("bo", [128, 8], F32, 2)
            y = k.sb("y", [128, 512], F32)
            t = k.sb("t", [128, 512], F32)
            st = k.rot("st", [128, 32], F32, 2)
            ob = k.rot("ob", [128, 512], BF16, 2)

            def h3(v):
                return v.rearrange("p (h d) -> p h d", h=8)
            for ti in range(self.nt):
                tok0 = ti * 128
                a, b, vg_, bo_, stt = yf[ti % 2], yb[ti % 2], vg[ti % 2], bo[ti % 2], st[ti % 2]
                k.dma("sp", a[:, :], S["yd0"][tok0:tok0 + 128, :])
                k.dma("sp", b[:, :], S["yd1"][tok0:tok0 + 128, :])
                k.dma("sp", vg_[:, :], S["rwpA"][tok0:tok0 + 128, 1024:2048])
                k.dma("sp", bo_[:, :], S["rwpB"][tok0:tok0 + 128, 3072:3080])
                k.tt("pool", y[:, :], a[:, :], b[:, :], ALU.add)
                k.reduce("dve", stt[:, 0:8], h3(y[:, :]))
                k.ts("dve", stt[:, 0:8], stt[:, 0:8], 1.0 / 64, None, ALU.mult)
                k.tt("dve", h3(y[:, :]), h3(y[:, :]), stt[:, 0:8].unsqueeze(2).bc([128, 8, 64]), ALU.subtract)
                k.act(t[:, :], y[:, :], AF.Square)
                k.reduce("dve", stt[:, 8:16], h3(t[:, :]))
                self.rstd(stt[:, 8:16], stt[:, 8:16], stt[:, 16:24], 1.0 / 64, 64e-5)
                k.tt("dve", h3(y[:, :]), h3(y[:, :]), stt[:, 8:16].unsqueeze(2).bc([128, 8, 64]), ALU.mult)
                k.tt("pool", y[:, :], y[:, :], lnw[:, :], ALU.mult)
                k.tt("pool", y[:, :], y[:, :], lnb[:, :], ALU.add)
                k.tt("dve", h3(t[:, :]), h3(vg_[:, 0:512]), bo_[:, :].unsqueeze(2).bc([128, 8, 64]), ALU.mult)
                k.tt("pool", y[:, :], y[:, :], t[:, :], ALU.add)
                k.tt("dve", ob[ti % 2][:, :], y[:, :], vg_[:, 512:1024], ALU.mult)
                k.dma("pool", S["yc"][tok0:tok0 + 128, :], ob[ti % 2][:, :])

    def pass_merge(self, l, xsrc):
        k, W, S = self.k, self.W, self.S
        with self.phase():
            idf = k.sb("idf", [128, 128], F32)
            idb = k.sb("idb", [128, 128], BF16)
            k.dma("sp", idf[:, :], self.c_ident[:, :])
            k.copy("dve", idb[:, :], idf[:, :])
            wa = k.sb("wa", [128, 4, D], BF16)
            wb = k.sb("wb", [128, 4, D], BF16)
            wc = k.sb("wc", [128, 4, D], BF16)
            wo = k.sb("wo", [128, 8, D], BF16)
            self.load_w(wa, W["na_proj"][l], 4, D)
            self.load_w(wb, W["mla_proj"][l], 4, D)
            self.load_w(wc, W["rw_proj"][l], 4, D)
            self.load_w(wo, W["w_out"][l], 8, D)
            xa = k.rot("xa", [128, 512], BF16, 2)
            xc = k.rot("xc", [128, 512], BF16, 2)
            oTb = k.rot("oTb", [128, 4, 128], BF16, 2)
            gt = k.rot("gt", [128, 3072], BF16, 2)
            xt = k.rot("xt", [128, D], F32, 2)
            aT = k.sb("aT", [128, 4, 128], BF16)
            cT = k.sb("cT", [128, 4, 128], BF16)
            mixed = k.sb("mixed", [128, D], F32)
            tmp = k.rot("tmp", [128, 512], F32, 2)
            mixb = k.sb("mixb", [128, D], BF16)
            mT = k.sb("mT", [128, 8, 128], BF16)
            xo = k.rot("xo", [128, D], F32, 2)
            pT = k.ps("pT", [128, 1024], BF16)
            pm = k.rot("pm", [128, 512], F32, 4, psum=True)
            npm = 0
            for ti in range(self.nt):
                tok0 = ti * 128
                i2 = ti % 2
                k.dma("sp", xa[i2][:, :], S["na_out"][tok0:tok0 + 128, :])
                k.dma("sp", xc[i2][:, :], S["yc"][tok0:tok0 + 128, :])
                k.dma("sp", oTb[i2][:, :, :], S["oT_b"][:, tok0:tok0 + 128].rearrange("(j p) t -> p j t", p=128))
                k.dma("sp", gt[i2][:, :], S["gates"][tok0:tok0 + 128, :])
                k.dma("sp", xt[i2][:, :], xsrc[tok0:tok0 + 128, :])
                for j in range(4):
                    k.tr(pT[:, j * 128:(j + 1) * 128], xa[i2][:, j * 128:(j + 1) * 128], idb[:, :])
                    k.tr(pT[:, 512 + j * 128:512 + (j + 1) * 128], xc[i2][:, j * 128:(j + 1) * 128], idb[:, :])
                k.copy("dve", aT[:, :, :], pT[:, 0:512].rearrange("p (j t) -> p j t", j=4))
                k.copy("dve", cT[:, :, :], pT[:, 512:1024].rearrange("p (j t) -> p j t", j=4))
                for bi, (T_, W_) in enumerate(((aT, wa), (oTb[i2], wb), (cT, wc))):
                    for nch in range(2):
                        p = pm[npm % 4]
                        npm += 1
                        for c in range(4):
                            k.mm(p[:, :], T_[:, c, :], W_[:, c, nch * 512:(nch + 1) * 512], start=(c == 0), stop=(c == 3))
                        gsl = gt[i2][:, bi * D + nch * 512:bi * D + (nch + 1) * 512]
                        if bi == 0:
                            k.tt("dve", mixed[:, nch * 512:(nch + 1) * 512], p[:, :], gsl, ALU.mult)
                        else:
                            t_ = tmp[npm % 2]
                            k.tt("dve", t_[:, :], p[:, :], gsl, ALU.mult)
                            k.tt("pool", mixed[:, nch * 512:(nch + 1) * 512], mixed[:, nch * 512:(nch + 1) * 512], t_[:, :], ALU.add)
                k.copy("act", mixb[:, :], mixed[:, :])
                for c in range(8):
                    k.tr(pT[:, c * 128:(c + 1) * 128], mixb[:, c * 128:(c + 1) * 128], idb[:, :])
                k.copy("dve", mT[:, :, :], pT[:, :].rearrange("p (j t) -> p j t", j=8))
                for nch in range(2):
                    p = pm[npm % 4]
                    npm += 1
                    for c in range(8):
                        k.mm(p[:, :], mT[:, c, :], wo[:, c, nch * 512:(nch + 1) * 512], start=(c == 0), stop=(c == 7))
                    k.tt("dve", xo[i2][:, nch * 512:(nch + 1) * 512], p[:, :], xt[i2][:, nch * 512:(nch + 1) * 512], ALU.add)
                k.dma("pool", S["xmid"][tok0:tok0 + 128, :], xo[i2][:, :])

    def pass_ffn(self, l, xdst):
        k, W, S = self.k, self.W, self.S
        NF = DFF // 128
        with self.phase():
            idf = k.sb("idf", [128, 128], F32)
            idb = k.sb("idb", [128, 128], BF16)
            g2 = k.sb("g2", [128, 8], F32)
            k.dma("sp", idf[:, :], self.c_ident[:, :])
            k.copy("dve", idb[:, :], idf[:, :])
            k.dma("sp", g2[:, :], W["norm2_g"][l, :].rearrange("(c p) -> p c", p=128), slow=True)
            wg = k.sb("wg", [128, 8, DFF], BF16)
            wu = k.sb("wu", [128, 8, DFF], BF16)
            wd = k.sb("wd", [128, NF, D], BF16)
            self.load_w(wg, W["ffn_w_gate"][l], 8, DFF, gcol=g2, chunk=256)
            self.load_w(wu, W["ffn_w_up"][l], 8, DFF, gcol=g2, chunk=256)
            self.load_w(wd, W["ffn_w_down"][l], NF, D, chunk=128)
            TS = 2
            xt = k.rot("xt", [128, TS, D], F32, 2)
            junk = k.sb("junk", [128, D], BF16)
            hb = k.sb("hb", [128, D], BF16)
            xT = k.sb("xT", [128, 8, TS * 128], BF16)
            st = k.rot("st", [128, 4], F32, 2)
            sg = k.rot("sg", [128, TS * 128], F32, 2)
            hT = k.sb("hT", [128, NF, TS * 128], BF16)
            xo = k.rot("xo", [128, D], F32, 2)
            pT = k.ps("pT", [128, 1024], BF16)
            pg = k.rot("pg", [128, 512], F32, 2, psum=True)
            pu = k.rot("pu", [128, 512], F32, 2, psum=True)
            pm = k.rot("pm", [128, 512], F32, 2, psum=True)
            npm = 0
            nx = 0
            for tb in range(self.nt // TS):
                x_ = xt[tb % 2]
                for s in range(TS):
                    tok0 = (tb * TS + s) * 128
                    stt = st[s % 2]
                    k.dma("sp", x_[:, s, :], S["xmid"][tok0:tok0 + 128, :])
                    k._memset("pool", stt[:, :], 0.0)
                    k.act(junk[:, :], x_[:, s, :], AF.Square, accum_out=stt[:, 0:1])
                    self.rstd(stt[:, 1:2], stt[:, 0:1], stt[:, 2:3], 1.0 / D, 1e-6)
                    k.act(hb[:, :], x_[:, s, :], AF.Copy, scale=stt[:, 1:2])
                    for c in range(8):
                        k.tr(pT[:, c * 128:(c + 1) * 128], hb[:, c * 128:(c + 1) * 128], idb[:, :])
                    k.copy("dve", xT[:, :, s * 128:(s + 1) * 128], pT[:, :].rearrange("p (c t) -> p c t", c=8))
                for f in range(NF):
                    pg_, pu_ = pg[f % 2], pu[f % 2]
                    for c in range(8):
                        k.mm(pg_[:, 0:TS * 128], wg[:, c, f * 128:(f + 1) * 128], xT[:, c, :], start=(c == 0), stop=(c == 7))
                    for c in range(8):
                        k.mm(pu_[:, 0:TS * 128], wu[:, c, f * 128:(f + 1) * 128], xT[:, c, :], start=(c == 0), stop=(c == 7))
                    sg_ = sg[f % 2]
                    k.act(sg_[:, :], pg_[:, 0:TS * 128], AF.Silu)
                    k.tt("dve", hT[:, f, :], sg_[:, :], pu_[:, 0:TS * 128], ALU.mult)
                for s in range(TS):
                    tok0 = (tb * TS + s) * 128
                    xo_ = xo[nx % 2]
                    nx += 1
                    for nch in range(2):
                        p = pm[npm % 2]
                        npm += 1
                        for f in range(NF):
                            k.mm(p[:, :], hT[:, f, s * 128:(s + 1) * 128], wd[:, f, nch * 512:(nch + 1) * 512],
                                 start=(f == 0), stop=(f == NF - 1))
                        k.tt("dve", xo_[:, nch * 512:(nch + 1) * 512], p[:, :], x_[:, s, nch * 512:(nch + 1) * 512], ALU.add)
                    k.dma("pool", xdst[tok0:tok0 + 128, :], xo_[:, :])


def make_consts(seqs, na_rpb):
    ntok = sum(seqs)
    c = {}
    c["c_ident"] = np.eye(128, dtype=np.float32)
    rope = np.zeros((ntok, 32), np.float32)
    o = 0
    inv = (10000.0 ** (-np.arange(8, dtype=np.float32) / 8)).astype(np.float32)
    for L in seqs:
        t = np.arange(L)
        row = (t // 64).astype(np.float32)
        colp = (t % 64).astype(np.float32)
        ang = np.concatenate([row[:, None] * inv, colp[:, None] * inv], axis=-1).astype(np.float32)
        rope[o:o + L, 0:16] = np.cos(ang)
        rope[o:o + L, 16:32] = np.sin(ang)
        o += L
    c["c_rope"] = rope
    p = np.arange(128)
    kc = np.arange(4)
    key = kc[None, :] * 128 + p[:, None]
    w = key // 64
    kcol = key % 64
    q = np.arange(64)
    win = np.clip(q - 8, 0, 48)
    valid = (kcol[:, :, None] >= win[None, None, :]) & (kcol[:, :, None] < win[None, None, :] + 16)
    dc = np.clip(kcol[:, :, None] - q[None, None, :] + 15, 0, 30)
    nab = np.full((DEPTH, 4, 128, 8, 2, 4, 64), -30000.0, np.float32)
    for d in range(8):
        dri = np.clip(w - d + 7, 0, 14)
        drb = np.broadcast_to(dri[:, :, None], dc.shape)
        for hp in range(4):
            for h2 in range(2):
                g = na_rpb[:, 2 * hp + h2][:, drb, dc]
                nab[:, hp, :, d, h2] = np.where(valid[None], g, np.float32(-30000.0))
    c["c_nab"] = nab.reshape(DEPTH, 4, 128, 8 * 512)
    j = np.arange(128)[:, None]
    t = np.arange(128)[None, :]
    same = (j // 64) == (t // 64)
    rwm = np.zeros((2, 4, 128, 128), np.float32)
    rwm[0, 0] = same & (j < t)
    rwm[0, 1] = rwm[0, 0].T
    rwm[0, 2] = same & (j <= t)
    rwm[0, 3] = same
    rwm[1, 0] = same & (j > t)
    rwm[1, 1] = rwm[1, 0].T
    rwm[1, 2] = same & (j >= t)
    rwm[1, 3] = same
    c["c_rwm"] = rwm
    eb = np.zeros((128, 512), np.float32)
    for tt in range(128):
        eb[tt, np.arange(8) * 64 + (tt % 64)] = 1.0
    c["c_eblk"] = eb
    return c


WNAMES = ["norm1_g", "w_in", "b_gate", "na_q_norm", "na_k_norm", "na_proj", "mla_cq_norm", "mla_ckv_norm", "mla_w_uq",
          "mla_w_ukv", "mla_q_norm", "mla_k_norm", "mla_proj", "rw_mu", "rw_w0", "rw_w_up", "rw_a0", "rw_a_up", "rw_g_up",
          "rw_k_k", "rw_k_a", "rw_r_k", "rw_ln_w", "rw_ln_b", "rw_proj", "w_out", "norm2_g", "ffn_w_gate", "ffn_w_up",
          "ffn_w_down"]

_PROG = {}


def get_prog(seqs, dbg=()):
    key = (tuple(seqs), tuple(sorted(dbg)))
    if key not in _PROG:
        _PROG[key] = Prog(seqs, dbg)
    return _PROG[key]


def kernel(**inputs):
    xp = np.asarray(inputs["x_prompt"], np.float32)
    xs = np.asarray(inputs["x_sample"], np.float32)
    n = 8
    prog = get_prog(FULL_SEQS)
    consts = make_consts(FULL_SEQS, np.asarray(inputs["na_rpb"], np.float32))
    shared = {nm: np.ascontiguousarray(np.asarray(inputs[nm], np.float32)) for nm in WNAMES}
    shared.update(consts)
    in_maps = []
    for c in range(n):
        m = dict(shared)
        m["x"] = np.ascontiguousarray(np.concatenate([xp[c], xs[2 * c], xs[2 * c + 1]], axis=0))
        in_maps.append(m)
    res = run_bass_kernel_spmd(prog.nc, in_maps, core_ids=list(range(n)))
    yp = np.empty_like(xp)
    ys = np.empty_like(xs)
    for c in range(n):
        y = np.asarray(res.results[c]["y"], np.float32)
        yp[c] = y[0:8192]
        ys[2 * c] = y[8192:12288]
        ys[2 * c + 1] = y[12288:16384]
    return (yp, ys)
```

```python
import numpy as np
import ml_dtypes
from contextlib import ExitStack
import concourse.bass as bass
import concourse.mybir as mybir
from concourse.bass_utils import run_bass_kernel_spmd

F32 = mybir.dt.float32
BF16 = mybir.dt.bfloat16
AF = mybir.ActivationFunctionType
ALU = mybir.AluOpType
AX = mybir.AxisListType

DEPTH = 2
D = 1024
DIN = 6944
DFF = 2816
NH = 8
C_Q, C_K, C_V, C_CQ, C_CKV, C_KR, C_RW, C_G = 0, 512, 1024, 1536, 1792, 1920, 1952, 3872
RWC = 5128
FULL_SEQS = (8192, 4096, 4096)


class V:
    __slots__ = ("ap", "buf")

    def __init__(self, ap, buf):
        self.ap = ap
        self.buf = buf

    def __getitem__(self, key):
        return V(self.ap[key], self.buf)

    def rearrange(self, pat, **kw):
        return V(self.ap.rearrange(pat, **kw), self.buf)

    def unsqueeze(self, a):
        return V(self.ap.unsqueeze(a), self.buf)

    def bc(self, shape):
        return V(self.ap.broadcast_to(list(shape)), self.buf)

    def bitcast(self, dt):
        return V(self.ap.bitcast(dt), self.buf)


class Buf:
    __slots__ = ("t", "w", "r", "name")

    def __init__(self, t, name):
        self.t = t
        self.w = None
        self.r = {}
        self.name = name

    def __getitem__(self, key):
        return V(self.t[key], self)


class K:
    NDMA = 16

    def __init__(self, nc, es):
        self.nc = nc
        self.es = es
        self.eng = {"pe": nc.tensor, "act": nc.scalar, "dve": nc.vector, "pool": nc.gpsimd, "sp": nc.sync}
        self.sem = {}
        self.cnt = {}
        self.seen = {n: {} for n in self.eng}
        for n in self.eng:
            self.sem[n] = es.enter_context(nc.semaphore("s_" + n))
            self.cnt[n] = 0
        self.dq = {"sp": 0, "pool": 0, "act": 0}
        for q in self.dq:
            for i in range(self.NDMA):
                n = "%s_d%d" % (q, i)
                self.sem[n] = es.enter_context(nc.semaphore("s_" + n))
                self.cnt[n] = 0
        self.uid = 0
        import os as _os
        self.limit = int(_os.environ.get("KLIMIT", "0")) or None
        self.nops = 0
        self.log = [] if _os.environ.get("KLOG") else None

    def _lim(self):
        self.nops += 1
        if self.log is not None:
            import traceback
            fr = traceback.extract_stack(limit=4)[0:2]
            self.log.append((self.nops, [(f.lineno) for f in fr]))
        return self.limit is not None and self.nops > self.limit

    def sb(self, name, shape, dt):
        self.uid += 1
        return Buf(self.es.enter_context(self.nc.sbuf_tensor("%s_%d" % (name, self.uid), list(shape), dt)), name)

    def ps(self, name, shape, dt):
        self.uid += 1
        return Buf(self.es.enter_context(self.nc.psum_tensor("%s_%d" % (name, self.uid), list(shape), dt)), name)

    def rot(self, name, shape, dt, n, psum=False):
        return [(self.ps if psum else self.sb)("%s%d" % (name, i), shape, dt) for i in range(n)]

    def _wait(self, e, deps):
        seen = self.seen[e]
        for key, v in deps.items():
            if key == e and e == "pe":
                continue
            if seen.get(key, 0) >= v:
                continue
            self.eng[e].wait_ge(self.sem[key], v)
            seen[key] = v

    @staticmethod
    def _add(deps, ev):
        if ev is None:
            return
        key, v = ev
        if deps.get(key, 0) < v:
            deps[key] = v

    def _deps(self, reads, writes):
        deps = {}
        for b in reads:
            self._add(deps, b.w)
        for b in writes:
            self._add(deps, b.w)
            for key, v in b.r.items():
                self._add(deps, (key, v))
        return deps

    def _mark(self, ev, reads, writes):
        key, v = ev
        for b in reads:
            if b.r.get(key, 0) < v:
                b.r[key] = v
        for b in writes:
            b.w = ev
            b.r = {}

    def op(self, e, meth, **kw):
        if self._lim():
            return None
        reads, writes = [], []
        args = {}
        for name, val in kw.items():
            if isinstance(val, V):
                (writes if name in ("out", "accum_out") else reads).append(val.buf)
                args[name] = val.ap
            else:
                args[name] = val
        self._wait(e, self._deps(reads, writes))
        ins = getattr(self.eng[e], meth)(**args)
        self.cnt[e] += 1
        ins.then_inc(self.sem[e], 1)
        self._mark((e, self.cnt[e]), reads, writes)
        return ins

    def dma(self, q, out, in_, slow=False):
        if self._lim():
            return None
        reads, writes = [], []
        if isinstance(out, V):
            writes.append(out.buf)
            out = out.ap
        if isinstance(in_, V):
            reads.append(in_.buf)
            in_ = in_.ap
        i = self.dq[q] % self.NDMA
        self.dq[q] += 1
        key = "%s_d%d" % (q, i)
        deps = self._deps(reads, writes)
        if self.cnt[key] > 0:
            self._add(deps, (key, self.cnt[key]))
        self._wait(q, deps)
        if slow:
            ins = self.eng[q].dma_start(out=out, in_=in_, allow_slow_non_contiguous=True)
        else:
            ins = self.eng[q].dma_start(out=out, in_=in_)
        self.cnt[key] += 16
        ins.then_inc(self.sem[key], 16)
        self._mark((key, self.cnt[key]), reads, writes)
        return ins

    def barrier(self):
        allev = {key: v for key, v in self.cnt.items() if v > 0}
        for e in self.eng:
            self._wait(e, dict(allev))

    def act(self, out, in_, func, **kw):
        return self.op("act", "activation", out=out, in_=in_, func=func, **kw)

    def tt(self, e, out, in0, in1, op):
        return self.op(e, "tensor_tensor", out=out, in0=in0, in1=in1, op=op)

    def stt(self, e, out, in0, scalar, in1, op0, op1):
        return self.op(e, "scalar_tensor_tensor", out=out, in0=in0, scalar=scalar, in1=in1, op0=op0, op1=op1)

    def ts(self, e, out, in0, s1, s2, op0, op1=None):
        if op1 is None:
            return self.op(e, "tensor_scalar", out=out, in0=in0, scalar1=s1, scalar2=None, op0=op0)
        return self.op(e, "tensor_scalar", out=out, in0=in0, scalar1=s1, scalar2=s2, op0=op0, op1=op1)

    def copy(self, e, out, in_):
        if e == "act":
            return self.act(out, in_, AF.Copy)
        return self.op(e, "tensor_copy", out=out, in_=in_)

    def memset(self, e, out, val):
        return self.op(e, "memset", ap=out.ap, **{}) if False else self._memset(e, out, val)

    def _memset(self, e, out, val):
        if self._lim():
            return None
        self._wait(e, self._deps([], [out.buf]))
        ins = self.eng[e].memset(out.ap, val)
        self.cnt[e] += 1
        ins.then_inc(self.sem[e], 1)
        self._mark((e, self.cnt[e]), [], [out.buf])
        return ins

    def mm(self, out, lhsT, rhs, start=True, stop=True):
        return self.op("pe", "matmul", out=out, lhsT=lhsT, rhs=rhs, start=start, stop=stop)

    def tr(self, out, in_, ident):
        return self.op("pe", "transpose", out=out, in_=in_, identity=ident)

    def reduce(self, e, out, in_, op=ALU.add):
        return self.op(e, "tensor_reduce", out=out, in_=in_, axis=AX.X, op=op)


class Prog:
    def __init__(self, seqs, dbg=()):
        self.seqs = tuple(seqs)
        self.ntok = sum(seqs)
        self.nt = self.ntok // 128
        self.dbg = set(dbg)
        self.seq_off = [sum(seqs[:i]) for i in range(len(seqs))]
        self.nc = bass.Bass("TRN2", target_bir_lowering=False)
        self.build()

    def din(self, name, shape, dt=F32):
        return self.nc.dram_tensor(name, list(shape), dt, kind="ExternalInput").ap()

    def dscr(self, name, shape, dt=F32):
        kind = "ExternalOutput" if name in self.dbg else "Internal"
        return self.nc.dram_tensor(name, list(shape), dt, kind=kind).ap()

    def seq_of_tile(self, ti):
        tok = ti * 128
        for s, (o, L) in enumerate(zip(self.seq_off, self.seqs)):
            if o <= tok < o + L:
                return s, o, L
        raise AssertionError

    def build(self):
        nc = self.nc
        NT = self.ntok
        W = {}
        self.W = W
        self.x = self.din("x", [NT, D])
        shapes = dict(
            norm1_g=[DEPTH, D], w_in=[DEPTH, D, DIN], b_gate=[DEPTH, 3 * D], na_q_norm=[DEPTH, 64], na_k_norm=[DEPTH, 64],
            na_proj=[DEPTH, 512, D], mla_cq_norm=[DEPTH, 256], mla_ckv_norm=[DEPTH, 128], mla_w_uq=[DEPTH, 256, 768],
            mla_w_ukv=[DEPTH, 128, 1024], mla_q_norm=[DEPTH, 96], mla_k_norm=[DEPTH, 96], mla_proj=[DEPTH, 512, D],
            rw_mu=[DEPTH, 1920], rw_w0=[DEPTH, 2, 512], rw_w_up=[DEPTH, 2, 64, 512], rw_a0=[DEPTH, 2, 512],
            rw_a_up=[DEPTH, 2, 64, 512], rw_g_up=[DEPTH, 128, 512], rw_k_k=[DEPTH, 512], rw_k_a=[DEPTH, 512],
            rw_r_k=[DEPTH, 8, 64], rw_ln_w=[DEPTH, 512], rw_ln_b=[DEPTH, 512], rw_proj=[DEPTH, 512, D],
            w_out=[DEPTH, D, D], norm2_g=[DEPTH, D], ffn_w_gate=[DEPTH, D, DFF], ffn_w_up=[DEPTH, D, DFF],
            ffn_w_down=[DEPTH, DFF, D])
        for n, s in shapes.items():
            W[n] = self.din(n, s)
        self.c_ident = self.din("c_ident", [128, 128])
        self.c_rope = self.din("c_rope", [NT, 32])
        self.c_nab = self.din("c_nab", [DEPTH, 4, 128, 8 * 512])
        self.c_rwm = self.din("c_rwm", [2, 4, 128, 128])
        self.c_eblk = self.din("c_eblk", [128, 512])
        self.y = nc.dram_tensor("y", [NT, D], F32, kind="ExternalOutput").ap()
        S = {}
        self.S = S
        S["xmid"] = self.dscr("xmid", [NT, D])
        S["xres"] = self.dscr("xres", [NT, D])
        S["qT_na"] = self.dscr("qT_na", [512, NT], BF16)
        S["kT_na"] = self.dscr("kT_na", [512, NT], BF16)
        S["v_na"] = self.dscr("v_na", [NT, 512], BF16)
        S["qT_m"] = self.dscr("qT_m", [8, 96, NT], BF16)
        S["kT_m"] = self.dscr("kT_m", [8, 96, NT], BF16)
        S["v_m"] = self.dscr("v_m", [NT, 512], BF16)
        S["rw_raw"] = self.dscr("rw_raw", [NT, 1920])
        S["gates"] = self.dscr("gates", [NT, 3072], BF16)
        S["na_out"] = self.dscr("na_out", [NT, 512], BF16)
        S["oT_b"] = self.dscr("oT_b", [512, NT], BF16)
        S["rwpA"] = self.dscr("rwpA", [NT, 2048])
        S["rwpB"] = self.dscr("rwpB", [NT, RWC - 2048])
        S["yd0"] = self.dscr("yd0", [NT, 512])
        S["yd1"] = self.dscr("yd1", [NT, 512])
        S["yc"] = self.dscr("yc", [NT, 512], BF16)

        with ExitStack() as es:
            k = K(nc, es)
            self.k = k
            stop_after = None
            for d in self.dbg:
                if d.startswith("stop:"):
                    stop_after = d[5:]
            done = False
            for l in range(DEPTH):
                xsrc = self.x if l == 0 else S["xres"]
                xdst = self.y if l == DEPTH - 1 else S["xres"]
                for name, fn in (("inproj", lambda: self.pass_inproj(l, xsrc)),
                                 ("na", lambda: self.pass_na(l)),
                                 ("mla", lambda: self.pass_mla(l)),
                                 ("rwprep", lambda: self.pass_rwprep(l)),
                                 ("rwscan", lambda: self.pass_rwscan(l)),
                                 ("rwfin", lambda: self.pass_rwfin(l)),
                                 ("merge", lambda: self.pass_merge(l, xsrc)),
                                 ("ffn", lambda: self.pass_ffn(l, xdst))):
                    if "skip:" + name in self.dbg:
                        continue
                    fn()
                    if stop_after == "%s%d" % (name, l):
                        done = True
                        break
                if done:
                    break
            k.barrier()

    def phase(self):
        prog = self

        class _P:
            def __enter__(s):
                s.st = ExitStack()
                s.old = prog.k.es
                prog.k.es = s.st
                s.st.__enter__()
                return s

            def __exit__(s, *a):
                prog.k.barrier()
                prog.k.es = s.old
                return s.st.__exit__(*a)
        return _P()

    def load_bc(self, dst, src_row):
        self.k.dma("sp", dst, src_row.partition_broadcast(128))

    def load_w(self, dst, src, kc, ncols, gcol=None, chunk=512, f32=False):
        k = self.k
        with ExitStack() as st:
            old = k.es
            k.es = st
            stg = k.rot("stg", [128, kc, chunk], F32, 2)
            i = 0
            for n0 in range(0, ncols, chunk):
                n1 = min(ncols, n0 + chunk)
                s = stg[i % 2]
                i += 1
                k.dma("sp", s[:, :, 0:n1 - n0], src[:, n0:n1].rearrange("(c p) n -> p c n", p=128))
                for c in range(kc):
                    if gcol is not None:
                        k.act(dst[:, c, n0:n1], s[:, c, 0:n1 - n0], AF.Copy, scale=gcol[:, c:c + 1])
                    else:
                        k.copy("dve" if c % 2 else "act", dst[:, c, n0:n1], s[:, c, 0:n1 - n0])
            k.barrier()
            k.es = old

    @staticmethod
    def zip_gens(gens):
        gens = list(gens)
        while gens:
            for g in list(gens):
                try:
                    next(g)
                except StopIteration:
                    gens.remove(g)
            yield

    @staticmethod
    def run_pipeline(gens, depth=2):
        active = []
        it = iter(gens)
        exhausted = False
        while True:
            while len(active) < depth and not exhausted:
                try:
                    active.append(next(it))
                except StopIteration:
                    exhausted = True
            if not active:
                break
            for g in list(active):
                try:
                    next(g)
                except StopIteration:
                    active.remove(g)

    def rstd(self, out, ss, tmp, scale, eps):
        k = self.k
        k.act(tmp, ss, AF.Ln, scale=scale, bias=eps)
        k.act(out, tmp, AF.Exp, scale=-0.5)

    def pass_inproj(self, l, xsrc):
        k, W, S = self.k, self.W, self.S
        with self.phase():
            Wb = k.sb("Wb", [128, 8, DIN], BF16)
            wuq = k.sb("wuq", [128, 2, 768], BF16)
            wukv = k.sb("wukv", [128, 1, 1024], BF16)
            g1 = k.sb("g1", [128, 8], F32)
            gcq = k.sb("gcq", [128, 2], F32)
            gckv = k.sb("gckv", [128, 1], F32)
            gq = k.sb("gq", [128, 64], F32)
            gk = k.sb("gk", [128, 64], F32)
            mq = k.sb("mq", [128, 96], F32)
            mk = k.sb("mk", [128, 96], F32)
            bg = k.sb("bg", [128, 3072], F32)
            idf = k.sb("idf", [128, 128], F32)
            idb = k.sb("idb", [128, 128], BF16)
            k.dma("sp", g1[:, :], W["norm1_g"][l, :].rearrange("(c p) -> p c", p=128), slow=True)
            k.dma("sp", gcq[:, :], W["mla_cq_norm"][l, :].rearrange("(c p) -> p c", p=128), slow=True)
            k.dma("sp", gckv[:, :], W["mla_ckv_norm"][l, :].rearrange("(c p) -> p c", p=128), slow=True)
            self.load_bc(gq[:, :], W["na_q_norm"][l, :])
            self.load_bc(gk[:, :], W["na_k_norm"][l, :])
            self.load_bc(mq[:, :], W["mla_q_norm"][l, :])
            self.load_bc(mk[:, :], W["mla_k_norm"][l, :])
            self.load_bc(bg[:, :], W["b_gate"][l, :])
            k.dma("sp", idf[:, :], self.c_ident[:, :])
            k.copy("dve", idb[:, :], idf[:, :])
            self.load_w(Wb, W["w_in"][l], 8, DIN, gcol=g1)
            self.load_w(wuq, W["mla_w_uq"][l], 2, 768, gcol=gcq)
            self.load_w(wukv, W["mla_w_ukv"][l], 1, 1024, gcol=gckv)

            XT = k.rot("xt", [128, D], F32, 2)
            junk = k.rot("junk", [128, D], BF16, 2)
            hb = k.rot("hb", [128, D], BF16, 2)
            xT = k.rot("xT", [128, D], BF16, 2)
            st = k.rot("st", [128, 16], F32, 2)
            st2 = k.rot("st2", [128, 16], F32, 4)
            sq = k.rot("sq", [128, 768], F32, 2)
            qf = k.rot("qf", [128, 512], F32, 2)
            qn = k.rot("qn", [128, 512], BF16, 2)
            qTs = k.rot("qTs", [128, 512], BF16, 2)
            vb = k.rot("vb", [128, 512], BF16, 2)
            ml = k.rot("ml", [128, 416], F32, 2)
            cn = k.rot("cn", [128, 384], BF16, 2)
            cT = k.rot("cT", [128, 384], BF16, 2)
            qm = k.rot("qm", [128, 8, 96], F32, 4)
            rp = k.rot("rp", [128, 4, 8, 16], F32, 2)
            qmb = k.rot("qmb", [128, 8, 96], BF16, 2)
            qmT = k.rot("qmT", [96, 1024], BF16, 2)
            vmb = k.rot("vmb", [128, 8, 64], BF16, 2)
            cs = k.rot("cs", [128, 32], F32, 2)
            ev = [k.rot("ev%d" % i, [128, 512], F32, 2) for i in range(2)]
            evb = [k.rot("evb%d" % i, [128, 512], BF16, 2) for i in range(2)]
            pT = k.rot("pT", [128, 1024], BF16, 2, psum=True)
            pmP = [k.rot("pm%d" % i, [128, 512], F32, 2, psum=True) for i in range(2)]
            cnt = {"pm0": 0, "pm1": 0, "ev0": 0, "ev1": 0}

            def proj(xTt, n0, n1, par):
                p = pmP[par][cnt["pm%d" % par] % 2]
                cnt["pm%d" % par] += 1
                for c in range(8):
                    k.mm(p[:, 0:n1 - n0], xTt[:, c * 128:(c + 1) * 128], Wb[:, c, n0:n1], start=(c == 0), stop=(c == 7))
                return p

            def headnorm(src3, dst3, nh, hd, stt, col, eps, scale, sq_):
                k.act(sq_[:, 0:nh * hd].rearrange("p (h d) -> p h d", h=nh), src3, AF.Square)
                k.reduce("dve", stt[:, col:col + nh], sq_[:, 0:nh * hd].rearrange("p (h d) -> p h d", h=nh))
                yield
                self.rstd(stt[:, col:col + nh], stt[:, col:col + nh], stt[:, col + 8:col + 8 + nh], scale, eps)
                yield
                k.tt("dve", dst3, src3, stt[:, col:col + nh].unsqueeze(2).bc([128, nh, hd]), ALU.mult)

            def na_qk(p, gain, dst, stt, par, tok0, pT_):
                qf_ = qf[par]
                yield from headnorm(p[:, :].rearrange("p (h d) -> p h d", h=8), qf_[:, :].rearrange("p (h d) -> p h d", h=8),
                                    8, 64, stt, 0, 1e-6, 1.0 / 64, sq[par])
                yield
                q_ = qn[par]
                qs = qTs[par]
                k.tt("pool", q_[:, :].rearrange("p (h d) -> p h d", h=8), qf_[:, :].rearrange("p (h d) -> p h d", h=8),
                     gain[:, :].unsqueeze(1).bc([128, 8, 64]), ALU.mult)
                yield
                for j in range(4):
                    k.tr(pT_[:, j * 128:(j + 1) * 128], q_[:, j * 128:(j + 1) * 128], idb[:, :])
                k.copy("dve", qs[:, :], pT_[:, 0:512])
                k.dma("pool", dst[:, tok0:tok0 + 128].rearrange("(j p) t -> p j t", p=128), qs[:, :].rearrange("p (j t) -> p j t", j=4))

            def normrope(src, gain, dst, stt, col, csb, tok0, par, pT_):
                yield from headnorm(src[:, :, :], src[:, :, :], 8, 96, stt, col, 1e-6, 1.0 / 96, sq[par])
                yield
                k.tt("pool", src[:, :, :], src[:, :, :], gain[:, :].unsqueeze(1).bc([128, 8, 96]), ALU.mult)
                yield
                x1 = src[:, :, 64:80]
                x2 = src[:, :, 80:96]
                cc = csb[:, 0:16].unsqueeze(1).bc([128, 8, 16])
                ss_ = csb[:, 16:32].unsqueeze(1).bc([128, 8, 16])
                rp_ = rp[par]
                k.tt("dve", rp_[:, 0], x1, cc, ALU.mult)
                k.tt("pool", rp_[:, 1], x2, ss_, ALU.mult)
                k.tt("dve", rp_[:, 2], x1, ss_, ALU.mult)
                k.tt("pool", rp_[:, 3], x2, cc, ALU.mult)
                qb = qmb[par]
                k.copy("act", qb[:, :, 0:64], src[:, :, 0:64])
                yield
                k.tt("dve", qb[:, :, 64:80], rp_[:, 0], rp_[:, 1], ALU.subtract)
                k.tt("dve", qb[:, :, 80:96], rp_[:, 2], rp_[:, 3], ALU.add)
                yield
                for h in range(8):
                    k.tr(pT_[0:96, h * 128:(h + 1) * 128], qb[:, h, :], idb[:, :])
                qt = qmT[par]
                k.copy("dve", qt[:, :], pT_[0:96, :])
                k.dma("pool", dst[:, :, tok0:tok0 + 128].rearrange("h d t -> d h t"), qt[:, :].rearrange("d (h t) -> d h t", h=8))

            def tile(ti):
                tok0 = ti * 128
                par = ti % 2
                xt = XT[par]
                stt = st[par]
                csb = cs[par]
                pT_ = pT[par]
                junk_, hb_, ml_, cn_, cT_ = junk[par], hb[par], ml[par], cn[par], cT[par]
                k.dma("sp", xt[:, :], xsrc[tok0:tok0 + 128, :])
                k.dma("sp", csb[:, :], self.c_rope[tok0:tok0 + 128, :])
                k._memset("pool", stt[:, :], 0.0)
                k.act(junk_[:, :], xt[:, :], AF.Square, accum_out=stt[:, 0:1])
                yield
                self.rstd(stt[:, 1:2], stt[:, 0:1], stt[:, 2:3], 1.0 / D, 1e-6)
                yield
                k.act(hb_[:, :], xt[:, :], AF.Copy, scale=stt[:, 1:2])
                yield
                for c in range(8):
                    k.tr(pT_[:, c * 128:(c + 1) * 128], hb_[:, c * 128:(c + 1) * 128], idb[:, :])
                xTt = xT[par]
                k.copy("dve", xTt[:, :], pT_[:, :])
                yield
                p = proj(xTt, C_Q, C_Q + 512, par)
                yield
                yield from na_qk(p, gq, S["qT_na"], st2[2 * par], par, tok0, pT_)
                p = proj(xTt, C_K, C_K + 512, par)
                yield
                yield from na_qk(p, gk, S["kT_na"], st2[2 * par + 1], par, tok0, pT_)
                p = proj(xTt, C_V, C_V + 512, par)
                v_ = vb[par]
                yield
                k.copy("act", v_[:, :], p[:, :])
                k.dma("pool", S["v_na"][tok0:tok0 + 128, :], v_[:, :])
                p = proj(xTt, C_CQ, C_CQ + 416, par)
                yield
                k.copy("act", ml_[:, :], p[:, 0:416])
                k._memset("pool", stt[:, 4:6], 0.0)
                yield
                k.act(junk_[:, 0:256], ml_[:, 0:256], AF.Square, accum_out=stt[:, 4:5])
                k.act(junk_[:, 256:384], ml_[:, 256:384], AF.Square, accum_out=stt[:, 5:6])
                yield
                k.act(stt[:, 6:7], stt[:, 4:5], AF.Ln, scale=1.0 / 256, bias=1e-6)
                k.act(stt[:, 7:8], stt[:, 5:6], AF.Ln, scale=1.0 / 128, bias=1e-6)
                yield
                k.act(stt[:, 4:6], stt[:, 6:8], AF.Exp, scale=-0.5)
                yield
                k.act(cn_[:, 0:256], ml_[:, 0:256], AF.Copy, scale=stt[:, 4:5])
                k.act(cn_[:, 256:384], ml_[:, 256:384], AF.Copy, scale=stt[:, 5:6])
                yield
                for j in range(3):
                    k.tr(pT_[:, j * 128:(j + 1) * 128], cn_[:, j * 128:(j + 1) * 128], idb[:, :])
                k.copy("dve", cT_[:, :], pT_[:, 0:384])
                yield
                qm_ = qm[2 * par]
                km_ = qm[2 * par + 1]
                pq = pmP[par]
                for half in range(2):
                    for c in range(2):
                        k.mm(pq[half][:, 0:384], cT_[:, c * 128:(c + 1) * 128], wuq[:, c, half * 384:(half + 1) * 384],
                             start=(c == 0), stop=(c == 1))
                    k.copy("act", qm_[:, half * 4:(half + 1) * 4, :], pq[half][:, 0:384].rearrange("p (h d) -> p h d", h=4))
                yield
                pk = pmP[par]
                for half in range(2):
                    k.mm(pk[half][:, :], cT_[:, 256:384], wukv[:, 0, half * 512:(half + 1) * 512])
                    k.copy("dve", km_[:, half * 4:(half + 1) * 4, 0:64],
                           pk[half][:, :].rearrange("p (h d) -> p h d", h=4)[:, :, 0:64])
                    k.copy("dve", vmb[par][:, half * 4:(half + 1) * 4, :],
                           pk[half][:, :].rearrange("p (h d) -> p h d", h=4)[:, :, 64:128])
                k.copy("pool", km_[:, :, 64:96], ml_[:, 384:416].unsqueeze(1).bc([128, 8, 32]))
                k.dma("pool", S["v_m"][tok0:tok0 + 128, :], vmb[par][:, :, :].rearrange("p h d -> p (h d)"))
                yield
                chunks = [(n0, min(C_G, n0 + 512), "rw") for n0 in range(C_RW, C_G, 512)] + \
                         [(n0, n0 + 512, "g") for n0 in range(C_G, DIN, 512)]

                def dense():
                    for (n0, n1, kind) in chunks:
                        p = proj(xTt, n0, n1, par)
                        yield
                        e_ = ev[par][cnt["ev%d" % par] % 2]
                        eb_ = evb[par][cnt["ev%d" % par] % 2]
                        cnt["ev%d" % par] += 1
                        if kind == "rw":
                            k.copy("dve", e_[:, 0:n1 - n0], p[:, 0:n1 - n0])
                            k.dma("pool", S["rw_raw"][tok0:tok0 + 128, n0 - C_RW:n1 - C_RW], e_[:, 0:n1 - n0])
                        else:
                            k.tt("dve", e_[:, :], p[:, :], bg[:, n0 - C_G:n0 - C_G + 512], ALU.add)
                            yield
                            k.act(eb_[:, :], e_[:, :], AF.Sigmoid)
                            k.dma("pool", S["gates"][tok0:tok0 + 128, n0 - C_G:n0 - C_G + 512], eb_[:, :])
                        yield

                def mla_chain():
                    yield from normrope(qm_, mq, S["qT_m"], st2[2 * par], 0, csb, tok0, par, pT_)
                    yield
                    yield from normrope(km_, mk, S["kT_m"], st2[2 * par + 1], 0, csb, tok0, par, pT_)
                yield from self.zip_gens([dense(), mla_chain()])

            self.run_pipeline([tile(ti) for ti in range(self.nt)], depth=2)

    def pass_na(self, l):
        k, S = self.k, self.S
        with self.phase():
            Lmax = max(self.seqs)
            Rmax = Lmax // 64
            KT = k.rot("KT", [64, 2, Lmax], BF16, 1)
            QT = k.rot("QT", [64, 2, Lmax], BF16, 1)
            VE = k.rot("VE", [128, Rmax // 2, 2, 66], BF16, 2)
            VO = k.rot("VO", [128, Rmax // 2, 2, 66], BF16, 2)
            BT = k.rot("BT", [128, 8, 512], F32, 2)
            sbs = k.rot("sbs", [128, 512], F32, 3)
            pts = k.rot("pts", [128, 512], BF16, 3)
            rc = k.rot("rc", [64, 2, 1], F32, 2)
            ob = k.rot("ob", [64, 4, 2, 64], BF16, 2)
            pS = k.rot("pS", [128, 512], F32, 2, psum=True)
            pO = k.rot("pO", [128, 512], F32, 2, psum=True)
            it = 0
            u = 0
            for s, (t0, L) in enumerate(zip(self.seq_off, self.seqs)):
                R = L // 64
                for hp in range(4):
                    kt, qt, ve, vo, bt = KT[0], QT[0], VE[it % 2], VO[it % 2], BT[it % 2]
                    it += 1
                    for h2 in range(2):
                        hh = 2 * hp + h2
                        k.dma("sp", kt[:, h2, 0:L], S["kT_na"][hh * 64:(hh + 1) * 64, t0:t0 + L])
                        k.dma("sp", qt[:, h2, 0:L], S["qT_na"][hh * 64:(hh + 1) * 64, t0:t0 + L])
                    k.dma("sp", bt[:, :, :], self.c_nab[l, hp, :, :].rearrange("p (c n) -> p c n", c=8))
                    k._memset("pool", ve[:, :, :, :], 1.0)
                    k._memset("pool", vo[:, :, :, :], 1.0)
                    for h2 in range(2):
                        c0 = (2 * hp + h2) * 64
                        k.dma("sp", ve[:, 0:R // 2, h2, 0:64],
                              S["v_na"][t0:t0 + L, c0:c0 + 64].rearrange("(c p) d -> p c d", p=128))
                        k.dma("sp", vo[:, 0:R // 2 - 1, h2, 0:64],
                              S["v_na"][t0 + 64:t0 + L - 64, c0:c0 + 64].rearrange("(c p) d -> p c d", p=128))
                    for r in range(R):
                        rs = min(max(r - 4, 0), R - 8)
                        dcase = r - rs
                        ps_ = pS[u % 2]
                        po = pO[u % 2]
                        for h2 in range(2):
                            for kc in range(4):
                                col = (h2 * 4 + kc) * 64
                                k.mm(ps_[:, col:col + 64], kt[:, h2, rs * 64 + kc * 128:rs * 64 + (kc + 1) * 128],
                                     qt[:, h2, r * 64:(r + 1) * 64])
                        sb_ = sbs[u % 3]
                        k.stt("dve", sb_[:, :], ps_[:, :], 0.125, bt[:, dcase, :], ALU.mult, ALU.add)
                        pt = pts[u % 3]
                        k.act(pt[:, :], sb_[:, :], AF.Exp)
                        vbuf, cb = (ve, rs // 2) if rs % 2 == 0 else (vo, (rs - 1) // 2)
                        for h2 in range(2):
                            for kc in range(4):
                                col = (h2 * 4 + kc) * 64
                                k.mm(po[0:64, h2 * 128:h2 * 128 + 66], pt[:, col:col + 64], vbuf[:, cb + kc, h2, :],
                                     start=(kc == 0), stop=(kc == 3))
                        po3 = po[0:64, 0:256].rearrange("q (h d) -> q h d", h=2)
                        rc_ = rc[u % 2]
                        k.op("dve", "reciprocal", out=rc_[:, :, :], in_=po3[:, :, 64:65])
                        ob_ = ob[(r // 4) % 2]
                        k.tt("dve", ob_[:, r % 4, :, :], po3[:, :, 0:64], rc_[:, :, :].bc([64, 2, 64]), ALU.mult)
                        u += 1
                        if r % 4 == 3:
                            r0 = r - 3
                            k.dma("pool", S["na_out"][t0 + r0 * 64:t0 + r0 * 64 + 256, hp * 128:(hp + 1) * 128].rearrange("(r q) c -> q r c", q=64),
                                  ob_[:, :, :, :].rearrange("q r h d -> q r (h d)"))

    def pass_mla(self, l):
        k, S = self.k, self.S
        with self.phase():
            Lmax = max(self.seqs)
            QT = k.rot("QT", [96, Lmax], BF16, 2)
            KT = k.rot("KT", [96, Lmax], BF16, 2)
            VA = k.rot("VA", [128, Lmax // 128, 128], BF16, 2)
            pts = k.rot("pts", [128, 512], BF16, 4)
            rcs = k.rot("rcs", [128, 512], F32, 2)
            ots = k.rot("ots", [64, 512], BF16, 2)
            pS = k.rot("pS", [128, 512], F32, 3, psum=True)
            pO = k.rot("pO", [128, 512], F32, 2, psum=True)
            scale = 96 ** -0.5
            it = 0
            u = 0
            g = 0
            for s, (t0, L) in enumerate(zip(self.seq_off, self.seqs)):
                nkt = L // 128
                for h in range(8):
                    qt, kt, va = QT[it % 2], KT[it % 2], VA[it % 2]
                    it += 1
                    k.dma("sp", qt[:, 0:L], S["qT_m"][h, :, t0:t0 + L])
                    k.dma("sp", kt[:, 0:L], S["kT_m"][h, :, t0:t0 + L])
                    k._memset("pool", va[:, :, 64:128], 1.0)
                    k.dma("sp", va[:, 0:nkt, 0:64], S["v_m"][t0:t0 + L, h * 64:(h + 1) * 64].rearrange("(c p) d -> p c d", p=128))
                    for qc in range(L // 512):
                        po = pO[g % 2]
                        q_ = qt[:, qc * 512:(qc + 1) * 512]
                        LOOK = 2
                        pend = []
                        for j in range(nkt + LOOK):
                            if j < nkt:
                                ps_ = pS[u % 3]
                                u += 1
                                k.mm(ps_[:, :], kt[:, j * 128:(j + 1) * 128], q_)
                                pend.append(ps_)
                            if j >= LOOK:
                                jj = j - LOOK
                                ps_ = pend[jj]
                                pt = pts[jj % 4]
                                k.act(pt[:, :], ps_[:, :], AF.Exp, scale=scale)
                                k.mm(po[:, :], va[:, jj, :], pt[:, :], start=(jj == 0), stop=(jj == nkt - 1))
                        rc_ = rcs[g % 2]
                        k.op("dve", "reciprocal", out=rc_[64:128, :], in_=po[64:128, :])
                        ot = ots[g % 2]
                        k.tt("dve", ot[:, :], po[0:64, :], rc_[64:128, :], ALU.mult)
                        k.dma("pool", S["oT_b"][h * 64:(h + 1) * 64, t0 + qc * 512:t0 + (qc + 1) * 512], ot[:, :])
                        g += 1

    def pass_rwprep(self, l):
        k, W, S = self.k, self.W, self.S
        with self.phase():
            mu = k.sb("mu", [128, 1920], F32)
            kkb = k.sb("kkb", [128, 512], F32)
            kab = k.sb("kab", [128, 512], F32)
            rkb = k.sb("rkb", [128, 512], F32)
            w0b = k.sb("w0b", [128, 2, 512], F32)
            a0b = k.sb("a0b", [128, 2, 512], F32)
            wup = k.sb("wup", [64, 2, 512], F32)
            aup = k.sb("aup", [64, 2, 512], F32)
            gup = k.sb("gup", [128, 512], F32)
            idf = k.sb("idf", [128, 128], F32)
            self.load_bc(mu[:, :], W["rw_mu"][l, :])
            self.load_bc(kkb[:, :], W["rw_k_k"][l, :])
            self.load_bc(kab[:, :], W["rw_k_a"][l, :])
            self.load_bc(rkb[:, :], W["rw_r_k"][l, :, :].rearrange("h d -> (h d)"))
            for d_ in range(2):
                self.load_bc(w0b[:, d_, :], W["rw_w0"][l, d_, :])
                self.load_bc(a0b[:, d_, :], W["rw_a0"][l, d_, :])
            k.dma("sp", wup[:, :, :], W["rw_w_up"][l, :, :, :].rearrange("d r c -> r d c"))
            k.dma("sp", aup[:, :, :], W["rw_a_up"][l, :, :, :].rearrange("d r c -> r d c"))
            k.dma("sp", gup[:, :], W["rw_g_up"][l, :, :])
            k.dma("sp", idf[:, :], self.c_ident[:, :])
            cur = k.rot("cur", [128, 1920], F32, 2)
            prv = k.rot("prv", [128, 1920], F32, 2)
            nxt = k.rot("nxt", [128, 1920], F32, 2)
            pp_ = k.rot("pp", [128, 1920], F32, 2)
            dd_ = k.rot("dd", [128, 1920], F32, 2)
            O = k.rot("O", [128, RWC], F32, 2)
            t1_ = k.rot("t1", [128, 512], F32, 2)
            t2_ = k.rot("t2", [128, 512], F32, 2)
            t3_ = k.rot("t3", [128, 512], F32, 2)
            a__ = k.rot("a_", [128, 512], F32, 2)
            st = k.rot("st", [128, 24], F32, 2)
            sm_ = k.rot("sm", [128, 384], F32, 2)
            tT_ = k.rot("tT", [64, 4, 128], F32, 2)
            gT_ = k.rot("gT", [128, 128], F32, 2)
            pT_ = k.rot("pT", [128, 512], F32, 2, psum=True)
            pT2_ = k.rot("pT2", [128, 512], F32, 2, psum=True)
            pw_ = [k.rot("pw%d" % i, [128, 512], F32, 2, psum=True) for i in range(2)]
            seq_starts = set(self.seq_off)
            seq_ends = set(o + L for o, L in zip(self.seq_off, self.seqs))

            def col(o, i):
                return o[:, i * 512:(i + 1) * 512]

            def tile(ti):
                tok0 = ti * 128
                par = ti % 2
                c_, p_, n_, o = cur[par], prv[par], nxt[par], O[par]
                pp, dd, t1, t2, t3, a_ = pp_[par], dd_[par], t1_[par], t2_[par], t3_[par], a__[par]
                sm, tT, gT, pT, pT2, pw = sm_[par], tT_[par], gT_[par], pT_[par], pT2_[par], pw_[par]
                stt = st[par]
                npw = 0
                k.dma("sp", c_[:, :], S["rw_raw"][tok0:tok0 + 128, :])
                if tok0 in seq_starts:
                    k._memset("pool", p_[0:32, :], 0.0)
                    k.dma("sp", p_[1:128, :], S["rw_raw"][tok0:tok0 + 127, :])
                else:
                    k.dma("sp", p_[:, :], S["rw_raw"][tok0 - 1:tok0 + 127, :])
                if tok0 + 128 in seq_ends:
                    k._memset("pool", n_[96:128, :], 0.0)
                    k.dma("sp", n_[0:127, :], S["rw_raw"][tok0 + 1:tok0 + 128, :])
                else:
                    k.dma("sp", n_[:, :], S["rw_raw"][tok0 + 1:tok0 + 129, :])
                yield
                k.tt("pool", dd[:, :], p_[:, :], n_[:, :], ALU.add)
                yield
                k.stt("dve", dd[:, :], dd[:, :], 0.5, c_[:, :], ALU.mult, ALU.subtract)
                yield
                k.tt("pool", dd[:, :], dd[:, :], mu[:, :], ALU.mult)
                yield
                k.tt("dve", pp[:, :], dd[:, :], c_[:, :], ALU.add)
                yield
                rr, kk_, vv = pp[:, 0:512], pp[:, 512:1024], pp[:, 1024:1536]
                k.copy("act", col(o, 0), rr)
                k.copy("act", col(o, 2), vv)
                k.tt("pool", t1[:, :], kk_, kkb[:, :], ALU.mult)
                k.act(sm[:, 0:128], pp[:, 1536:1664], AF.Tanh)
                k.copy("dve", sm[:, 128:256], pp[:, 1664:1792])
                k.act(sm[:, 256:384], pp[:, 1792:1920], AF.Sigmoid)
                yield
                k.act(t2[:, :], t1[:, :], AF.Square)
                for j in range(4):
                    k.tr(pT[0:64, j * 128:(j + 1) * 128], sm[:, j * 64:(j + 1) * 64], idf[:, :])
                k.tr(pT2[:, 0:128], sm[:, 256:384], idf[:, :])
                yield
                k.reduce("dve", stt[:, 0:8], t2[:, :].rearrange("p (h d) -> p h d", h=8))
                k.copy("dve", tT[:, :, :], pT[0:64, :].rearrange("p (j t) -> p j t", j=4))
                k.copy("dve", gT[:, :], pT2[:, 0:128])
                yield
                k.act(stt[:, 8:16], stt[:, 0:8], AF.Ln, scale=1.0, bias=1e-12)
                for d_ in range(2):
                    k.mm(pw[d_][:, :], tT[:, d_, :], wup[:, d_, :])
                yield
                k.act(stt[:, 0:8], stt[:, 8:16], AF.Exp, scale=-0.5)
                for d_ in range(2):
                    k.tt("dve", (t2 if d_ == 0 else t3)[:, :], pw[d_][:, :], w0b[:, d_, :], ALU.add)
                yield
                k.tt("dve", col(o, 1).rearrange("p (h d) -> p h d", h=8), t1[:, :].rearrange("p (h d) -> p h d", h=8),
                     stt[:, 0:8].unsqueeze(2).bc([128, 8, 64]), ALU.mult)
                k.act(t2[:, :], t2[:, :], AF.Sigmoid)
                k.act(t3[:, :], t3[:, :], AF.Sigmoid)
                for d_ in range(2):
                    k.mm(pw[d_][:, :], tT[:, 2 + d_, :], aup[:, d_, :])
                yield
                k.ts("pool", col(o, 4), t2[:, :], -0.6065306597126334, None, ALU.mult)
                k.ts("pool", col(o, 7), t3[:, :], -0.6065306597126334, None, ALU.mult)
                k.tt("dve", a_[:, :], pw[0][:, :], a0b[:, 0, :], ALU.add)
                k.tt("dve", t1[:, :], pw[1][:, :], a0b[:, 1, :], ALU.add)
                yield
                k.act(a_[:, :], a_[:, :], AF.Sigmoid)
                k.act(t1[:, :], t1[:, :], AF.Sigmoid)
                k.mm(pw[0][:, :], gT[:, :], gup[:, :])
                yield
                for d_, av in ((0, a_), (1, t1)):
                    k.tt("pool", col(o, 6 + 3 * d_), av[:, :], col(o, 1), ALU.mult)
                    k.stt("dve", (t2 if d_ == 0 else t3)[:, :], av[:, :], -1.0, kab[:, :], ALU.add, ALU.mult)
                k.copy("act", col(o, 3), pw[0][:, :])
                yield
                k.stt("dve", col(o, 5), t2[:, :], 1.0, kk_, ALU.add, ALU.mult)
                k.stt("dve", col(o, 8), t3[:, :], 1.0, kk_, ALU.add, ALU.mult)
                yield
                k.tt("pool", t2[:, :], col(o, 5), col(o, 8), ALU.add)
                yield
                k.tt("dve", t2[:, :], t2[:, :], rr, ALU.mult)
                yield
                k.tt("pool", t2[:, :], t2[:, :], rkb[:, :], ALU.mult)
                yield
                k.reduce("dve", o[:, 5120:5128], t2[:, :].rearrange("p (h d) -> p h d", h=8))
                k.dma("pool", S["rwpA"][tok0:tok0 + 128, :], o[:, 0:2048])
                k.dma("pool", S["rwpB"][tok0:tok0 + 128, :], o[:, 2048:RWC])

            self.run_pipeline([tile(ti) for ti in range(self.nt)], depth=2)

    def pass_rwscan(self, l):
        k, S = self.k, self.S
        with self.phase():
            idf = k.sb("idf", [128, 128], F32)
            idb = k.sb("idb", [128, 128], BF16)
            eblk = k.sb("eblk", [128, 512], F32)
            k.dma("sp", idf[:, :], self.c_ident[:, :])
            k.dma("sp", eblk[:, :], self.c_eblk[:, :])
            k.copy("dve", idb[:, :], idf[:, :])

            def mkbufs(dr):
                B = {}
                B["M4"] = k.sb("M4", [128, 4, 128], F32)
                B["MI"] = k.sb("MI", [128, 128], F32)
                B["BLK"] = k.sb("BLK", [128, 128], F32)
                B["A"] = k.sb("A", [128, 1536], F32)
                B["Bd"] = k.sb("Bd", [128, 1536], F32)
                for nm in ("cumS", "tmp", "e1", "e2", "e3", "e4", "e5"):
                    B[nm] = k.sb(nm, [128, 512], F32)
                for nm in ("Abar", "Rbar", "Kt", "Bt", "Yd", "Vb"):
                    B[nm] = k.sb(nm, [128, 512], BF16)
                B["Kcs"] = [k.sb("Kc%d" % i, [128, 512], BF16) for i in range(2)]
                B["Bcs"] = [k.sb("Bc%d" % i, [128, 512], BF16) for i in range(2)]
                B["e4c"] = [k.sb("e4c%d" % i, [128, 512], F32) for i in range(2)]
                B["XTs"] = [k.sb("XT%d" % i, [64, 8, 128], BF16) for i in range(4)]
                B["PM"] = k.sb("PM", [128, 8, 4, 128], BF16)
                B["MRK"] = k.sb("MRK", [128, 8, 128], BF16)
                B["Xs"] = k.rot("Xs", [128, 8, 128], BF16, 2)
                B["XsT"] = k.rot("XsT", [128, 8, 128], BF16, 2)
                B["Z"] = k.rot("Z", [128, 8, 128], BF16, 2)
                B["NZ"] = k.sb("NZ", [128, 8, 128], BF16)
                B["RTa"] = k.sb("RTa", [64, 8, 128], F32)
                B["RTb"] = k.sb("RTb", [64, 8, 128], F32)
                B["Y0"] = k.sb("Y0", [128, 512], F32)
                B["GT"] = k.sb("GT", [64, 2, 8, 64], F32)
                B["HS"] = k.sb("HS", [64, 2, 8, 64], F32)
                B["ST"] = k.rot("ST", [64, 8, 64], F32, 2)
                B["yo"] = k.sb("yo", [128, 512], F32)
                B["pA"] = k.rot("pA", [128, 512], F32, 4, psum=True)
                B["npa"] = 0
                return B

            def sweep(dr, B):
                M4, MI, BLK = B["M4"], B["MI"], B["BLK"]
                cumS, tmp, e1, e2, e3, e4, e5 = (B[n] for n in ("cumS", "tmp", "e1", "e2", "e3", "e4", "e5"))
                Abar, Rbar, Kt, Bt, Yd, Vb = (B[n] for n in ("Abar", "Rbar", "Kt", "Bt", "Yd", "Vb"))
                Kcs, Bcs, e4c, XTs, PM, MRK = B["Kcs"], B["Bcs"], B["e4c"], B["XTs"], B["PM"], B["MRK"]
                Xs, XsT, Z, NZ, RTa, RTb = B["Xs"], B["XsT"], B["Z"], B["NZ"], B["RTa"], B["RTb"]
                Y0, GT, HS, ST, yo_ = B["Y0"], B["GT"], B["HS"], B["ST"], B["yo"]

                def bank():
                    b = B["pA"][B["npa"] % 4]
                    B["npa"] += 1
                    return b
                k._memset("pool", RTa[:, :, :], 0.0)
                k._memset("pool", RTb[:, :, :], 0.0)
                k.dma("sp", M4[:, 0, :], self.c_rwm[dr, 0])
                k.dma("sp", M4[:, 1, :], self.c_rwm[dr, 1])
                k.dma("sp", M4[:, 2, :], self.c_rwm[dr, 0])
                k.dma("sp", M4[:, 3, :], self.c_rwm[dr, 2])
                k.dma("sp", MI[:, :], self.c_rwm[dr, 2])
                k.dma("sp", BLK[:, :], self.c_rwm[dr, 3])
                ydst = S["yd%d" % dr]
                sti = 0
                for s, (t0, L) in enumerate(zip(self.seq_off, self.seqs)):
                    ntl = L // 128
                    tiles = range(ntl) if dr == 0 else range(ntl - 1, -1, -1)
                    st_cur = ST[sti % 2]
                    k._memset("pool", st_cur[:, :, :], 0.0)
                    for tl in tiles:
                        tok0 = t0 + tl * 128
                        a, b = B["A"], B["Bd"]
                        k.dma("sp", a[:, :], S["rwpA"][tok0:tok0 + 128, 0:1536])
                        k.dma("sp", b[:, :], S["rwpB"][tok0:tok0 + 128, dr * 1536:(dr + 1) * 1536])
                        Rr, KK, Vv = a[:, 0:512], a[:, 512:1024], a[:, 1024:1536]
                        LW, KD, BE = b[:, 0:512], b[:, 512:1024], b[:, 1024:1536]
                        yield
                        pc = bank()
                        pcc = bank()
                        k.mm(pc[:, :], MI[:, :], LW)
                        k.mm(pcc[:, :], BLK[:, :], LW)
                        k.copy("act", Vb[:, :], Vv)
                        yield
                        k.copy("act", cumS[:, :], pc[:, :])
                        k.act(e1[:, :], pc[:, :], AF.Exp)
                        k.act(e3[:, :], pc[:, :], AF.Exp, scale=-1.0)
                        k.copy("dve", e5[:, :], pcc[:, :])
                        yield
                        k.tt("pool", tmp[:, :], cumS[:, :], LW, ALU.subtract)
                        k.tt("dve", e4[:, :], e5[:, :], cumS[:, :], ALU.subtract)
                        k.tt("dve", Rbar[:, :], Rr, e1[:, :], ALU.mult)
                        yield
                        k.act(e2[:, :], tmp[:, :], AF.Exp)
                        k.act(e4[:, :], e4[:, :], AF.Exp)
                        k.act(e5[:, :], e5[:, :], AF.Exp)
                        k.tt("pool", Kt[:, :], KD, e3[:, :], ALU.mult)
                        k.tt("dve", Bt[:, :], BE, e3[:, :], ALU.mult)
                        yield
                        k.tt("pool", Abar[:, :], KK, e2[:, :], ALU.mult)
                        for c in range(2):
                            k.ts("dve", e4c[c][:, :], e4[:, :], BLK[:, c * 127:c * 127 + 1], None, ALU.mult)
                        yield
                        for c in range(2):
                            k.tt("pool", Kcs[c][:, :], KD, e4c[c][:, :], ALU.mult)
                            k.tt("dve", Bcs[c][:, :], BE, e4c[c][:, :], ALU.mult)
                        k.tt("pool", Yd[:, :], eblk[:, :], e5[:, :], ALU.mult)
                        for i, X in enumerate((Abar, Bt, Kt, Rbar)):
                            for half in range(2):
                                p = bank()
                                pb = p[:, :].bitcast(BF16)
                                for hh in range(4):
                                    h = half * 4 + hh
                                    k.tr(pb[0:64, hh * 128:(hh + 1) * 128], X[:, h * 64:(h + 1) * 64], idb[:, :])
                                k.copy("act" if half else "dve", XTs[i][:, half * 4:(half + 1) * 4, :],
                                       pb[0:64, 0:512].rearrange("p (j t) -> p j t", j=4))
                            yield
                        aT, bT, kT_, rT = XTs

                        def hv(X, h):
                            return X[:, h, :]
                        for h in range(8):
                            p = bank()
                            k.mm(p[:, 0:128], hv(bT, h), hv(aT, h))
                            k.mm(p[:, 128:256], hv(aT, h), hv(bT, h))
                            k.mm(p[:, 256:384], hv(kT_, h), hv(aT, h))
                            k.mm(p[:, 384:512], hv(bT, h), hv(rT, h))
                            k.tt("dve", PM[:, h, :, :], p[:, :].rearrange("p (j t) -> p j t", j=4), M4[:, :, :], ALU.mult)
                            if h % 2 == 1:
                                yield
                        for half in range(2):
                            p = bank()
                            for hh in range(4):
                                h = half * 4 + hh
                                k.mm(p[:, hh * 128:(hh + 1) * 128], hv(kT_, h), hv(rT, h))
                            k.tt("dve", MRK[:, half * 4:(half + 1) * 4, :], p[:, :].rearrange("p (j t) -> p j t", j=4),
                                 MI[:, :].unsqueeze(1).bc([128, 4, 128]), ALU.mult)
                        z = Z[0]
                        zi = 0
                        p = bank()
                        for h in range(8):
                            k.mm(p[:, h * 64:(h + 1) * 64], PM[:, h, 2, :], Vb[:, h * 64:(h + 1) * 64])
                        k.copy("act", z[:, :, 0:64], Abar[:, :].rearrange("p (h d) -> p h d", h=8))
                        k.copy("dve", z[:, :, 64:128], p[:, :].rearrange("p (h d) -> p h d", h=8))
                        yield
                        curX = PM[:, :, 1, :]
                        curXT = PM[:, :, 0, :]
                        for lev in range(6):
                            znew = Z[(zi + 1) % 2]
                            for half in range(2):
                                p = bank()
                                for hh in range(4):
                                    h = half * 4 + hh
                                    k.mm(p[:, hh * 128:(hh + 1) * 128], curXT[:, h, :], z[:, h, :])
                                k.tt("dve", znew[:, half * 4:(half + 1) * 4, :], z[:, half * 4:(half + 1) * 4, :],
                                     p[:, :].rearrange("p (j t) -> p j t", j=4), ALU.subtract if lev == 0 else ALU.add)
                            z = znew
                            zi += 1
                            if lev == 5:
                                break
                            nX, nXT = Xs[lev % 2], XsT[lev % 2]
                            for half in range(2):
                                p = bank()
                                for hh in range(4):
                                    h = half * 4 + hh
                                    k.mm(p[:, hh * 128:(hh + 1) * 128], curX[:, h, :], curXT[:, h, :])
                                k.copy("act", nXT[:, half * 4:(half + 1) * 4, :], p[:, :].rearrange("p (j t) -> p j t", j=4))
                                if lev < 4:
                                    p = bank()
                                    for hh in range(4):
                                        h = half * 4 + hh
                                        k.mm(p[:, hh * 128:(hh + 1) * 128], curXT[:, h, :], curX[:, h, :])
                                    k.copy("act", nX[:, half * 4:(half + 1) * 4, :], p[:, :].rearrange("p (j t) -> p j t", j=4))
                            curX = nX[:, :, :]
                            curXT = nXT[:, :, :]
                            yield
                        k.ts("pool", NZ[:, :, :], z[:, :, :], -1.0, None, ALU.mult)
                        yield
                        for half in range(2):
                            p = bank()
                            for hh in range(4):
                                h = half * 4 + hh
                                k.mm(p[0:64, hh * 128:(hh + 1) * 128], Rbar[:, h * 64:(h + 1) * 64], idb[:, :], start=True, stop=False)
                                k.mm(p[0:64, hh * 128:(hh + 1) * 128], NZ[:, h, 0:64], PM[:, h, 3, :], start=False, stop=True)
                            p3 = p[0:64, :].rearrange("p (j t) -> p j t", j=4)
                            k.copy("dve", RTa[:, half * 4:(half + 1) * 4, 0:64], p3[:, :, 0:64])
                            k.copy("dve", RTb[:, half * 4:(half + 1) * 4, 64:128], p3[:, :, 64:128])
                        yield
                        p = bank()
                        for h in range(8):
                            k.mm(p[:, h * 64:(h + 1) * 64], MRK[:, h, :], Vb[:, h * 64:(h + 1) * 64], start=True, stop=False)
                            k.mm(p[:, h * 64:(h + 1) * 64], PM[:, h, 3, :], NZ[:, h, 64:128], start=False, stop=True)
                        k.copy("act", Y0[:, :], p[:, :])
                        yield
                        for c in range(2):
                            cs_ = slice(c * 64, (c + 1) * 64)
                            p = bank()
                            p2 = bank()
                            for h in range(8):
                                hs_ = slice(h * 64, (h + 1) * 64)
                                k.mm(p[0:64, hs_], NZ[:, h, 0:64], Bcs[c][:, hs_], start=True, stop=False)
                                k.mm(p[0:64, hs_], idb[:, cs_], Yd[:, hs_], start=False, stop=True)
                                k.mm(p2[0:64, hs_], Kcs[c][:, hs_], Vb[:, hs_], start=True, stop=False)
                                k.mm(p2[0:64, hs_], Bcs[c][:, hs_], NZ[:, h, 64:128], start=False, stop=True)
                            k.copy("act", GT[:, c, :, :], p[0:64, :].rearrange("p (h d) -> p h d", h=8))
                            k.copy("dve", HS[:, c, :, :], p2[0:64, :].rearrange("p (h d) -> p h d", h=8))
                            yield
                        order = (0, 1) if dr == 0 else (1, 0)
                        pYs = [bank(), bank()]
                        for ci, c in enumerate(order):
                            RTx = RTa if c == 0 else RTb
                            for h in range(8):
                                k.mm(pYs[ci][:, h * 64:(h + 1) * 64], RTx[:, h, :], st_cur[:, h, :])
                            p = bank()
                            for h in range(8):
                                k.mm(p[0:64, h * 64:(h + 1) * 64], GT[:, c, h, :], st_cur[:, h, :])
                            sti += 1
                            st_new = ST[sti % 2]
                            k.tt("dve", st_new[:, :, :], p[0:64, :].rearrange("p (h d) -> p h d", h=8), HS[:, c, :, :], ALU.add)
                            st_cur = st_new
                            yield
                        k.tt("dve", yo_[:, :], pYs[0][:, :], Y0[:, :], ALU.add)
                        k.tt("dve", yo_[:, :], pYs[1][:, :], yo_[:, :], ALU.add)
                        k.dma("pool", ydst[tok0:tok0 + 128, :], yo_[:, :])
                        yield

            B0 = mkbufs(0)
            B1 = mkbufs(1)
            for _ in self.zip_gens([sweep(0, B0), sweep(1, B1)]):
                pass

    def pass_rwfin(self, l):
        k, W, S = self.k, self.W, self.S
        with self.phase():
            lnw = k.sb("lnw", [128, 512], F32)
            lnb = k.sb("lnb", [128, 512], F32)
            self.load_bc(lnw[:, :], W["rw_ln_w"][l, :])
            self.load_bc(lnb[:, :], W["rw_ln_b"][l, :])
            yf = k.rot("yf", [128, 512], F32, 2)
            yb = k.rot("yb", [128, 512], F32, 2)
            vg = k.rot("vg", [128, 1024], F32, 2)
            bo = k.rot("bo", [128, 8], F32, 2)
            y = k.sb("y", [128, 512], F32)
            t = k.sb("t", [128, 512], F32)
            st = k.rot("st", [128, 32], F32, 2)
            ob = k.rot("ob", [128, 512], BF16, 2)

            def h3(v):
                return v.rearrange("p (h d) -> p h d", h=8)
            for ti in range(self.nt):
                tok0 = ti * 128
                a, b, vg_, bo_, stt = yf[ti % 2], yb[ti % 2], vg[ti % 2], bo[ti % 2], st[ti % 2]
                k.dma("sp", a[:, :], S["yd0"][tok0:tok0 + 128, :])
                k.dma("sp", b[:, :], S["yd1"][tok0:tok0 + 128, :])
                k.dma("sp", vg_[:, :], S["rwpA"][tok0:tok0 + 128, 1024:2048])
                k.dma("sp", bo_[:, :], S["rwpB"][tok0:tok0 + 128, 3072:3080])
                k.tt("pool", y[:, :], a[:, :], b[:, :], ALU.add)
                k.reduce("dve", stt[:, 0:8], h3(y[:, :]))
                k.ts("dve", stt[:, 0:8], stt[:, 0:8], 1.0 / 64, None, ALU.mult)
                k.tt("dve", h3(y[:, :]), h3(y[:, :]), stt[:, 0:8].unsqueeze(2).bc([128, 8, 64]), ALU.subtract)
                k.act(t[:, :], y[:, :], AF.Square)
                k.reduce("dve", stt[:, 8:16], h3(t[:, :]))
                self.rstd(stt[:, 8:16], stt[:, 8:16], stt[:, 16:24], 1.0 / 64, 64e-5)
                k.tt("dve", h3(y[:, :]), h3(y[:, :]), stt[:, 8:16].unsqueeze(2).bc([128, 8, 64]), ALU.mult)
                k.tt("pool", y[:, :], y[:, :], lnw[:, :], ALU.mult)
                k.tt("pool", y[:, :], y[:, :], lnb[:, :], ALU.add)
                k.tt("dve", h3(t[:, :]), h3(vg_[:, 0:512]), bo_[:, :].unsqueeze(2).bc([128, 8, 64]), ALU.mult)
                k.tt("pool", y[:, :], y[:, :], t[:, :], ALU.add)
                k.tt("dve", ob[ti % 2][:, :], y[:, :], vg_[:, 512:1024], ALU.mult)
                k.dma("pool", S["yc"][tok0:tok0 + 128, :], ob[ti % 2][:, :])

    def pass_merge(self, l, xsrc):
        k, W, S = self.k, self.W, self.S
        with self.phase():
            idf = k.sb("idf", [128, 128], F32)
            idb = k.sb("idb", [128, 128], BF16)
            k.dma("sp", idf[:, :], self.c_ident[:, :])
            k.copy("dve", idb[:, :], idf[:, :])
            wa = k.sb("wa", [128, 4, D], BF16)
            wb = k.sb("wb", [128, 4, D], BF16)
            wc = k.sb("wc", [128, 4, D], BF16)
            wo = k.sb("wo", [128, 8, D], BF16)
            self.load_w(wa, W["na_proj"][l], 4, D)
            self.load_w(wb, W["mla_proj"][l], 4, D)
            self.load_w(wc, W["rw_proj"][l], 4, D)
            self.load_w(wo, W["w_out"][l], 8, D)
            xa = k.rot("xa", [128, 512], BF16, 2)
            xc = k.rot("xc", [128, 512], BF16, 2)
            oTb = k.rot("oTb", [128, 4, 128], BF16, 2)
            gt = k.rot("gt", [128, 3072], BF16, 2)
            xt = k.rot("xt", [128, D], F32, 2)
            aT = k.sb("aT", [128, 4, 128], BF16)
            cT = k.sb("cT", [128, 4, 128], BF16)
            mixed = k.sb("mixed", [128, D], F32)
            tmp = k.rot("tmp", [128, 512], F32, 2)
            mixb = k.sb("mixb", [128, D], BF16)
            mT = k.sb("mT", [128, 8, 128], BF16)
            xo = k.rot("xo", [128, D], F32, 2)
            pT = k.ps("pT", [128, 1024], BF16)
            pm = k.rot("pm", [128, 512], F32, 4, psum=True)
            npm = 0
            for ti in range(self.nt):
                tok0 = ti * 128
                i2 = ti % 2
                k.dma("sp", xa[i2][:, :], S["na_out"][tok0:tok0 + 128, :])
                k.dma("sp", xc[i2][:, :], S["yc"][tok0:tok0 + 128, :])
                k.dma("sp", oTb[i2][:, :, :], S["oT_b"][:, tok0:tok0 + 128].rearrange("(j p) t -> p j t", p=128))
                k.dma("sp", gt[i2][:, :], S["gates"][tok0:tok0 + 128, :])
                k.dma("sp", xt[i2][:, :], xsrc[tok0:tok0 + 128, :])
                for j in range(4):
                    k.tr(pT[:, j * 128:(j + 1) * 128], xa[i2][:, j * 128:(j + 1) * 128], idb[:, :])
                    k.tr(pT[:, 512 + j * 128:512 + (j + 1) * 128], xc[i2][:, j * 128:(j + 1) * 128], idb[:, :])
                k.copy("dve", aT[:, :, :], pT[:, 0:512].rearrange("p (j t) -> p j t", j=4))
                k.copy("dve", cT[:, :, :], pT[:, 512:1024].rearrange("p (j t) -> p j t", j=4))
                for bi, (T_, W_) in enumerate(((aT, wa), (oTb[i2], wb), (cT, wc))):
                    for nch in range(2):
                        p = pm[npm % 4]
                        npm += 1
                        for c in range(4):
                            k.mm(p[:, :], T_[:, c, :], W_[:, c, nch * 512:(nch + 1) * 512], start=(c == 0), stop=(c == 3))
                        gsl = gt[i2][:, bi * D + nch * 512:bi * D + (nch + 1) * 512]
                        if bi == 0:
                            k.tt("dve", mixed[:, nch * 512:(nch + 1) * 512], p[:, :], gsl, ALU.mult)
                        else:
                            t_ = tmp[npm % 2]
                            k.tt("dve", t_[:, :], p[:, :], gsl, ALU.mult)
                            k.tt("pool", mixed[:, nch * 512:(nch + 1) * 512], mixed[:, nch * 512:(nch + 1) * 512], t_[:, :], ALU.add)
                k.copy("act", mixb[:, :], mixed[:, :])
                for c in range(8):
                    k.tr(pT[:, c * 128:(c + 1) * 128], mixb[:, c * 128:(c + 1) * 128], idb[:, :])
                k.copy("dve", mT[:, :, :], pT[:, :].rearrange("p (j t) -> p j t", j=8))
                for nch in range(2):
                    p = pm[npm % 4]
                    npm += 1
                    for c in range(8):
                        k.mm(p[:, :], mT[:, c, :], wo[:, c, nch * 512:(nch + 1) * 512], start=(c == 0), stop=(c == 7))
                    k.tt("dve", xo[i2][:, nch * 512:(nch + 1) * 512], p[:, :], xt[i2][:, nch * 512:(nch + 1) * 512], ALU.add)
                k.dma("pool", S["xmid"][tok0:tok0 + 128, :], xo[i2][:, :])

    def pass_ffn(self, l, xdst):
        k, W, S = self.k, self.W, self.S
        NF = DFF // 128
        with self.phase():
            idf = k.sb("idf", [128, 128], F32)
            idb = k.sb("idb", [128, 128], BF16)
            g2 = k.sb("g2", [128, 8], F32)
            k.dma("sp", idf[:, :], self.c_ident[:, :])
            k.copy("dve", idb[:, :], idf[:, :])
            k.dma("sp", g2[:, :], W["norm2_g"][l, :].rearrange("(c p) -> p c", p=128), slow=True)
            wg = k.sb("wg", [128, 8, DFF], BF16)
            wu = k.sb("wu", [128, 8, DFF], BF16)
            wd = k.sb("wd", [128, NF, D], BF16)
            self.load_w(wg, W["ffn_w_gate"][l], 8, DFF, gcol=g2, chunk=256)
            self.load_w(wu, W["ffn_w_up"][l], 8, DFF, gcol=g2, chunk=256)
            self.load_w(wd, W["ffn_w_down"][l], NF, D, chunk=128)
            TS = 2
            xt = k.rot("xt", [128, TS, D], F32, 2)
            junk = k.sb("junk", [128, D], BF16)
            hb = k.sb("hb", [128, D], BF16)
            xT = k.sb("xT", [128, 8, TS * 128], BF16)
            st = k.rot("st", [128, 4], F32, 2)
            sg = k.rot("sg", [128, TS * 128], F32, 2)
            hT = k.sb("hT", [128, NF, TS * 128], BF16)
            xo = k.rot("xo", [128, D], F32, 2)
            pT = k.ps("pT", [128, 1024], BF16)
            pg = k.rot("pg", [128, 512], F32, 2, psum=True)
            pu = k.rot("pu", [128, 512], F32, 2, psum=True)
            pm = k.rot("pm", [128, 512], F32, 2, psum=True)
            npm = 0
            nx = 0
            for tb in range(self.nt // TS):
                x_ = xt[tb % 2]
                for s in range(TS):
                    tok0 = (tb * TS + s) * 128
                    stt = st[s % 2]
                    k.dma("sp", x_[:, s, :], S["xmid"][tok0:tok0 + 128, :])
                    k._memset("pool", stt[:, :], 0.0)
                    k.act(junk[:, :], x_[:, s, :], AF.Square, accum_out=stt[:, 0:1])
                    self.rstd(stt[:, 1:2], stt[:, 0:1], stt[:, 2:3], 1.0 / D, 1e-6)
                    k.act(hb[:, :], x_[:, s, :], AF.Copy, scale=stt[:, 1:2])
                    for c in range(8):
                        k.tr(pT[:, c * 128:(c + 1) * 128], hb[:, c * 128:(c + 1) * 128], idb[:, :])
                    k.copy("dve", xT[:, :, s * 128:(s + 1) * 128], pT[:, :].rearrange("p (c t) -> p c t", c=8))
                for f in range(NF):
                    pg_, pu_ = pg[f % 2], pu[f % 2]
                    for c in range(8):
                        k.mm(pg_[:, 0:TS * 128], wg[:, c, f * 128:(f + 1) * 128], xT[:, c, :], start=(c == 0), stop=(c == 7))
                    for c in range(8):
                        k.mm(pu_[:, 0:TS * 128], wu[:, c, f * 128:(f + 1) * 128], xT[:, c, :], start=(c == 0), stop=(c == 7))
                    sg_ = sg[f % 2]
                    k.act(sg_[:, :], pg_[:, 0:TS * 128], AF.Silu)
                    k.tt("dve", hT[:, f, :], sg_[:, :], pu_[:, 0:TS * 128], ALU.mult)
                for s in range(TS):
                    tok0 = (tb * TS + s) * 128
                    xo_ = xo[nx % 2]
                    nx += 1
                    for nch in range(2):
                        p = pm[npm % 2]
                        npm += 1
                        for f in range(NF):
                            k.mm(p[:, :], hT[:, f, s * 128:(s + 1) * 128], wd[:, f, nch * 512:(nch + 1) * 512],
                                 start=(f == 0), stop=(f == NF - 1))
                        k.tt("dve", xo_[:, nch * 512:(nch + 1) * 512], p[:, :], x_[:, s, nch * 512:(nch + 1) * 512], ALU.add)
                    k.dma("pool", xdst[tok0:tok0 + 128, :], xo_[:, :])


def make_consts(seqs, na_rpb):
    ntok = sum(seqs)
    c = {}
    c["c_ident"] = np.eye(128, dtype=np.float32)
    rope = np.zeros((ntok, 32), np.float32)
    o = 0
    inv = (10000.0 ** (-np.arange(8, dtype=np.float32) / 8)).astype(np.float32)
    for L in seqs:
        t = np.arange(L)
        row = (t // 64).astype(np.float32)
        colp = (t % 64).astype(np.float32)
        ang = np.concatenate([row[:, None] * inv, colp[:, None] * inv], axis=-1).astype(np.float32)
        rope[o:o + L, 0:16] = np.cos(ang)
        rope[o:o + L, 16:32] = np.sin(ang)
        o += L
    c["c_rope"] = rope
    p = np.arange(128)
    kc = np.arange(4)
    key = kc[None, :] * 128 + p[:, None]
    w = key // 64
    kcol = key % 64
    q = np.arange(64)
    win = np.clip(q - 8, 0, 48)
    valid = (kcol[:, :, None] >= win[None, None, :]) & (kcol[:, :, None] < win[None, None, :] + 16)
    dc = np.clip(kcol[:, :, None] - q[None, None, :] + 15, 0, 30)
    nab = np.full((DEPTH, 4, 128, 8, 2, 4, 64), -30000.0, np.float32)
    for d in range(8):
        dri = np.clip(w - d + 7, 0, 14)
        drb = np.broadcast_to(dri[:, :, None], dc.shape)
        for hp in range(4):
            for h2 in range(2):
                g = na_rpb[:, 2 * hp + h2][:, drb, dc]
                nab[:, hp, :, d, h2] = np.where(valid[None], g, np.float32(-30000.0))
    c["c_nab"] = nab.reshape(DEPTH, 4, 128, 8 * 512)
    j = np.arange(128)[:, None]
    t = np.arange(128)[None, :]
    same = (j // 64) == (t // 64)
    rwm = np.zeros((2, 4, 128, 128), np.float32)
    rwm[0, 0] = same & (j < t)
    rwm[0, 1] = rwm[0, 0].T
    rwm[0, 2] = same & (j <= t)
    rwm[0, 3] = same
    rwm[1, 0] = same & (j > t)
    rwm[1, 1] = rwm[1, 0].T
    rwm[1, 2] = same & (j >= t)
    rwm[1, 3] = same
    c["c_rwm"] = rwm
    eb = np.zeros((128, 512), np.float32)
    for tt in range(128):
        eb[tt, np.arange(8) * 64 + (tt % 64)] = 1.0
    c["c_eblk"] = eb
    return c


WNAMES = ["norm1_g", "w_in", "b_gate", "na_q_norm", "na_k_norm", "na_proj", "mla_cq_norm", "mla_ckv_norm", "mla_w_uq",
          "mla_w_ukv", "mla_q_norm", "mla_k_norm", "mla_proj", "rw_mu", "rw_w0", "rw_w_up", "rw_a0", "rw_a_up", "rw_g_up",
          "rw_k_k", "rw_k_a", "rw_r_k", "rw_ln_w", "rw_ln_b", "rw_proj", "w_out", "norm2_g", "ffn_w_gate", "ffn_w_up",
          "ffn_w_down"]

_PROG = {}


def get_prog(seqs, dbg=()):
    key = (tuple(seqs), tuple(sorted(dbg)))
    if key not in _PROG:
        _PROG[key] = Prog(seqs, dbg)
    return _PROG[key]


def kernel(**inputs):
    xp = np.asarray(inputs["x_prompt"], np.float32)
    xs = np.asarray(inputs["x_sample"], np.float32)
    n = 8
    prog = get_prog(FULL_SEQS)
    consts = make_consts(FULL_SEQS, np.asarray(inputs["na_rpb"], np.float32))
    shared = {nm: np.ascontiguousarray(np.asarray(inputs[nm], np.float32)) for nm in WNAMES}
    shared.update(consts)
    in_maps = []
    for c in range(n):
        m = dict(shared)
        m["x"] = np.ascontiguousarray(np.concatenate([xp[c], xs[2 * c], xs[2 * c + 1]], axis=0))
        in_maps.append(m)
    res = run_bass_kernel_spmd(prog.nc, in_maps, core_ids=list(range(n)))
    yp = np.empty_like(xp)
    ys = np.empty_like(xs)
    for c in range(n):
        y = np.asarray(res.results[c]["y"], np.float32)
        yp[c] = y[0:8192]
        ys[2 * c] = y[8192:12288]
        ys[2 * c + 1] = y[12288:16384]
    return (yp, ys)
```
